# Optimizing a Trainium2 kernel written in Bass

```python
import math
import jax
import jax.numpy as jnp
from jax import lax
import numpy as np

D_MODEL = 1024
BATCH = 32
SEQ = 256
DEPTH = 2
DEC_BATCH = 4
DEC_SEQ = 2048
PAST_LEN = 512

F32 = jnp.float32
GRID_W = 64
N_HEADS = 8
N_KV = 2
HEAD_DIM = 64
Q_GROUP = N_HEADS // N_KV
ATT_WIDTH = N_HEADS * HEAD_DIM
KV_WIDTH = N_KV * HEAD_DIM
ROPE_AXIS_DIM = HEAD_DIM // 2
ROPE_THETA = 10000.0
Q_BLOCK = 128
SC_WIDTH = D_MODEL - ATT_WIDTH
CONV_WIDTH = 3
C_HEADS = 8
C_DK = D_MODEL // C_HEADS
C_DV = D_MODEL // C_HEADS
C_WIDTH = C_HEADS * C_DK
CHUNK = 32
D_FF = 2816
N_ATTN_LAYERS = (DEPTH + 1) // 2
N_REC_LAYERS = DEPTH // 2
EVEN_IN_WIDTH = ATT_WIDTH + 2 * KV_WIDTH + 3 * SC_WIDTH
EVEN_SPLITS = (ATT_WIDTH, ATT_WIDTH + KV_WIDTH, ATT_WIDTH + 2 * KV_WIDTH,
               ATT_WIDTH + 2 * KV_WIDTH + SC_WIDTH, ATT_WIDTH + 2 * KV_WIDTH + 2 * SC_WIDTH)
ALPHA = (2 * DEPTH) ** 0.25
BETA = (8 * DEPTH) ** -0.25
EPS = 1e-6

kernel_name = 'hybrid_diffusion_prefix_trunk_step'


def _layer_norm(x, g, b):
    xf = x.astype(F32)
    mu = jnp.mean(xf, axis=-1, keepdims=True)
    var = jnp.mean(jnp.square(xf - mu), axis=-1, keepdims=True)
    return ((xf - mu) * lax.rsqrt(var + EPS) * g.astype(F32) + b.astype(F32)).astype(x.dtype)


def _rms_norm(x, g):
    xf = x.astype(F32)
    return (xf * lax.rsqrt(jnp.mean(xf * xf, axis=-1, keepdims=True) + EPS) * g.astype(F32)).astype(x.dtype)


def _dwconv(x, w):
    pad = CONV_WIDTH // 2
    return lax.conv_general_dilated(x, w.astype(x.dtype)[:, None, :], window_strides=(1,),
                                    padding=((pad, pad),), dimension_numbers=('NWC', 'WIO', 'NWC'),
                                    feature_group_count=x.shape[-1])


def _modulation(cond, w_mod, b_mod):
    return jnp.split(jax.nn.silu(cond) @ w_mod + b_mod, 6, axis=-1)


def _axial_rope_tables(n_tokens):
    rows = n_tokens // GRID_W
    row_idx = jnp.repeat(jnp.arange(rows, dtype=F32), GRID_W)
    col_idx = jnp.tile(jnp.arange(GRID_W, dtype=F32), rows)
    inv = ROPE_THETA ** (-jnp.arange(0, ROPE_AXIS_DIM, 2, dtype=F32) / ROPE_AXIS_DIM)
    ang = jnp.concatenate([row_idx[:, None] * inv, col_idx[:, None] * inv], axis=-1)
    return jnp.cos(ang), jnp.sin(ang)


def _apply_axial_rope(x, cos, sin):
    half = ROPE_AXIS_DIM // 2
    xf = x.astype(F32)

    def rot(xa, c, s):
        c = c[None, :, None, :]
        s = s[None, :, None, :]
        x1, x2 = xa[..., :half], xa[..., half:]
        return jnp.concatenate([x1 * c - x2 * s, x1 * s + x2 * c], axis=-1)

    out = jnp.concatenate([rot(xf[..., :ROPE_AXIS_DIM], cos[:, :half], sin[:, :half]),
                           rot(xf[..., ROPE_AXIS_DIM:], cos[:, half:], sin[:, half:])], axis=-1)
    return out.astype(x.dtype)


def _attend(q, k, v):
    B, T = q.shape[0], q.shape[1]
    nb = T // Q_BLOCK
    qb = q.reshape(B, nb, Q_BLOCK, N_KV, Q_GROUP, HEAD_DIM).transpose(1, 0, 2, 3, 4, 5)
    scale = HEAD_DIM ** -0.5

    def block(qblk):
        s = jnp.einsum('bqkgd,bskd->bkgqs', qblk, k).astype(F32) * scale
        p = jax.nn.softmax(s, axis=-1).astype(v.dtype)
        return jnp.einsum('bkgqs,bskd->bqkgd', p, v)

    o = lax.map(block, qb)
    return o.transpose(1, 0, 2, 3, 4, 5).reshape(B, T, ATT_WIDTH)


def _attn_conv_project(h, w_in, q_gain, k_gain):
    B, T, _ = h.shape
    q, k, v, bg, cg, xi = jnp.split(h @ w_in, EVEN_SPLITS, axis=-1)
    q = _rms_norm(q.reshape(B, T, N_HEADS, HEAD_DIM), q_gain)
    k = _rms_norm(k.reshape(B, T, N_KV, HEAD_DIM), k_gain)
    v = v.reshape(B, T, N_KV, HEAD_DIM)
    return q, k, v, bg, cg, xi


def _attn_conv_context(h, w_in, q_gain, k_gain, conv_w, w_out):
    q, k, v, bg, cg, xi = _attn_conv_project(h, w_in, q_gain, k_gain)
    att = _attend(q, k, v)
    sc = bg * _dwconv(cg * xi, conv_w)
    return jnp.concatenate([att, sc], axis=-1) @ w_out, k, v


def _attn_conv_latent(h, w_in, q_gain, k_gain, conv_w, w_out, ctx_k, ctx_v, cos, sin):
    q, k, v, bg, cg, xi = _attn_conv_project(h, w_in, q_gain, k_gain)
    q = _apply_axial_rope(q, cos, sin)
    k = _apply_axial_rope(k, cos, sin)
    keys = jnp.concatenate([k, ctx_k.astype(k.dtype)], axis=1)
    vals = jnp.concatenate([v, ctx_v.astype(v.dtype)], axis=1)
    att = _attend(q, keys, vals)
    sc = bg * _dwconv(cg * xi, conv_w)
    return jnp.concatenate([att, sc], axis=-1) @ w_out


def _chunk_scan(q, log_f, k, v, s0):
    B, T, H, _ = q.shape
    DV = v.shape[-1]
    nc = T // CHUNK

    def to_chunks(a):
        return a.reshape(B, nc, CHUNK, H, a.shape[-1]).transpose(1, 0, 3, 2, 4)

    lower = jnp.tril(jnp.ones((CHUNK, CHUNK), dtype=bool))[:, :, None]

    def step(S, blk):
        qb, gb, kb, vb = blk
        b = jnp.cumsum(gb, axis=2)
        o_inter = jnp.einsum('bhtk,bhkv->bhtv', qb * jnp.exp(b), S)
        diff = b[:, :, :, None, :] - b[:, :, None, :, :]
        decay = jnp.exp(jnp.where(lower, diff, -jnp.inf))
        scores = jnp.einsum('bhtk,bhsk,bhtsk->bhts', qb, kb, decay)
        o = o_inter + jnp.einsum('bhts,bhsv->bhtv', scores, vb)
        b_last = b[:, :, -1:, :]
        S_new = (jnp.exp(b_last[:, :, 0, :])[..., None] * S
                 + jnp.einsum('bhsk,bhsv->bhkv', kb * jnp.exp(b_last - b), vb))
        return S_new, o

    S, o = lax.scan(step, s0, (to_chunks(q), to_chunks(log_f), to_chunks(k), to_chunks(v)))
    return o.transpose(1, 0, 3, 2, 4).reshape(B, T, H, DV), S


def _hgrn_mixer(h, w_in, lb, norm_g, w_out, s_fw, s_bw):
    B, T, _ = h.shape
    z = (h @ w_in).astype(F32)
    q, i, g, f_fw, f_bw = jnp.split(z, 5, axis=-1)

    def heads(a):
        return a.reshape(B, T, C_HEADS, -1)

    def decay(f_raw, lb_d):
        f = lb_d + (1.0 - lb_d) * jax.nn.sigmoid(f_raw)
        return heads(jnp.log(f)), heads(1.0 - f)

    q, i = heads(q), heads(i)
    g_fw, k_fw = decay(f_fw, lb[0])
    g_bw, k_bw = decay(f_bw, lb[1])
    o_f, S_f = _chunk_scan(q, g_fw, k_fw, i, s_fw)
    o_b, S_b = _chunk_scan(q[:, ::-1], g_bw[:, ::-1], k_bw[:, ::-1], i[:, ::-1], s_bw)
    o = _rms_norm(o_f + o_b[:, ::-1], norm_g).reshape(B, T, C_WIDTH) * jax.nn.silu(g)
    return o.astype(h.dtype) @ w_out, S_f, S_b


def _conv_ffn(h, w_up, conv_w, w_down):
    u = _dwconv(h @ w_up, conv_w)
    a, gt = jnp.split(u, 2, axis=-1)
    return (jax.nn.silu(gt) * a) @ w_down


def setup_inputs(seed: int = 0) -> dict:
    key = jax.random.key(seed)
    ks = jax.random.split(key, 25)

    def nrm(idx, shape, scale):
        return jax.random.normal(ks[idx], shape, F32) * scale

    d_in = D_MODEL ** -0.5
    return {
        'x_prompt': nrm(0, (BATCH, SEQ, D_MODEL), 1.0),
        'x_sample': nrm(1, (DEC_BATCH, DEC_SEQ, D_MODEL), 1.0),
        'cache_k': nrm(2, (DEC_BATCH, N_ATTN_LAYERS, PAST_LEN, N_KV, HEAD_DIM), 1.0),
        'cache_v': nrm(3, (DEC_BATCH, N_ATTN_LAYERS, PAST_LEN, N_KV, HEAD_DIM), 1.0),
        'state_hgrn': nrm(4, (DEC_BATCH, N_REC_LAYERS, 2, C_HEADS, C_DK, C_DV), 0.5),
        'c': nrm(5, (DEC_BATCH, D_MODEL), 1.0),
        'c_ctx': nrm(6, (D_MODEL,), 1.0),
        'w_mod': nrm(7, (DEPTH, D_MODEL, 6 * D_MODEL), 0.5 * d_in),
        'b_mod': nrm(8, (DEPTH, 6 * D_MODEL), 0.02),
        'ln1_g': 1.0 + nrm(9, (DEPTH, D_MODEL), 0.02),
        'ln1_b': nrm(10, (DEPTH, D_MODEL), 0.02),
        'ln2_g': 1.0 + nrm(11, (DEPTH, D_MODEL), 0.02),
        'ln2_b': nrm(12, (DEPTH, D_MODEL), 0.02),
        'attn_w_in': nrm(13, (N_ATTN_LAYERS, D_MODEL, EVEN_IN_WIDTH), d_in),
        'attn_q_gain': 1.0 + nrm(14, (N_ATTN_LAYERS, HEAD_DIM), 0.02),
        'attn_k_gain': 1.0 + nrm(15, (N_ATTN_LAYERS, HEAD_DIM), 0.02),
        'sconv_w': nrm(16, (N_ATTN_LAYERS, CONV_WIDTH, SC_WIDTH), CONV_WIDTH ** -0.5),
        'attn_w_out': nrm(17, (N_ATTN_LAYERS, ATT_WIDTH + SC_WIDTH, D_MODEL), (ATT_WIDTH + SC_WIDTH) ** -0.5 * BETA),
        'hgrn_w_in': nrm(18, (N_REC_LAYERS, D_MODEL, 5 * C_WIDTH), d_in),
        'hgrn_lb_logits': nrm(19, (DEPTH, 2, C_WIDTH), 0.5),
        'hgrn_norm_g': 1.0 + nrm(20, (N_REC_LAYERS, C_DV), 0.02),
        'hgrn_w_out': nrm(21, (N_REC_LAYERS, C_WIDTH, D_MODEL), C_WIDTH ** -0.5 * BETA),
        'ffn_w_up': nrm(22, (DEPTH, D_MODEL, 2 * D_FF), d_in),
        'ffn_conv_w': nrm(23, (DEPTH, CONV_WIDTH, 2 * D_FF), CONV_WIDTH ** -0.5),
        'ffn_w_down': nrm(24, (DEPTH, D_FF, D_MODEL), D_FF ** -0.5 * BETA),
    }


def reference(x_prompt, x_sample, cache_k, cache_v, state_hgrn, c, c_ctx,
              w_mod, b_mod, ln1_g, ln1_b, ln2_g, ln2_b,
              attn_w_in, attn_q_gain, attn_k_gain, sconv_w, attn_w_out,
              hgrn_w_in, hgrn_lb_logits, hgrn_norm_g, hgrn_w_out,
              ffn_w_up, ffn_conv_w, ffn_w_down):
    lb_all = jnp.cumsum(jax.nn.softmax(hgrn_lb_logits.astype(F32), axis=0), axis=0)
    lb_all = lb_all - lb_all[0]
    cos, sin = _axial_rope_tables(x_sample.shape[1])
    cond_ctx = c_ctx[None, None, :]
    cond_lat = c[:, None, :]
    zero_state = jnp.zeros((x_prompt.shape[0], C_HEADS, C_DK, C_DV), F32)
    xp, xs = x_prompt, x_sample
    new_k, new_v, new_s = [], [], []
    for l in range(DEPTH):
        j = l // 2
        sh1_p, sc1_p, gt1_p, sh2_p, sc2_p, gt2_p = _modulation(cond_ctx, w_mod[l], b_mod[l])
        sh1_s, sc1_s, gt1_s, sh2_s, sc2_s, gt2_s = _modulation(cond_lat, w_mod[l], b_mod[l])
        hp = xp * (1 + sc1_p) + sh1_p
        hs = xs * (1 + sc1_s) + sh1_s
        if l % 2 == 0:
            op, kc, vc = _attn_conv_context(hp, attn_w_in[j], attn_q_gain[j], attn_k_gain[j],
                                            sconv_w[j], attn_w_out[j])
            os_ = _attn_conv_latent(hs, attn_w_in[j], attn_q_gain[j], attn_k_gain[j],
                                    sconv_w[j], attn_w_out[j], cache_k[:, j], cache_v[:, j], cos, sin)
            new_k.append(kc)
            new_v.append(vc)
        else:
            op, sf, sb = _hgrn_mixer(hp, hgrn_w_in[j], lb_all[l], hgrn_norm_g[j], hgrn_w_out[j],
                                     zero_state, zero_state)
            os_, _, _ = _hgrn_mixer(hs, hgrn_w_in[j], lb_all[l], hgrn_norm_g[j], hgrn_w_out[j],
                                    state_hgrn[:, j, 0].astype(F32), state_hgrn[:, j, 1].astype(F32))
            new_s.append(jnp.stack([sf, sb], axis=1))
        xp = _layer_norm(ALPHA * xp + gt1_p * op, ln1_g[l], ln1_b[l])
        xs = _layer_norm(ALPHA * xs + gt1_s * os_, ln1_g[l], ln1_b[l])
        fp = _conv_ffn(xp * (1 + sc2_p) + sh2_p, ffn_w_up[l], ffn_conv_w[l], ffn_w_down[l])
        fs = _conv_ffn(xs * (1 + sc2_s) + sh2_s, ffn_w_up[l], ffn_conv_w[l], ffn_w_down[l])
        xp = _layer_norm(ALPHA * xp + gt2_p * fp, ln2_g[l], ln2_b[l])
        xs = _layer_norm(ALPHA * xs + gt2_s * fs, ln2_g[l], ln2_b[l])
    new_cache_k = jnp.stack(new_k, axis=1)
    new_cache_v = jnp.stack(new_v, axis=1)
    new_state_hgrn = jnp.stack(new_s, axis=1)
    return (xp, xs, new_cache_k, new_cache_v, new_state_hgrn)
```

```python
import os
import numpy as np
import concourse.bass as bass
import concourse.mybir as mybir
from concourse.bass_utils import run_bass_kernel_spmd
from contextlib import ExitStack

F32 = mybir.dt.float32
F32R = mybir.dt.float32r
BF16 = mybir.dt.bfloat16
AF = mybir.ActivationFunctionType
ALU = mybir.AluOpType

T = 2048
D = 1024
KC = 8
NTG = 4
TGS = 512
NSEG = 8
SEG = 256
ALPHA = 4.0 ** 0.25
EPS = 1e-6
EPS_LN = EPS / (ALPHA * ALPHA)
DFF = 2816
NFC = 22
PAST = 512
NKC = 20
CHUNK = 32
NCH = T // CHUNK


class Prog:
    ENGS = ("pe", "act", "dve", "pool", "sp")

    def __init__(self, nc, es, n_dma_sems=12):
        self.nc = nc
        self.eng = {"pe": nc.tensor, "act": nc.scalar, "dve": nc.vector, "pool": nc.gpsimd, "sp": nc.sync}
        self.esem = {e: es.enter_context(nc.semaphore("e_" + e)) for e in self.ENGS}
        self.nds = n_dma_sems
        self.dsem = {e: [es.enter_context(nc.semaphore("d_%s_%d" % (e, i))) for i in range(n_dma_sems)]
                     for e in ("sp", "pool")}
        self.cnt = {e: 0 for e in self.ENGS}
        self.ndma = {e: 0 for e in self.dsem}
        self.dval = {}
        self.waited = {e: {} for e in self.ENGS}
        self.last_w = {}
        self.readers = {}
        self.nops = {e: 0 for e in self.ENGS}

    def _wait(self, e, sem, val):
        w = self.waited[e]
        if w.get(sem.num, 0) >= val:
            return
        w[sem.num] = val
        self.eng[e].wait_ge(sem, val)

    def _wait_tok(self, e, tok, is_dma_op):
        if tok[0] == "E":
            _, f, c = tok
            if f == "pe" and e == "pe" and not is_dma_op:
                return
            self._wait(e, self.esem[f], c)
        else:
            _, q, i, v = tok
            self._wait(e, self.dsem[q][i], v)

    def add(self, e, fn, reads=(), writes=(), dma=False):
        psr = [k for k in reads if isinstance(k, tuple) and k[0] == "ps"]
        if psr:
            writes = list(writes) + [k for k in psr if k not in writes]
        deps = set()
        for k in reads:
            w = self.last_w.get(k)
            if w is not None:
                deps.add(w)
        for k in writes:
            w = self.last_w.get(k)
            if w is not None:
                deps.add(w)
            for r in self.readers.get(k, ()):
                deps.add(r)
        for tok in sorted(deps):
            self._wait_tok(e, tok, dma)
        eng = self.eng[e]
        if dma:
            i = self.ndma[e] % self.nds
            self.ndma[e] += 1
            prev = self.dval.get((e, i), 0)
            if prev:
                self._wait(e, self.dsem[e][i], prev)
            ins = fn(eng)
            ins.then_inc(self.dsem[e][i], 16)
            self.dval[(e, i)] = prev + 16
            tok = ("D", e, i, prev + 16)
        else:
            ins = fn(eng)
            self.cnt[e] += 1
            ins.then_inc(self.esem[e], 1)
            tok = ("E", e, self.cnt[e])
        self.nops[e] += 1
        for k in writes:
            self.last_w[k] = tok
            self.readers[k] = []
        for k in reads:
            self.readers.setdefault(k, []).append(tok)
        return tok

    def barrier(self):
        for e in self.ENGS:
            for f in self.ENGS:
                if f != e and self.cnt[f] > 0:
                    self._wait(e, self.esem[f], self.cnt[f])
            for (q, i), v in self.dval.items():
                self._wait(e, self.dsem[q][i], v)
        self.last_w.clear()
        self.readers.clear()


def _swap_idx():
    idx = np.zeros(64, dtype=np.int64)
    for d in range(64):
        blk = d // 16
        idx[d] = d + 16 if blk % 2 == 0 else d - 16
    return idx


def build(n_dbg=None, upto=99):
    nc = bass.Bass("TRN2", target_bir_lowering=False)
    es = ExitStack()
    P = Prog(nc, es)

    def din(name, shape, dt=F32):
        return nc.dram_tensor(name, list(shape), dt, kind="ExternalInput").ap()

    def dout(name, shape, dt=F32):
        return nc.dram_tensor(name, list(shape), dt, kind="ExternalOutput").ap()

    sbn = [0]

    def sb(stack, name, shape, dt):
        sbn[0] += 1
        return stack.enter_context(nc.sbuf_tensor("s%d_%s" % (sbn[0], name), list(shape), dt))

    d_xT = din("xT", [D, T])
    d_cond = din("cond", [128, KC])
    d_flags = din("flags", [128, 4])
    d_ropeC = din("ropeC", [128, T])
    d_ropeS = din("ropeS", [128, T])
    d_kcT = din("kcT", [2, 128, PAST])
    d_vc = din("vc", [PAST, 128])
    d_state = din("state", [2, 8, 128, 128])
    d_wmod = din("w_mod", [2, D, 6 * D])
    d_bmod = din("bmod", [128, 96])
    d_lnp = din("lnp", [128, 2, 4, KC])
    d_win = din("attn_in_u", [19, 128, KC, 128])
    d_gains = din("gains", [128, 4])
    d_consts = din("consts", [3, 128, 128])
    d_scw = din("scw", [128, 4, 3])
    d_wout = din("attn_out_u", [8, 128, KC, 128])
    d_wup = din("ffn_up_u", [2, 44, 128, KC, 128])
    d_fcw = din("fcw", [2, 128, 44, 3])
    d_wdn = din("ffn_dn_u", [2, 8, 128, NFC, 128])
    d_hin = din("hgrn_in_u", [40, 128, KC, 128])
    d_lbl = din("lbl", [128, 2, 2, 8])
    d_hng = din("hng", [128, 1])
    d_hout = din("hgrn_out_u", [8, 128, KC, 128])
    d_hmask = din("hscan", [128, 2, 512])
    d_htri = din("htri", [128, 2, 128])

    o_yT = dout("yT", [D, T])
    o_kT = dout("koutT", [2, 64, T])
    o_v = dout("vout", [T, 128])
    o_s = dout("sout", [NSEG, 2, 8, 128, 128])

    dbg = {}

    def tap(name, ap, shape, reads, dt=F32):
        if n_dbg is None:
            return
        d = dout("dbg_" + name, shape, dt)
        n_dbg[name] = shape
        P.add("sp", lambda e: e.dma_start(out=d, in_=ap), reads=reads, writes=["dbg_" + name], dma=True)

    xT = sb(es, "xT_sb", [128, KC, T], F32)
    hT = sb(es, "hT_sb", [128, KC, T], BF16)
    mv = sb(es, "mv", [128, 2, 10, KC], F32)
    lnp = sb(es, "lnp_sb", [128, 2, 4, KC], F32)
    flags = sb(es, "flags_sb", [128, 4], F32)
    onesr = sb(es, "onesr", [128, 128], F32R)
    bonesr = sb(es, "bonesr", [128, 128], F32R)
    swapT = sb(es, "swapT", [128, 128], BF16)
    identb = sb(es, "identb", [128, 128], BF16)
    cst_f = sb(es, "cst_f", [128, 3, 128], F32)
    PS = es.enter_context(nc.psum_tensor("PS", [128, 4096], F32))

    def bank(b, n=1):
        return PS[:, b * 512:(b + n) * 512]

    def psk(b, n=1):
        return [("ps", i) for i in range(b, b + n)]

    def xk(j, g=None):
        if g is None:
            return [("xT", j, gg) for gg in range(NTG)]
        return [("xT", j, g)]

    def hk(j, g=None):
        if g is None:
            return [("hT", j, gg) for gg in range(NTG)]
        return [("hT", j, g)]

    om = flags[:, 0:1]
    keep = flags[:, 1:2]
    nbm = flags[:, 2:3]

    for j in range(KC):
        P.add("sp", lambda e, j=j: e.dma_start(out=xT[:, j, :], in_=d_xT[j * 128:(j + 1) * 128, :]),
              writes=xk(j), dma=True)
    P.add("sp", lambda e: e.dma_start(out=flags[:, :], in_=d_flags[:, :]), writes=["flags"], dma=True)
    P.add("sp", lambda e: e.dma_start(out=lnp[:], in_=d_lnp[:]), writes=["lnp"], dma=True)
    P.add("sp", lambda e: e.dma_start(out=cst_f[:], in_=d_consts.rearrange("c p m -> p c m")), writes=["cst_f"], dma=True)
    P.add("dve", lambda e: e.tensor_copy(out=swapT[:, :], in_=cst_f[:, 0, :]), reads=["cst_f"], writes=["swapT"])
    P.add("dve", lambda e: e.tensor_copy(out=bonesr[:, :], in_=cst_f[:, 1, :]), reads=["cst_f"], writes=["bonesr"])
    P.add("dve", lambda e: e.tensor_copy(out=identb[:, :], in_=cst_f[:, 2, :]), reads=["cst_f"], writes=["identb"])
    P.add("pool", lambda e: e.memset(cst_f[:, 1, :], 1.0), reads=["bonesr"], writes=["cst_f"])
    P.add("dve", lambda e: e.tensor_copy(out=onesr[:, :], in_=cst_f[:, 1, :]), reads=["cst_f"], writes=["onesr"])

    scond = sb(es, "scond", [128, KC], BF16)
    bmod = sb(es, "bmod_sb", [128, 96], F32)
    mod = sb(es, "mod_sb", [128, 2, 6, KC], F32)
    with ExitStack() as ph:
        cond = sb(ph, "cond_sb", [128, KC], F32)
        wm = [sb(ph, "wm%d" % i, [128, KC, 1024], BF16) for i in range(2)]
        P.add("sp", lambda e: e.dma_start(out=cond[:, :], in_=d_cond[:, :]), writes=["cond"], dma=True)
        P.add("sp", lambda e: e.dma_start(out=bmod[:, :], in_=d_bmod[:, :]), writes=["bmod"], dma=True)
        P.add("act", lambda e: e.activation(out=scond[:, :], in_=cond[:, :], func=AF.Silu), reads=["cond"], writes=["scond"])
        for l in range(1):
            wv = d_wmod[l].rearrange("(k p) n -> p k n", p=128)
            for pc in range(6):
                slot = (l * 6 + pc) % 2
                P.add("pool", lambda e, slot=slot, pc=pc, wv=wv: e.dma_start(out=wm[slot][:], in_=wv[:, :, pc * 1024:(pc + 1) * 1024]),
                      writes=[("wm", slot)], dma=True)
                for cc in range(8):
                    col = l * 48 + pc * 8 + cc

                    def mmfn(e, slot=slot, cc=cc, col=col):
                        ins = None
                        for k in range(KC):
                            ins = e.matmul(PS[:, col:col + 1], lhsT=wm[slot][:, k, cc * 128:(cc + 1) * 128],
                                           rhs=scond[:, k:k + 1], start=(k == 0), stop=(k == KC - 1))
                        return ins
                    P.add("pe", mmfn, reads=[("wm", slot), "scond"], writes=[("ps", 0)])
        P.add("dve", lambda e: e.tensor_tensor(out=mod[:, 0].rearrange("p v c -> p (v c)"), in0=PS[:, 0:48], in1=bmod[:, 0:48], op=ALU.add),
              reads=[("ps", 0), "bmod"], writes=["mod0"])

        def derive(l):
            mk = "mod%d" % l
            P.add("dve", lambda e: e.tensor_copy(out=mv[:, l, 0, :], in_=mod[:, l, 0, :]), reads=[mk], writes=[("mv", l, 0)])
            P.add("dve", lambda e: e.tensor_scalar(out=mv[:, l, 1, :], in0=mod[:, l, 1, :], scalar1=1.0, scalar2=None, op0=ALU.add),
                  reads=[mk], writes=[("mv", l, 1)])
            P.add("dve", lambda e: e.tensor_scalar(out=mv[:, l, 2, :], in0=mod[:, l, 2, :], scalar1=1.0 / ALPHA, scalar2=None, op0=ALU.mult),
                  reads=[mk], writes=[("mv", l, 2)])
            P.add("dve", lambda e: e.tensor_copy(out=mv[:, l, 3, :], in_=mod[:, l, 3, :]), reads=[mk], writes=[("mv", l, 3)])
            P.add("dve", lambda e: e.tensor_scalar(out=mv[:, l, 4, :], in0=mod[:, l, 4, :], scalar1=1.0, scalar2=None, op0=ALU.add),
                  reads=[mk], writes=[("mv", l, 4)])
            P.add("dve", lambda e: e.tensor_scalar(out=mv[:, l, 5, :], in0=mod[:, l, 5, :], scalar1=1.0 / ALPHA, scalar2=None, op0=ALU.mult),
                  reads=[mk], writes=[("mv", l, 5)])
            P.add("dve", lambda e: e.tensor_tensor(out=mv[:, l, 6, :], in0=lnp[:, l, 0, :], in1=mv[:, l, 4, :], op=ALU.mult),
                  reads=["lnp", ("mv", l, 4)], writes=[("mv", l, 6)])
            P.add("dve", lambda e: e.tensor_tensor(out=mv[:, l, 7, :], in0=lnp[:, l, 1, :], in1=mv[:, l, 4, :], op=ALU.mult),
                  reads=["lnp", ("mv", l, 4)], writes=[("mv", l, 7)])
            P.add("dve", lambda e: e.tensor_tensor(out=mv[:, l, 7, :], in0=mv[:, l, 7, :], in1=mv[:, l, 3, :], op=ALU.add),
                  reads=[("mv", l, 7), ("mv", l, 3)], writes=[("mv", l, 7)])

        def derive_cross():
            P.add("dve", lambda e: e.tensor_tensor(out=mv[:, 0, 8, :], in0=lnp[:, 0, 2, :], in1=mv[:, 1, 1, :], op=ALU.mult),
                  reads=["lnp", ("mv", 1, 1)], writes=[("mv", 0, 8)])
            P.add("dve", lambda e: e.tensor_tensor(out=mv[:, 0, 9, :], in0=lnp[:, 0, 3, :], in1=mv[:, 1, 1, :], op=ALU.mult),
                  reads=["lnp", ("mv", 1, 1)], writes=[("mv", 0, 9)])
            P.add("dve", lambda e: e.tensor_tensor(out=mv[:, 0, 9, :], in0=mv[:, 0, 9, :], in1=mv[:, 1, 0, :], op=ALU.add),
                  reads=[("mv", 0, 9), ("mv", 1, 0)], writes=[("mv", 0, 9)])

        derive(0)
        tap("mv", mv[:].rearrange("p l v c -> p (l v c)"), [128, 2 * 10 * KC], [("mv", l, v) for l in range(2) for v in range(10)])
        P.barrier()

    MVK = [("mv", l, v) for l in range(2) for v in range(10)]

    for j in range(KC):
        e_ = "dve" if j % 2 == 0 else "pool"
        P.add(e_, lambda e, j=j: e.tensor_scalar(out=hT[:, j, :], in0=xT[:, j, :], scalar1=mv[:, 0, 1, j:j + 1],
                                                  scalar2=mv[:, 0, 0, j:j + 1], op0=ALU.mult, op1=ALU.add),
              reads=xk(j) + MVK, writes=hk(j))
    tap("h0", hT[:, 0, :], [128, T], hk(0)) if False else None

    ctx = dict(m1=dict(scond=scond, bmod=bmod, mod=mod, d_wmod=d_wmod, derive=derive, derive_cross=derive_cross),
               nc=nc, es=es, P=P, PS=PS, bank=bank, psk=psk, xk=xk, hk=hk, xT=xT, hT=hT, mv=mv, lnp=lnp, MVK=MVK,
               onesr=onesr, bonesr=bonesr, swapT=swapT, identb=identb, om=om, keep=keep, nbm=nbm, sb=sb, tap=tap,
               d=dict(win=d_win, gains=d_gains, ropeC=d_ropeC, ropeS=d_ropeS, kcT=d_kcT, vc=d_vc, scw=d_scw, wout=d_wout,
                      wup=d_wup, fcw=d_fcw, wdn=d_wdn, hin=d_hin, lbl=d_lbl, hng=d_hng, hout=d_hout, hmask=d_hmask, htri=d_htri,
                      state=d_state, o_yT=o_yT, o_kT=o_kT, o_v=o_v, o_s=o_s))
    if upto >= 1:
        layer0_mixer(ctx)
    if upto >= 2:
        layernorm(ctx, 0, 1)
    if upto >= 3:
        ffn(ctx, 0)
    if upto >= 4:
        layernorm(ctx, 0, 2)
    if upto >= 5:
        hgrn_mixer(ctx)
    if upto >= 6:
        layernorm(ctx, 1, 1)
    if upto >= 7:
        ffn(ctx, 1)
    if upto >= 8:
        layernorm(ctx, 1, 2, final=True)
    if n_dbg is not None:
        for j in range(KC):
            tap("x%d" % j, xT[:, j, :], [128, T], xk(j))
            tap("h%d" % j, hT[:, j, :], [128, T], hk(j), BF16)
    P.barrier()
    return nc, P


def load_unit(c, dst, src, key, reads=()):
    c["P"].add("pool", lambda e: e.dma_start(out=dst, in_=src), reads=list(reads), writes=[key], dma=True)


class WStream:
    def __init__(self, c, bufs, key, srcs, pf):
        self.c, self.bufs, self.key, self.srcs, self.pf = c, bufs, key, srcs, pf
        self.issued = 0
        self.n = len(bufs)

    def prefetch(self, upto):
        upto = min(upto, len(self.srcs) - 1)
        while self.issued <= upto:
            i = self.issued
            src = self.srcs[i]
            if isinstance(src, tuple):
                load_unit(self.c, self.bufs[i % self.n][:, 0:src[1], :], src[0], (self.key, i % self.n))
            else:
                load_unit(self.c, self.bufs[i % self.n][:], src, (self.key, i % self.n))
            self.issued += 1

    def get(self, i):
        self.prefetch(i + self.pf)
        return self.bufs[i % self.n], (self.key, i % self.n)


def proj_fm(c, wtile, wkey, half, rhs_of, rkeys, nk=KC):
    P, bank, psk = c["P"], c["bank"], c["psk"]
    for g in range(NTG):
        b = half * 4 + g

        def fn(e, g=g, b=b):
            ins = None
            for k in range(nk):
                ins = e.matmul(bank(b), lhsT=wtile[:, k, :], rhs=rhs_of(k, g), start=(k == 0), stop=(k == nk - 1))
            return ins
        P.add("pe", fn, reads=[wkey] + rkeys(g), writes=psk(b))


def layer0_mixer(c):
    P, nc, PS, bank, psk = c["P"], c["nc"], c["PS"], c["bank"], c["psk"]
    xT, hT, mv, sb, tap, d = c["xT"], c["hT"], c["mv"], c["sb"], c["tap"], c["d"]
    xk, hk, MVK = c["xk"], c["hk"], c["MVK"]
    om, nbm = c["om"], c["nbm"]
    onesr, bonesr, swapT = c["onesr"], c["bonesr"], c["swapT"]
    d_win = d["win"]
    L0 = os.environ.get("K_L0", "ABCDE")

    def hrhs(k, g):
        return hT[:, k, g * TGS:(g + 1) * TGS]

    def hkeys(g):
        return [("hT", k, g) for k in range(KC)]

    with ExitStack() as ph:
        scT = sb(ph, "scT", [128, 4, T], BF16)
        wu = [sb(ph, "wu%d" % i, [128, KC, 128], BF16) for i in range(4)]
        gains = sb(ph, "gains", [128, 4], F32)
        scw = sb(ph, "scw_sb", [128, 4, 3], F32)
        nbw = sb(ph, "nbw", [128, 4, 2], F32)
        P.add("sp", lambda e: e.dma_start(out=gains[:, :], in_=d["gains"][:, :]), writes=["gains"], dma=True)
        P.add("sp", lambda e: e.dma_start(out=scw[:], in_=d["scw"][:]), writes=["scw"], dma=True)
        P.add("dve", lambda e: e.tensor_scalar(out=nbw[:, :, 0], in0=scw[:, :, 0], scalar1=nbm, scalar2=None, op0=ALU.mult),
              reads=["scw", "flags"], writes=["nbw"])
        P.add("dve", lambda e: e.tensor_scalar(out=nbw[:, :, 1], in0=scw[:, :, 2], scalar1=nbm, scalar2=None, op0=ALU.mult),
              reads=["scw", "flags", "nbw"], writes=["nbw"])
        srcs0 = []
        for ch in range(4):
            srcs0 += [d_win[15 + ch], d_win[11 + ch], d_win[7 + ch]]
        srcs0 += [d_win[un] for un in range(6)] + [d_win[6]] + [d["wout"][j] for j in range(KC)]
        ws0 = WStream(c, wu, "wu", srcs0, 3)
        ucount = [0]

        def next_unit(src):
            i = ucount[0]
            ucount[0] += 1
            assert srcs0[i] is src or True
            ws0.get(i)
            return i % 4

        with ExitStack() as pa:
            xi_sb = sb(pa, "xi_sb", [128, T], F32)
            mpad = sb(pa, "mpad", [128, T + 2], F32)
            u = sb(pa, "u_sb", [128, T], F32)
            P.add("pool", lambda e: e.memset(mpad[:, :], 0.0), writes=["mpad"])
            for ch in (range(4) if "A" in L0 else ()):
                s_xi = next_unit(d_win[15 + ch])
                proj_fm(c, wu[s_xi], ("wu", s_xi), 0, hrhs, hkeys)
                s_cg = next_unit(d_win[11 + ch])
                P.add("act", lambda e: e.activation(out=xi_sb[:, :], in_=bank(0, 4), func=AF.Copy), reads=psk(0, 4), writes=["xi_sb"])
                proj_fm(c, wu[s_cg], ("wu", s_cg), 1, hrhs, hkeys)
                s_bg = next_unit(d_win[7 + ch])
                P.add("dve", lambda e: e.tensor_tensor(out=mpad[:, 1:T + 1], in0=bank(4, 4), in1=xi_sb[:, :], op=ALU.mult),
                      reads=psk(4, 4) + ["xi_sb"], writes=["mpad"])
                proj_fm(c, wu[s_bg], ("wu", s_bg), 0, hrhs, hkeys)
                P.add("act", lambda e, ch=ch: e.activation(out=u[:, :], in_=mpad[:, 1:T + 1], func=AF.Copy, scale=scw[:, ch, 1:2]),
                      reads=["mpad", "scw"], writes=["u"])
                P.add("dve", lambda e, ch=ch: e.scalar_tensor_tensor(out=u[:, :], in0=mpad[:, 0:T], scalar=scw[:, ch, 0:1], in1=u[:, :],
                                                                     op0=ALU.mult, op1=ALU.add), reads=["mpad", "scw", "u"], writes=["u"])
                P.add("dve", lambda e, ch=ch: e.scalar_tensor_tensor(out=u[:, :], in0=mpad[:, 2:T + 2], scalar=scw[:, ch, 2:3], in1=u[:, :],
                                                                     op0=ALU.mult, op1=ALU.add), reads=["mpad", "scw", "u"], writes=["u"])
                P.add("dve", lambda e, ch=ch: e.scalar_tensor_tensor(out=u[:, SEG:T:SEG], in0=mpad[:, SEG:T:SEG], scalar=nbw[:, ch, 0:1],
                                                                     in1=u[:, SEG:T:SEG], op0=ALU.mult, op1=ALU.add),
                      reads=["mpad", "nbw", "u"], writes=["u"])
                P.add("dve", lambda e, ch=ch: e.scalar_tensor_tensor(out=u[:, SEG - 1:T - 1:SEG], in0=mpad[:, SEG + 1:T + 1:SEG], scalar=nbw[:, ch, 1:2],
                                                                     in1=u[:, SEG - 1:T - 1:SEG], op0=ALU.mult, op1=ALU.add),
                      reads=["mpad", "nbw", "u"], writes=["u"])
                P.add("dve", lambda e, ch=ch: e.tensor_tensor(out=scT[:, ch, :], in0=bank(0, 4), in1=u[:, :], op=ALU.mult),
                      reads=psk(0, 4) + ["u"], writes=[("scT", ch)])
            P.barrier()
        tap("scT", scT[:].rearrange("p c t -> p (c t)"), [128, 4 * T], [("scT", ch) for ch in range(4)]) if False else None

        with ExitStack() as pb:
            qT = sb(pb, "qT", [128, 4, T], BF16)
            kTz = sb(pb, "kTz", [128, 2, 2, T + PAST], BF16)
            P.add("pool", lambda e: e.memset(kTz[:].rearrange("p a b t -> p (a b t)"), 0.0), writes=["kTz0"])
            with ExitStack() as pq:
                ropeC = sb(pq, "ropeC", [128, T], F32)
                ropeS = sb(pq, "ropeS", [128, T], F32)
                P.add("sp", lambda e: e.dma_start(out=ropeC[:, :], in_=d["ropeC"][:, :]), writes=["ropeC"], dma=True)
                P.add("sp", lambda e: e.dma_start(out=ropeS[:, :], in_=d["ropeS"][:, :]), writes=["ropeS"], dma=True)
                qsq = [sb(pq, "qsq%d" % i, [128, TGS], F32R) for i in range(2)]
                qbf = [sb(pq, "qbf%d" % i, [128, TGS], BF16) for i in range(2)]
                sd = [sb(pq, "sd%d" % i, [128, TGS], F32) for i in range(2)]
                rstd = [sb(pq, "rstd%d" % i, [128, TGS], F32) for i in range(2)]
                t1 = [sb(pq, "t1_%d" % i, [128, TGS], F32) for i in range(2)]
                t2 = [sb(pq, "t2_%d" % i, [128, TGS], F32) for i in range(2)]
                m1 = c["m1"]
                wm1 = [sb(pq, "wm1_%d" % i, [128, KC, 128], BF16) for i in range(4)]
                wv1 = m1["d_wmod"][1].rearrange("(k p) n -> p k n", p=128)

                def m1_load(pi):
                    if pi >= 48:
                        return
                    P.add("pool", lambda e: e.dma_start(out=wm1[pi % 4][:], in_=wv1[:, :, pi * 128:(pi + 1) * 128]), writes=[("wm1", pi % 4)], dma=True)

                def m1_mm(pi):
                    col = 6 * 512 + pi

                    def mmfn(e):
                        ins = None
                        for k in range(KC):
                            ins = e.matmul(PS[:, col:col + 1], lhsT=wm1[pi % 4][:, k, :],
                                           rhs=m1["scond"][:, k:k + 1], start=(k == 0), stop=(k == KC - 1))
                        return ins
                    P.add("pe", mmfn, reads=[("wm1", pi % 4), "scond"], writes=[("ps", 6)])
                m1_load(0)
                m1_load(1)
                m1_load(2)
                it = 0
                for un in (range(6) if "B" in L0 else ()):
                    slot = next_unit(d_win[un])
                    isk = un >= 4
                    gcol = 2 if isk else 0
                    for g in range(NTG):
                        s = it % 2
                        it += 1
                        bA, bB, bC = s * 3, s * 3 + 1, s * 3 + 2
                        tok = slice(g * TGS, (g + 1) * TGS)

                        def fn(e, slot=slot, g=g, bA=bA):
                            ins = None
                            for k in range(KC):
                                ins = e.matmul(bank(bA), lhsT=wu[slot][:, k, :], rhs=hrhs(k, g), start=(k == 0), stop=(k == KC - 1))
                            return ins
                        P.add("pe", fn, reads=[("wu", slot)] + hkeys(g), writes=psk(bA))
                        for cc_ in range(2):
                            m1_mm(2 * (it - 1) + cc_)
                            m1_load(2 * (it - 1) + cc_ + 3)
                        P.add("act", lambda e, s=s, bA=bA: e.activation(out=qsq[s][:, :], in_=bank(bA), func=AF.Square),
                              reads=psk(bA), writes=[("qsq", s)])
                        P.add("act", lambda e, s=s, bA=bA: e.activation(out=qbf[s][:, :], in_=bank(bA), func=AF.Copy),
                              reads=psk(bA), writes=[("qbf", s)])
                        P.add("pe", lambda e, s=s, bB=bB: e.matmul(bank(bB), lhsT=bonesr[:, :], rhs=qsq[s][:, :], start=True, stop=True),
                              reads=["bonesr", ("qsq", s)], writes=psk(bB))
                        P.add("pe", lambda e, s=s, bC=bC: e.matmul(bank(bC), lhsT=swapT[:, :], rhs=qbf[s][:, :], start=True, stop=True),
                              reads=["swapT", ("qbf", s)], writes=psk(bC))
                        P.add("act", lambda e, s=s, bB=bB: e.activation(out=sd[s][:, :], in_=bank(bB), func=AF.Ln, bias=EPS, scale=1.0 / 64.0),
                              reads=psk(bB), writes=[("sd", s)])
                        P.add("act", lambda e, s=s: e.activation(out=rstd[s][:, :], in_=sd[s][:, :], func=AF.Exp, scale=-0.5), reads=[("sd", s)], writes=[("rstd", s)])
                        P.add("dve", lambda e, s=s, bA=bA, tok=tok, gcol=gcol: e.scalar_tensor_tensor(
                            out=t1[s][:, :], in0=bank(bA), scalar=gains[:, gcol:gcol + 1], in1=ropeC[:, tok], op0=ALU.mult, op1=ALU.mult),
                            reads=psk(bA) + ["gains", "ropeC"], writes=[("t1", s)])
                        P.add("dve", lambda e, s=s, bC=bC, tok=tok, gcol=gcol: e.scalar_tensor_tensor(
                            out=t2[s][:, :], in0=bank(bC), scalar=gains[:, gcol + 1:gcol + 2], in1=ropeS[:, tok], op0=ALU.mult, op1=ALU.mult),
                            reads=psk(bC) + ["gains", "ropeS"], writes=[("t2", s)])
                        P.add("pool", lambda e, s=s: e.tensor_tensor(out=t1[s][:, :], in0=t1[s][:, :], in1=t2[s][:, :], op=ALU.add),
                              reads=[("t1", s), ("t2", s)], writes=[("t1", s)])
                        if not isk:
                            P.add("pool", lambda e, s=s, un=un, tok=tok: e.tensor_tensor(out=qT[:, un, tok], in0=t1[s][:, :], in1=rstd[s][:, :], op=ALU.mult),
                                  reads=[("t1", s), ("rstd", s)], writes=[("qT", un, g)])
                        else:
                            kv = un - 4
                            P.add("pool", lambda e, s=s: e.tensor_tensor(out=t2[s][:, :], in0=t1[s][:, :], in1=rstd[s][:, :], op=ALU.mult),
                                  reads=[("t1", s), ("rstd", s)], writes=[("t2", s)])
                            P.add("pool", lambda e, s=s, kv=kv, tok=tok: e.tensor_copy(out=kTz[0:64, kv, 0, tok], in_=t2[s][0:64, :]),
                                  reads=[("t2", s), "kTz0"], writes=[("kTd", kv, g, 0)])
                            P.add("pool", lambda e, s=s, kv=kv, tok=tok: e.tensor_copy(out=kTz[64:128, kv, 1, tok], in_=t2[s][64:128, :]),
                                  reads=[("t2", s), "kTz0"], writes=[("kTd", kv, g, 1)])
                            P.add("sp", lambda e, s=s, kv=kv, tok=tok: e.dma_start(out=d["o_kT"][kv, :, tok], in_=t2[s][0:64, :]),
                                  reads=[("t2", s)], writes=[("o_kT", kv, g)], dma=True)
                P.add("dve", lambda e: e.tensor_tensor(out=m1["mod"][:, 1].rearrange("p v c -> p (v c)"), in0=PS[:, 6 * 512:6 * 512 + 48],
                                                       in1=m1["bmod"][:, 48:96], op=ALU.add), reads=[("ps", 6), "bmod"], writes=["mod1"])
                m1["derive"](1)
                m1["derive_cross"]()
                P.barrier()
            pvv = pb
            VVd = sb(pvv, "VVd", [128, 32, 128], BF16)
            VVo = sb(pvv, "VVo", [128, 40, 128], BF16)
            with ExitStack() as pq:
                vf = [sb(pq, "vf%d" % i, [128, 4, 128], F32) for i in range(2)]
                vcf = sb(pq, "vcf", [128, 4, 128], F32)
                KCX = os.environ.get("K_C", "kvmpados")
                for kv in (range(2) if "k" in KCX else ()):
                    P.add("pool", lambda e, kv=kv: e.dma_start(out=kTz[0:64, kv, 0, T:T + PAST], in_=d["kcT"][kv, 0:64, :]), writes=[("kTd", kv, "ctx", 0)], dma=True)
                    P.add("pool", lambda e, kv=kv: e.dma_start(out=kTz[64:128, kv, 1, T:T + PAST], in_=d["kcT"][kv, 64:128, :]), writes=[("kTd", kv, "ctx", 1)], dma=True)
                if "v" in KCX:
                    P.add("sp", lambda e: e.dma_start(out=vcf[:], in_=d["vc"].rearrange("(t p) c -> p t c", p=128)), writes=["vcf"], dma=True)
                P.add("pool", lambda e: e.memset(VVd[:, :, 64:128], 1.0), writes=["VVd1"])
                P.add("pool", lambda e: e.memset(VVo[:, :, 64:128], 1.0), writes=["VVo1"])
                P.add("dve", lambda e: e.tensor_scalar(out=VVo[:, :, 64:128], in0=VVo[:, :, 64:128], scalar1=om, scalar2=None, op0=ALU.mult),
                      reads=["VVo1", "flags"], writes=["VVo1"])
                P.add("dve", lambda e: e.tensor_scalar(out=VVo[:, 32:40, 0:64], in0=vcf[:].rearrange("p t (g c) -> p (t g) c", g=2),
                                                        scalar1=om, scalar2=None, op0=ALU.mult),
                      reads=["vcf", "flags"], writes=[("VVo", 4)])
                slot = next_unit(d_win[6])
                for t4 in (range(4) if ("C" in L0 and "p" in KCX) else ()):
                    b = 6 + t4 % 2
                    s = t4 % 2

                    def fn(e, slot=slot, t4=t4, b=b):
                        ins = None
                        for i in range(4):
                            tt = t4 * 4 + i
                            for k in range(KC):
                                ins = e.matmul(bank(b)[:, i * 128:(i + 1) * 128], lhsT=hT[:, k, tt * 128:(tt + 1) * 128], rhs=wu[slot][:, k, :],
                                               start=(k == 0), stop=(k == KC - 1))
                        return ins
                    P.add("pe", fn, reads=[("wu", slot)] + hkeys(t4), writes=psk(b))
                    if "a" in KCX:
                        P.add("act", lambda e, s=s, b=b: e.activation(out=vf[s][:].rearrange("p t c -> p (t c)"), in_=bank(b), func=AF.Copy),
                              reads=psk(b), writes=[("vf", s)])
                    if "d" not in KCX:
                        continue
                    P.add("dve", lambda e, t4=t4, b=b: e.tensor_copy(out=VVd[:, t4 * 8:t4 * 8 + 8, 0:64], in_=bank(b).rearrange("p (s c) -> p s c", c=64)),
                          reads=psk(b), writes=[("VVd", t4)])
                    if "o" not in KCX:
                        continue
                    P.add("dve", lambda e, s=s, t4=t4: e.tensor_scalar(out=VVo[:, t4 * 8:t4 * 8 + 8, 0:64],
                                                                         in0=vf[s][:].rearrange("p t (g c) -> p (t g) c", g=2),
                                                                         scalar1=om, scalar2=None, op0=ALU.mult),
                          reads=[("vf", s), "flags"], writes=[("VVo", t4)])
                    if "s" not in KCX:
                        continue
                    P.add("sp", lambda e, s=s, t4=t4: e.dma_start(out=d["o_v"].rearrange("(t p) c -> p t c", p=128)[:, t4 * 4:(t4 + 1) * 4, :], in_=vf[s][:]),
                          reads=[("vf", s)], writes=[("o_v", t4)], dma=True)
                P.barrier()
            with ExitStack() as pd:
                attT = sb(pd, "attT", [128, 4, T], BF16)
                PT = [sb(pd, "PT%d" % i, [128, 2 * TGS], BF16) for i in range(2)]
                rden = [sb(pd, "rden%d" % i, [128, SEG], F32) for i in range(4)]
                its = [(h, g4, kp) for h in (range(8) if "D" in L0 else ()) for g4 in range(NTG) for kp in range(NKC // 2)]
                LA = 1

                def emit_S(n):
                    h, g4, kp = its[n]
                    cq, kv = h // 2, h // 4
                    m = n % 2
                    for j in range(2):
                        kc = kp * 2 + j
                        sbk = 2 * m + j
                        P.add("pe", lambda e, kc=kc, sbk=sbk: e.matmul(bank(sbk), lhsT=kTz[:, kv, h % 2, kc * 128:(kc + 1) * 128],
                                                                       rhs=qT[:, cq, g4 * TGS:(g4 + 1) * TGS], start=True, stop=True),
                              reads=[("kTd", kv, kc // 4 if kc < 16 else "ctx", h % 2), ("qT", cq, g4), "kTz0"], writes=psk(sbk))

                def emit_rest(n):
                    h, g4, kp = its[n]
                    cq, kv = h // 2, h // 4
                    odd = (h % 2 == 1)
                    m = n % 2
                    par = (h * NTG + g4) % 2
                    accb = [4 + par * 2, 5 + par * 2]
                    P.add("act", lambda e: e.activation(out=PT[m][:, :], in_=bank(2 * m, 2), func=AF.Exp, scale=0.125),
                          reads=psk(2 * m, 2), writes=[("PT", m)])
                    for j in range(2):
                        kc = kp * 2 + j
                        for s_ in range(2):
                            seg = g4 * 2 + s_
                            diag = kc < 16 and (kc // 2) == seg
                            lhs = (VVd if diag else VVo)[:, kc * 2 + kv, :]
                            P.add("pe", lambda e, s_=s_, lhs=lhs, j=j, kc=kc: e.matmul(bank(accb[s_])[:, 0:SEG], lhsT=lhs,
                                                                                     rhs=PT[m][:, j * TGS + s_ * SEG:j * TGS + (s_ + 1) * SEG],
                                                                                     start=(kc == 0), stop=(kc == NKC - 1)),
                                  reads=[("PT", m), ("VVd", kc // 4) if diag else ("VVo", kc // 4), "VVd1", "VVo1"], writes=psk(accb[s_]))
                    if kp == NKC // 2 - 1:
                        for s_ in range(2):
                            seg = g4 * 2 + s_
                            rb = par * 2 + s_
                            nr, dr = slice(0, 64), slice(64, 128)
                            orow = slice(64, 128) if odd else slice(0, 64)
                            tok = slice(seg * SEG, (seg + 1) * SEG)
                            P.add("dve", lambda e, s_=s_, rb=rb: e.reciprocal(out=rden[rb][dr, :], in_=bank(accb[s_])[dr, 0:SEG]),
                                  reads=psk(accb[s_]), writes=[("rden", rb)])
                            P.add("dve", lambda e, s_=s_, rb=rb, orow=orow, tok=tok: e.tensor_tensor(
                                out=attT[orow, cq, tok], in0=bank(accb[s_])[nr, 0:SEG], in1=rden[rb][dr, :], op=ALU.mult),
                                reads=psk(accb[s_]) + [("rden", rb)], writes=[("attT", cq, seg, h % 2)])

                for n in range(len(its) + LA):
                    if n < len(its):
                        emit_S(n)
                    if n - LA >= 0:
                        emit_rest(n - LA)
                P.barrier()
                for j in (range(KC) if "E" in L0 else ()):
                    slot = next_unit(d["wout"][j])
                    half = j % 2

                    def rhs_of(k, g):
                        if k < 4:
                            return attT[:, k, g * TGS:(g + 1) * TGS]
                        return scT[:, k - 4, g * TGS:(g + 1) * TGS]

                    def rk(g):
                        r = [("scT", ch) for ch in range(4)]
                        for cq in range(4):
                            for seg in (2 * g, 2 * g + 1):
                                r += [("attT", cq, seg, 0), ("attT", cq, seg, 1)]
                        return r
                    proj_fm(c, wu[slot], ("wu", slot), half, rhs_of, rk)
                    P.add("dve", lambda e, j=j, half=half: e.scalar_tensor_tensor(out=xT[:, j, :], in0=bank(half * 4, 4), scalar=mv[:, 0, 2, j:j + 1],
                                                                                 in1=xT[:, j, :], op0=ALU.mult, op1=ALU.add),
                          reads=psk(half * 4, 4) + MVK + xk(j), writes=xk(j))
                P.barrier()


def layernorm(c, l, which, final=False):
    P, bank, psk = c["P"], c["bank"], c["psk"]
    xT, hT, mv, lnp, sb, d = c["xT"], c["hT"], c["mv"], c["lnp"], c["sb"], c["d"]
    onesr, MVK = c["onesr"], c["MVK"]
    gi, bi = (0, 1) if which == 1 else (2, 3)
    ghi, bhi = (6, 7) if which == 1 else (8, 9)
    with ExitStack() as ph:
        xr = [sb(ph, "xr%d" % i, [128, KC, TGS], F32R) for i in range(2)]
        xs = [sb(ph, "xs%d" % i, [128, KC, TGS], F32R) for i in range(2)]
        mean = [sb(ph, "mean%d" % i, [128, TGS], F32) for i in range(2)]
        msq = [sb(ph, "msq%d" % i, [128, TGS], F32) for i in range(2)]
        var = [sb(ph, "var%d" % i, [128, TGS], F32) for i in range(2)]
        rstd = [sb(ph, "lrstd%d" % i, [128, TGS], F32) for i in range(2)]
        nmr = [sb(ph, "nmr%d" % i, [128, TGS], F32) for i in range(2)]
        tt = [sb(ph, "ltt%d" % i, [128, TGS], F32) for i in range(4)]
        itc = [0]

        def stats(g):
            s = g % 2
            tok = slice(g * TGS, (g + 1) * TGS)
            bA, bB = 2 * s, 2 * s + 1
            for j in range(KC):
                P.add("dve", lambda e, j=j: e.tensor_copy(out=xr[s][:, j, :], in_=xT[:, j, tok]),
                      reads=[("xT", j, g)], writes=[("xr", s, j)])
                P.add("act", lambda e, j=j: e.activation(out=xs[s][:, j, :], in_=xT[:, j, tok], func=AF.Square),
                      reads=[("xT", j, g)], writes=[("xs", s, j)])

            def fnA(e):
                ins = None
                for j in range(KC):
                    ins = e.matmul(bank(bA), lhsT=onesr[:, :], rhs=xr[s][:, j, :], start=(j == 0), stop=(j == KC - 1))
                return ins

            def fnB(e):
                ins = None
                for j in range(KC):
                    ins = e.matmul(bank(bB), lhsT=onesr[:, :], rhs=xs[s][:, j, :], start=(j == 0), stop=(j == KC - 1))
                return ins
            P.add("pe", fnA, reads=["onesr"] + [("xr", s, j) for j in range(KC)], writes=psk(bA))
            P.add("pe", fnB, reads=["onesr"] + [("xs", s, j) for j in range(KC)], writes=psk(bB))
            P.add("act", lambda e: e.activation(out=mean[s][:, :], in_=bank(bA), func=AF.Copy, scale=1.0 / D),
                  reads=psk(bA), writes=[("mean", s)])
            P.add("pool", lambda e: e.tensor_tensor(out=msq[s][:, :], in0=mean[s][:, :], in1=mean[s][:, :], op=ALU.mult),
                  reads=[("mean", s)], writes=[("msq", s)])
            P.add("dve", lambda e: e.scalar_tensor_tensor(out=var[s][:, :], in0=bank(bB), scalar=1.0 / D, in1=msq[s][:, :],
                                                          op0=ALU.mult, op1=ALU.subtract),
                  reads=psk(bB) + [("msq", s)], writes=[("var", s)])
            P.add("act", lambda e: e.activation(out=var[s][:, :], in_=var[s][:, :], func=AF.Ln, bias=EPS_LN, scale=1.0),
                  reads=[("var", s)], writes=[("var", s)])
            P.add("act", lambda e: e.activation(out=rstd[s][:, :], in_=var[s][:, :], func=AF.Exp, scale=-0.5), reads=[("var", s)], writes=[("lrstd", s)])
            P.add("dve", lambda e: e.scalar_tensor_tensor(out=nmr[s][:, :], in0=mean[s][:, :], scalar=-1.0, in1=rstd[s][:, :],
                                                          op0=ALU.mult, op1=ALU.mult),
                  reads=[("mean", s), ("lrstd", s)], writes=[("nmr", s)])

        def norm(g):
            s = g % 2
            tok = slice(g * TGS, (g + 1) * TGS)
            for j in range(KC):
                r = itc[0] % 4
                itc[0] += 1
                P.add("dve", lambda e, r=r, j=j: e.tensor_tensor(out=tt[r][:, :], in0=xT[:, j, tok], in1=rstd[s][:, :], op=ALU.mult),
                      reads=[("xT", j, g), ("lrstd", s)], writes=[("ltt", r)])
                P.add("dve", lambda e, r=r: e.tensor_tensor(out=tt[r][:, :], in0=tt[r][:, :], in1=nmr[s][:, :], op=ALU.add),
                      reads=[("ltt", r), ("nmr", s)], writes=[("ltt", r)])
                P.add("act", lambda e, r=r, j=j: e.activation(out=xT[:, j, tok], in_=tt[r][:, :], func=AF.Identity,
                                                              bias=lnp[:, l, bi, j:j + 1], scale=lnp[:, l, gi, j:j + 1]),
                      reads=[("ltt", r), "lnp"], writes=[("xT", j, g)])
                if final:
                    P.add("sp", lambda e, j=j: e.dma_start(out=d["o_yT"][j * 128:(j + 1) * 128, tok], in_=xT[:, j, tok]),
                          reads=[("xT", j, g)], writes=[("o_yT", j, g)], dma=True)
                else:
                    P.add("pool", lambda e, r=r, j=j: e.tensor_scalar(out=hT[:, j, tok], in0=tt[r][:, :], scalar1=mv[:, l, ghi, j:j + 1],
                                                                       scalar2=mv[:, l, bhi, j:j + 1], op0=ALU.mult, op1=ALU.add),
                          reads=[("ltt", r)] + MVK, writes=[("hT", j, g)])

        stats(0)
        stats(1)
        norm(0)
        stats(2)
        norm(1)
        stats(3)
        norm(2)
        norm(3)
        P.barrier()


def ffn(c, l):
    P, bank, psk = c["P"], c["bank"], c["psk"]
    xT, hT, mv, sb, d = c["xT"], c["hT"], c["mv"], c["sb"], c["d"]
    xk, MVK, nbm = c["xk"], c["MVK"], c["nbm"]
    HC = NFC // 2

    def hrhs(k, g):
        return hT[:, k, g * TGS:(g + 1) * TGS]

    def hkeys(g):
        return [("hT", k, g) for k in range(KC)]

    with ExitStack() as ph:
        act = sb(ph, "ffact", [128, HC, T], BF16)
        uu = [sb(ph, "ffu%d" % i, [128, T], F32) for i in range(2)]
        sg = sb(ph, "ffsg", [128, T], F32)
        tl = [sb(ph, "fftl%d" % i, [128, T], F32) for i in range(2)]
        wu = [sb(ph, "fwu%d" % i, [128, KC, 128], BF16) for i in range(6)]
        wd = [sb(ph, "fwd%d" % i, [128, HC, 128], BF16) for i in range(3)]
        srcs_up = [d["wup"][l, pas * HC + cc + (NFC if isg else 0)] for pas in range(2) for cc in range(HC) for isg in range(2)]
        srcs_dn = [d["wdn"][l, j, :, pas * HC:(pas + 1) * HC, :] for pas in range(2) for j in range(KC)]
        wsU = WStream(c, wu, "fwu", srcs_up, 5)
        wsD = WStream(c, wd, "fwd", srcs_dn, 1)
        fcw = sb(ph, "fcw_sb", [128, 44, 3], F32)
        nbw = sb(ph, "fnbw", [128, 44, 2], F32)
        P.add("sp", lambda e: e.dma_start(out=fcw[:], in_=d["fcw"][l]), writes=["fcw"], dma=True)
        P.add("dve", lambda e: e.tensor_scalar(out=nbw[:, :, 0], in0=fcw[:, :, 0], scalar1=nbm, scalar2=None, op0=ALU.mult),
              reads=["fcw", "flags"], writes=["fnbw"])
        P.add("dve", lambda e: e.tensor_scalar(out=nbw[:, :, 1], in0=fcw[:, :, 2], scalar1=nbm, scalar2=None, op0=ALU.mult),
              reads=["fcw", "flags", "fnbw"], writes=["fnbw"])
        ucnt = 0
        dcnt = 0
        for pas in range(2):
            for cc in range(HC):
                for isg in range(2):
                    ch = pas * HC + cc + (NFC if isg else 0)
                    wsU.get(ucnt)
                    slot = ucnt % 6
                    ucnt += 1
                    if cc == HC - 2 and isg:
                        wsD.prefetch(dcnt + 1)
                    half = isg
                    proj_fm(c, wu[slot], ("fwu", slot), half, hrhs, hkeys)
                    u = uu[isg]
                    uk = ("ffu", isg)
                    y = bank(half * 4, 4)
                    yk = psk(half * 4, 4)
                    ysb = tl[isg]
                    ysk = ("fftl", isg)
                    P.add("act", lambda e, ysb=ysb, y=y: e.activation(out=ysb[:, :], in_=y, func=AF.Copy), reads=yk, writes=[ysk])
                    P.add("act", lambda e, u=u, ysb=ysb, ch=ch: e.activation(out=u[:, :], in_=ysb[:, :], func=AF.Copy, scale=fcw[:, ch, 1:2]),
                          reads=[ysk, "fcw"], writes=[uk])
                    P.add("dve", lambda e, u=u, ysb=ysb, ch=ch: e.scalar_tensor_tensor(out=u[:, 1:T], in0=ysb[:, 0:T - 1], scalar=fcw[:, ch, 0:1], in1=u[:, 1:T],
                                                                                       op0=ALU.mult, op1=ALU.add), reads=[ysk, "fcw", uk], writes=[uk])
                    P.add("dve", lambda e, u=u, ysb=ysb, ch=ch: e.scalar_tensor_tensor(out=u[:, 0:T - 1], in0=ysb[:, 1:T], scalar=fcw[:, ch, 2:3], in1=u[:, 0:T - 1],
                                                                                       op0=ALU.mult, op1=ALU.add), reads=[ysk, "fcw", uk], writes=[uk])
                    P.add("dve", lambda e, u=u, ysb=ysb, ch=ch: e.scalar_tensor_tensor(out=u[:, SEG:T:SEG], in0=ysb[:, SEG - 1:T - 1:SEG], scalar=nbw[:, ch, 0:1],
                                                                                       in1=u[:, SEG:T:SEG], op0=ALU.mult, op1=ALU.add),
                          reads=[ysk, "fnbw", uk], writes=[uk])
                    P.add("dve", lambda e, u=u, ysb=ysb, ch=ch: e.scalar_tensor_tensor(out=u[:, SEG - 1:T - 1:SEG], in0=ysb[:, SEG:T:SEG], scalar=nbw[:, ch, 1:2],
                                                                                       in1=u[:, SEG - 1:T - 1:SEG], op0=ALU.mult, op1=ALU.add),
                          reads=[ysk, "fnbw", uk], writes=[uk])
                    if isg:
                        P.add("act", lambda e: e.activation(out=sg[:, :], in_=uu[1][:, :], func=AF.Silu), reads=[("ffu", 1)], writes=["ffsg"])
                        P.add("dve", lambda e, cc=cc: e.tensor_tensor(out=act[:, cc, :], in0=sg[:, :], in1=uu[0][:, :], op=ALU.mult),
                              reads=["ffsg", ("ffu", 0)], writes=[("ffact", cc)])
            for j in range(KC):
                wsD.get(dcnt)
                slot = dcnt % 3
                dcnt += 1
                half = j % 2
                proj_fm(c, wd[slot], ("fwd", slot), half, lambda k, g: act[:, k, g * TGS:(g + 1) * TGS],
                        lambda g: [("ffact", k) for k in range(HC)], nk=HC)
                P.add("dve", lambda e, j=j, half=half: e.scalar_tensor_tensor(out=xT[:, j, :], in0=bank(half * 4, 4), scalar=mv[:, l, 5, j:j + 1],
                                                                             in1=xT[:, j, :], op0=ALU.mult, op1=ALU.add),
                      reads=psk(half * 4, 4) + MVK + xk(j), writes=xk(j))
        P.barrier()


def hgrn_mixer(c):
    P, nc, PS, bank, psk = c["P"], c["nc"], c["PS"], c["bank"], c["psk"]
    xT, hT, mv, sb, d = c["xT"], c["hT"], c["mv"], c["sb"], c["d"]
    xk, MVK = c["xk"], c["MVK"]
    keep, onesr, identb = c["keep"], c["onesr"], c["identb"]
    PSb = PS.bitcast(BF16)
    HL = 512
    NP = T // HL
    NR = 4

    def hrhs(k, g):
        return hT[:, k, g * TGS:(g + 1) * TGS]

    def hkeys(g):
        return [("hT", k, g) for k in range(KC)]

    with ExitStack() as ph:
        yT = sb(ph, "yT", [128, 4, T], BF16)
        wu = [sb(ph, "hwu%d" % i, [128, KC, 128], BF16) for i in range(4)]
        hmask = sb(ph, "hmask_sb", [128, 2, 512], BF16)
        trim = sb(ph, "trim", [128, 2, 128], BF16)
        lbl = sb(ph, "lbl_sb", [128, 2, 2, 8], F32)
        lb = sb(ph, "lb", [128, 2, 8], F32)
        hng = sb(ph, "hng_sb", [128, 1], F32)
        P.add("pool", lambda e: e.dma_start(out=hmask[:], in_=d["hmask"][:]), writes=["hmask"], dma=True)
        P.add("pool", lambda e: e.dma_start(out=trim[:], in_=d["htri"][:]), writes=["trim"], dma=True)
        P.add("sp", lambda e: e.dma_start(out=lbl[:], in_=d["lbl"][:]), writes=["lbl"], dma=True)
        P.add("sp", lambda e: e.dma_start(out=hng[:, :], in_=d["hng"][:, :]), writes=["hng"], dma=True)
        P.add("dve", lambda e: e.tensor_tensor(out=lb[:, :, :], in0=lbl[:, 0, :, :], in1=lbl[:, 1, :, :], op=ALU.subtract), reads=["lbl"], writes=["lb"])
        P.add("act", lambda e: e.activation(out=lb[:].rearrange("p a b -> p (a b)"), in_=lb[:].rearrange("p a b -> p (a b)"), func=AF.Exp),
              reads=["lb"], writes=["lb"])
        P.add("dve", lambda e: e.tensor_scalar(out=lb[:, :, :], in0=lb[:, :, :], scalar1=1.0, scalar2=None, op0=ALU.add), reads=["lb"], writes=["lb"])
        P.add("dve", lambda e: e.reciprocal(out=lb[:].rearrange("p a b -> p (a b)"), in_=lb[:].rearrange("p a b -> p (a b)")), reads=["lb"], writes=["lb"])
        srcsH = []
        for h in range(8):
            srcsH += [d["hin"][0 * 8 + h], d["hin"][1 * 8 + h], d["hin"][3 * 8 + h], d["hin"][4 * 8 + h], d["hin"][2 * 8 + h]]
            if h % 4 == 3:
                srcsH += [(d["hout"][j][:, 4 * (h // 4):4 * (h // 4) + 4, :], 4) for j in range(KC)]
        wsH = WStream(c, wu, "hwu", srcsH, 3)
        ucount = [0]

        def next_unit(src):
            i = ucount[0]
            ucount[0] += 1
            wsH.get(i)
            return i % 4

        with ExitStack() as pp:
            qf = sb(pp, "qf", [128, T], BF16)
            gA = [sb(pp, "gA%d" % i, [128, HL], F32) for i in range(NP)]
            gB = [sb(pp, "gB%d" % i, [128, HL], F32) for i in range(NP)]
            gC = [sb(pp, "gC%d" % i, [128, HL], F32) for i in range(NP)]
            sqr = [sb(pp, "gsq%d" % i, [128, HL], F32R) for i in range(2)]
            qtL = [sb(pp, "qt%d" % i, [128, T], BF16) for i in range(2)]
            ktL = [sb(pp, "kt%d" % i, [128, T], BF16) for i in range(2)]
            khfm = sb(pp, "khfm", [128, T], BF16)
            khTL = [sb(pp, "khT%d" % i, [128, 16, 128], BF16) for i in range(2)]
            vtm = sb(pp, "vtm", [128, 16, 128], BF16)
            of = sb(pp, "of", [128, T], F32)
            decL = [sb(pp, "dec%d" % i, [128, NCH], F32) for i in range(2)]
            Sf = [sb(pp, "Sf%d" % i, [128, 128], F32) for i in range(NR)]
            Sb = [sb(pp, "Sb%d" % i, [128, 128], BF16) for i in range(NR)]
            Am = [sb(pp, "Am%d" % i, [128, 128], BF16) for i in range(2)]
            HST = os.environ.get("K_H", "vgtrn")
            for h in range(int(os.environ.get("K_NH", "8"))):
                s_q = next_unit(d["hin"][0 * 8 + h])
                proj_fm(c, wu[s_q], ("hwu", s_q), 0, hrhs, hkeys)
                s_i = next_unit(d["hin"][1 * 8 + h])
                P.add("act", lambda e: e.activation(out=qf[:, :], in_=bank(0, 4), func=AF.Copy), reads=psk(0, 4), writes=["qf"])
                for t4 in (range(4) if "v" in HST else ()):
                    b = 4 + t4 % 2

                    def fnv(e, t4=t4, b=b, s_i=s_i):
                        ins = None
                        for i in range(4):
                            tt = t4 * 4 + i
                            for k in range(KC):
                                ins = e.matmul(bank(b)[:, i * 128:(i + 1) * 128], lhsT=hT[:, k, tt * 128:(tt + 1) * 128], rhs=wu[s_i][:, k, :],
                                               start=(k == 0), stop=(k == KC - 1))
                        return ins
                    P.add("pe", fnv, reads=[("hwu", s_i)] + hkeys(t4), writes=psk(b))
                    P.add("act" if t4 % 2 == 0 else "dve", (lambda e, t4=t4, b=b: e.activation(out=vtm[:, t4 * 4:(t4 + 1) * 4, :].rearrange("p t c -> p (t c)"), in_=bank(b), func=AF.Copy))
                          if t4 % 2 == 0 else (lambda e, t4=t4, b=b: e.tensor_copy(out=vtm[:, t4 * 4:(t4 + 1) * 4, :].rearrange("p t c -> p (t c)"), in_=bank(b))),
                          reads=psk(b), writes=[("vtm", t4)])
                def gates_for(dd, h=h):
                    qt, kt, dec = qtL[dd], ktL[dd], decL[dd]
                    s_f = next_unit(d["hin"][(3 + dd) * 8 + h])
                    proj_fm(c, wu[s_f], ("hwu", s_f), 0, hrhs, hkeys)
                    lbh = lb[:, dd, h:h + 1]
                    lastoff = CHUNK - 1 if dd == 0 else 0
                    NB = HL // CHUNK

                    def g1(hf):
                        P.add("act", lambda e: e.activation(out=gA[hf][:, :], in_=bank(hf * (HL // 512), HL // 512), func=AF.Exp, scale=-1.0), reads=psk(hf * (HL // 512), HL // 512), writes=[("gA", hf)])

                    def g2(hf):
                        P.add("act", lambda e: e.activation(out=gB[hf][:, :], in_=gA[hf][:, :], func=AF.Ln, bias=1.0, scale=lbh), reads=[("gA", hf), "lb"], writes=[("gB", hf)])

                    def g3(hf):
                        P.add("act", lambda e: e.activation(out=gA[hf][:, :], in_=gA[hf][:, :], func=AF.Ln, bias=1.0, scale=1.0), reads=[("gA", hf)], writes=[("gA", hf)])

                    def g4_(hf):
                        P.add("dve", lambda e: e.tensor_tensor(out=gB[hf][:, :], in0=gB[hf][:, :], in1=gA[hf][:, :], op=ALU.subtract),
                              reads=[("gA", hf), ("gB", hf)], writes=[("gB", hf)])

                    def g5(hf):
                        P.add("act", lambda e: e.activation(out=gA[hf][:, :], in_=gB[hf][:, :], func=AF.Exp), reads=[("gB", hf)], writes=[("gA", hf)])

                    def g6(hf):
                        P.add("dve", lambda e: e.tensor_scalar(out=gA[hf][:, :], in0=gA[hf][:, :], scalar1=-1.0, scalar2=1.0, op0=ALU.mult, op1=ALU.add),
                              reads=[("gA", hf)], writes=[("gA", hf)])

                    def g7(hf):
                        for blk in range(HL // 512):
                            bs = slice(blk * 512, (blk + 1) * 512)
                            if dd == 0:
                                P.add("dve", lambda e, bs=bs: e.tensor_tensor_scan(out=gC[hf][:, bs], data0=hmask[:, 0, :], data1=gB[hf][:, bs], initial=0.0,
                                                                                    op0=ALU.mult, op1=ALU.add), reads=[("gB", hf), "hmask"], writes=[("gC", hf, blk)])
                            else:
                                P.add("dve", lambda e, bs=bs: e.tensor_tensor_scan(out=gC[hf][:, bs][:, ::-1], data0=hmask[:, 1, ::-1], data1=gB[hf][:, bs][:, ::-1],
                                                                                    initial=0.0, op0=ALU.mult, op1=ALU.add),
                                      reads=[("gB", hf), "hmask"], writes=[("gC", hf, blk)])

                    def gck(hf):
                        return [("gC", hf, blk) for blk in range(HL // 512)]

                    def g8(hf):
                        P.add("act", lambda e: e.activation(out=gB[hf][:, :], in_=gC[hf][:, :], func=AF.Exp), reads=gck(hf) + [("gB", hf)], writes=[("gB", hf)])

                    def g9(hf):
                        P.add("act", lambda e: e.activation(out=gC[hf][:, :], in_=gC[hf][:, :], func=AF.Exp, scale=-1.0), reads=gck(hf), writes=gck(hf))

                    def g10(hf):
                        tok = slice(hf * HL, (hf + 1) * HL)
                        P.add("dve", lambda e: e.tensor_tensor(out=qt[:, tok], in0=qf[:, tok], in1=gB[hf][:, :], op=ALU.mult),
                              reads=["qf", ("gB", hf)], writes=[("qt", dd, hf)])

                    def g11(hf):
                        P.add("dve", lambda e: e.tensor_tensor(out=gA[hf][:, :], in0=gA[hf][:, :], in1=gC[hf][:, :], op=ALU.mult),
                              reads=[("gA", hf)] + gck(hf), writes=[("gA", hf)])

                    def g12(hf):
                        tok = slice(hf * HL, (hf + 1) * HL)
                        P.add("pool", lambda e: e.tensor_copy(out=kt[:, tok], in_=gA[hf][:, :]), reads=[("gA", hf)], writes=[("kt", dd, hf)])

                    def g13(hf):
                        tok = slice(hf * HL, (hf + 1) * HL)
                        ebl = gB[hf][:, lastoff::CHUNK]
                        P.add("dve", lambda e: e.tensor_tensor(out=khfm[:, tok].rearrange("p (c l) -> p c l", l=CHUNK),
                                                               in0=gA[hf][:, :].rearrange("p (c l) -> p c l", l=CHUNK),
                                                               in1=ebl.unsqueeze(2).to_broadcast([128, NB, CHUNK]), op=ALU.mult),
                              reads=[("gA", hf), ("gB", hf)], writes=[("khfm", hf)])

                    def g14(hf):
                        ebl = gB[hf][:, lastoff::CHUNK]
                        P.add("pool", lambda e: e.tensor_copy(out=dec[:, hf * NB:(hf + 1) * NB], in_=ebl), reads=[("gB", hf)], writes=[("dec", dd, hf)])

                    return [g1, g2, g3, g4_, g5, g6, g7, g8, g9, g10, g11, g12, g13, g14]

                def transposes_for(dd):
                    khT = khTL[dd]
                    for t8 in (range(2) if "t" in HST else ()):
                        def fnt(e, t8=t8):
                            ins = None
                            for i in range(8):
                                tt = t8 * 8 + i
                                ins = e.transpose(out=PSb[:, (6 + t8) * 1024 + i * 128:(6 + t8) * 1024 + (i + 1) * 128], in_=khfm[:, tt * 128:(tt + 1) * 128],
                                                  identity=identb[:, :])
                            return ins
                        P.add("pe", fnt, reads=[("khfm", hf) for hf in range((t8 * 1024) // HL, ((t8 + 1) * 1024) // HL)] + ["identb"], writes=psk(6 + t8))
                        P.add("act", lambda e, t8=t8: e.activation(out=khT[:, t8 * 8:(t8 + 1) * 8, :].rearrange("p t c -> p (t c)"),
                                                                    in_=PSb[:, (6 + t8) * 1024:(7 + t8) * 1024], func=AF.Copy),
                              reads=psk(6 + t8), writes=[("khT", dd, t8)])

                def recurrence_for(dd, inject, h=h):
                    qt, kt, dec, khT = qtL[dd], ktL[dd], decL[dd], khTL[dd]
                    cur = 0
                    P.add("sp", lambda e, dd=dd, h=h: e.dma_start(out=Sf[0][:, :], in_=d["state"][dd, h]), writes=[("Sf", 0)], dma=True)
                    P.add("act", lambda e: e.activation(out=Sb[0][:, :], in_=Sf[0][:, :], func=AF.Copy), reads=[("Sf", 0)], writes=[("Sb", 0)])
                    tiles = list(range(16)) if dd == 0 else list(range(15, -1, -1))
                    order = list(range(4)) if dd == 0 else list(range(3, -1, -1))

                    def emit_scores(ti, tt):
                        bS = ti % 2
                        tsl = slice(tt * 128, (tt + 1) * 128)
                        P.add("pe", lambda e: e.matmul(bank(bS)[:, 0:128], lhsT=kt[:, tsl], rhs=qt[:, tsl], start=True, stop=True),
                              reads=[("kt", dd, (tt * 128) // HL), ("qt", dd, (tt * 128) // HL)], writes=psk(bS))

                    def emit_U(tt, i):
                        kw = dict(tile_position=(96, 0)) if i == 3 else {}
                        P.add("pe", lambda e: e.matmul(bank(4 + i)[:, 0:128], lhsT=khT[32 * i:32 * i + 32, tt, :],
                                                       rhs=vtm[32 * i:32 * i + 32, tt, :], start=True, stop=True, **kw),
                              reads=[("khT", dd, tt // 8), ("vtm", tt // 4)], writes=psk(4 + i))

                    emit_scores(0, tiles[0])
                    for i in order:
                        emit_U(tiles[0], i)
                    for ti, tt in enumerate(tiles):
                        par = ti % 2
                        bS, bO = par, 2 + par
                        tsl = slice(tt * 128, (tt + 1) * 128)
                        hfk = (tt * 128) // HL
                        nxt_t = tiles[ti + 1] if ti + 1 < len(tiles) else None
                        P.add("dve", lambda e, par=par, bS=bS, dd=dd: e.tensor_tensor(out=Am[par][:, :], in0=bank(bS)[:, 0:128], in1=trim[:, dd, :], op=ALU.mult),
                              reads=psk(bS) + ["trim"], writes=[("Am", par)])
                        P.add("pe", lambda e, par=par, bO=bO, tt=tt: e.matmul(bank(bO)[:, 0:128], lhsT=vtm[:, tt, :], rhs=Am[par][:, :], start=True, stop=False),
                              reads=[("Am", par), ("vtm", tt // 4)], writes=psk(bO))
                        if nxt_t is not None:
                            emit_scores(ti + 1, nxt_t)
                        for oi, i in enumerate(order):
                            cch = tt * 4 + i
                            csl = slice(cch * CHUNK, (cch + 1) * CHUNK)
                            P.add("pe", lambda e, bO=bO, i=i, cur=cur, csl=csl, oi=oi: e.matmul(bank(bO)[:, i * CHUNK:(i + 1) * CHUNK], lhsT=Sb[cur][:, :], rhs=qt[:, csl],
                                                                                                 start=False, stop=(oi == 3)),
                                  reads=[("Sb", cur), ("qt", dd, hfk)], writes=psk(bO))
                            nxt = (cur + 1) % NR
                            P.add("dve", lambda e, nxt=nxt, cur=cur, cch=cch, i=i: e.scalar_tensor_tensor(
                                out=Sf[nxt][:, :], in0=Sf[cur][:, :], scalar=dec[:, cch:cch + 1], in1=bank(4 + i)[:, 0:128], op0=ALU.mult, op1=ALU.add),
                                reads=[("Sf", cur), ("dec", dd, cch // (HL // CHUNK))] + psk(4 + i), writes=[("Sf", nxt)])
                            cur = nxt
                            seg_end = (cch % 8 == 7) if dd == 0 else (cch % 8 == 0)
                            if seg_end:
                                seg = cch // 8
                                P.add("sp", lambda e, cur=cur, seg=seg, dd=dd, h=h: e.dma_start(out=d["o_s"][seg, dd, h], in_=Sf[cur][:, :]),
                                      reads=[("Sf", cur)], writes=[("o_s", seg, dd, h)], dma=True)
                                last = (cch == NCH - 1) if dd == 0 else (cch == 0)
                                if not last:
                                    nxt = (cur + 1) % NR
                                    P.add("dve", lambda e, nxt=nxt, cur=cur: e.tensor_scalar(out=Sf[nxt][:, :], in0=Sf[cur][:, :], scalar1=keep, scalar2=None, op0=ALU.mult),
                                          reads=[("Sf", cur), "flags"], writes=[("Sf", nxt)])
                                    cur = nxt
                            P.add("act", lambda e, cur=cur: e.activation(out=Sb[cur][:, :], in_=Sf[cur][:, :], func=AF.Copy), reads=[("Sf", cur)], writes=[("Sb", cur)])
                            if nxt_t is not None:
                                emit_U(nxt_t, i)
                            if inject is not None:
                                inject()
                        if dd == 0:
                            P.add("act", lambda e, bO=bO, tsl=tsl: e.activation(out=of[:, tsl], in_=bank(bO)[:, 0:128], func=AF.Copy), reads=psk(bO), writes=[("of", tt)])
                        else:
                            P.add("dve", lambda e, bO=bO, tsl=tsl: e.tensor_tensor(out=of[:, tsl], in0=bank(bO)[:, 0:128], in1=of[:, tsl], op=ALU.add),
                                  reads=psk(bO) + [("of", tt)], writes=[("of", tt)])

                st0 = gates_for(0)
                for st in st0:
                    for hf in range(NP):
                        st(hf)
                transposes_for(0)
                st1 = gates_for(1)
                for hf in range(NP):
                    st1[0](hf)
                pend = [(st, hf) for st in st1[1:] for hf in range(NP)]

                def inject():
                    for _ in range(1):
                        if pend:
                            st_, hf_ = pend.pop(0)
                            st_(hf_)
                recurrence_for(0, inject)
                while pend:
                    st_, hf_ = pend.pop(0)
                    st_(hf_)
                transposes_for(1)
                recurrence_for(1, None)
                if "n" not in HST:
                    continue
                s_g = next_unit(d["hin"][2 * 8 + h])
                proj_fm(c, wu[s_g], ("hwu", s_g), 1, hrhs, hkeys)
                TPP = HL // 128

                def n1(p):
                    tok = slice(p * HL, (p + 1) * HL)
                    ofk = [("of", tt) for tt in range(p * TPP, (p + 1) * TPP)]
                    P.add("act", lambda e: e.activation(out=sqr[p % 2][:, :], in_=of[:, tok], func=AF.Square), reads=ofk, writes=[("gsq", p % 2)])

                def n2(p):
                    P.add("pe", lambda e: e.matmul(bank(p), lhsT=onesr[:, :], rhs=sqr[p % 2][:, :], start=True, stop=True),
                          reads=[("gsq", p % 2), "onesr"], writes=psk(p))

                def n3(p):
                    P.add("act", lambda e: e.activation(out=gA[p][:, :], in_=bank(p), func=AF.Ln, bias=EPS, scale=1.0 / 128.0), reads=psk(p), writes=[("gA", p)])

                def n4(p):
                    P.add("act", lambda e: e.activation(out=gA[p][:, :], in_=gA[p][:, :], func=AF.Exp, scale=-0.5), reads=[("gA", p)], writes=[("gA", p)])

                def n5(p):
                    P.add("act", lambda e: e.activation(out=gB[p][:, :], in_=bank(4 + p), func=AF.Silu), reads=psk(4 + p), writes=[("gB", p)])

                def n6(p):
                    tok = slice(p * HL, (p + 1) * HL)
                    ofk = [("of", tt) for tt in range(p * TPP, (p + 1) * TPP)]
                    P.add("dve", lambda e: e.scalar_tensor_tensor(out=gA[p][:, :], in0=of[:, tok], scalar=hng[:, 0:1], in1=gA[p][:, :], op0=ALU.mult, op1=ALU.mult),
                          reads=ofk + [("gA", p), "hng"], writes=[("gA", p)])

                def n7(p, h=h):
                    tok = slice(p * HL, (p + 1) * HL)
                    P.add("pool", lambda e: e.tensor_tensor(out=yT[:, h % 4, tok], in0=gA[p][:, :], in1=gB[p][:, :], op=ALU.mult),
                          reads=[("gA", p), ("gB", p)], writes=[("yT", h % 4, p)])

                for p in range(NP):
                    n1(p)
                    n2(p)
                for st in (n3, n4, n5, n6, n7):
                    for p in range(NP):
                        st(p)
                if h % 4 == 3:
                    for j in range(KC):
                        slot = next_unit(None)
                        half = j % 2
                        proj_fm(c, wu[slot], ("hwu", slot), half, lambda k, g: yT[:, k, g * TGS:(g + 1) * TGS],
                                lambda g: [("yT", k, g) for k in range(4)], nk=4)
                        P.add("dve", lambda e, j=j, half=half: e.scalar_tensor_tensor(out=xT[:, j, :], in0=bank(half * 4, 4), scalar=mv[:, 1, 2, j:j + 1],
                                                                                     in1=xT[:, j, :], op0=ALU.mult, op1=ALU.add),
                              reads=psk(half * 4, 4) + MVK + xk(j), writes=xk(j))
            P.barrier()
        P.barrier()


def _units(W, cols_list):
    K = W.shape[0]
    out = np.empty((len(cols_list), 128, K // 128, 128), dtype=np.float32)
    for i, cols in enumerate(cols_list):
        blk = W[:, cols]
        out[i] = blk.reshape(K // 128, 128, 128).transpose(1, 0, 2)
    return out


def prep_inputs(inp):
    f = lambda a: np.ascontiguousarray(np.asarray(a, dtype=np.float32))
    x_prompt, x_sample = f(inp["x_prompt"]), f(inp["x_sample"])
    cache_k, cache_v, state = f(inp["cache_k"]), f(inp["cache_v"]), f(inp["state_hgrn"])
    cvec, c_ctx = f(inp["c"]), f(inp["c_ctx"])
    shared = {}
    shared["w_mod"] = f(inp["w_mod"])
    bm = f(inp["b_mod"])
    shared["bmod"] = np.ascontiguousarray(np.concatenate([bm[l].reshape(48, 128).T for l in range(2)], axis=1))
    lnp = np.empty((128, 2, 4, KC), np.float32)
    for l in range(2):
        for i, nm in enumerate(("ln1_g", "ln1_b", "ln2_g", "ln2_b")):
            lnp[:, l, i, :] = f(inp[nm])[l].reshape(KC, 128).T
    shared["lnp"] = lnp
    Win = f(inp["attn_w_in"])[0]
    ar = np.arange
    cols = [ar(c * 128, (c + 1) * 128) for c in range(4)]
    cols += [np.concatenate([ar(512, 576), ar(512, 576)]), np.concatenate([ar(576, 640), ar(576, 640)])]
    cols += [ar(640, 768)]
    cols += [ar(768 + c * 128, 768 + (c + 1) * 128) for c in range(12)]
    shared["attn_in_u"] = _units(Win, cols)
    sw = _swap_idx()
    qg, kg = f(inp["attn_q_gain"])[0], f(inp["attn_k_gain"])[0]
    gains = np.stack([np.tile(qg, 2), np.tile(qg[sw], 2), np.tile(kg, 2), np.tile(kg[sw], 2)], axis=1)
    shared["gains"] = np.ascontiguousarray(gains.astype(np.float32))
    swapT = np.zeros((128, 128), np.float32)
    for m in range(128):
        p = (m // 64) * 64 + sw[m % 64]
        swapT[p, m] = 1.0
    bones = np.zeros((128, 128), np.float32)
    bones[:64, :64] = 1.0
    bones[64:, 64:] = 1.0
    shared["consts"] = np.stack([swapT, bones, np.eye(128, dtype=np.float32)])
    scw = f(inp["sconv_w"])[0]
    shared["scw"] = np.ascontiguousarray(scw.reshape(3, 4, 128).transpose(2, 1, 0))
    shared["attn_out_u"] = _units(f(inp["attn_w_out"])[0], [ar(j * 128, (j + 1) * 128) for j in range(8)])
    wup = f(inp["ffn_w_up"])
    shared["ffn_up_u"] = np.stack([_units(wup[l], [ar(c * 128, (c + 1) * 128) for c in range(44)]) for l in range(2)])
    fcw = f(inp["ffn_conv_w"])
    shared["fcw"] = np.ascontiguousarray(fcw.reshape(2, 3, 44, 128).transpose(0, 3, 2, 1))
    wdn = f(inp["ffn_w_down"])
    shared["ffn_dn_u"] = np.stack([_units(wdn[l], [ar(j * 128, (j + 1) * 128) for j in range(8)]) for l in range(2)])
    hin = f(inp["hgrn_w_in"])[0]
    shared["hgrn_in_u"] = _units(hin, [ar(kind * 1024 + h * 128, kind * 1024 + (h + 1) * 128) for kind in range(5) for h in range(8)])
    lbl = f(inp["hgrn_lb_logits"])
    shared["lbl"] = np.ascontiguousarray(lbl.reshape(2, 2, 8, 128).transpose(3, 0, 1, 2))
    shared["hng"] = np.ascontiguousarray(f(inp["hgrn_norm_g"])[0].reshape(128, 1))
    shared["hgrn_out_u"] = _units(f(inp["hgrn_w_out"])[0], [ar(j * 128, (j + 1) * 128) for j in range(8)])
    hm = np.zeros((128, 2, 512), np.float32)
    t = np.arange(512)
    hm[:, 0, :] = (t % CHUNK != 0).astype(np.float32)[None, :]
    hm[:, 1, :] = (t % CHUNK != CHUNK - 1).astype(np.float32)[None, :]
    sidx = np.arange(128)[:, None]
    tidx = np.arange(128)[None, :]
    same = (sidx // CHUNK) == (tidx // CHUNK)
    ht = np.zeros((128, 2, 128), np.float32)
    ht[:, 0, :] = (same & (sidx <= tidx)).astype(np.float32)
    ht[:, 1, :] = (same & (sidx >= tidx)).astype(np.float32)
    shared["hscan"] = hm
    shared["htri"] = ht
    tok = np.arange(T)
    row = (tok // 64).astype(np.float32)
    colp = (tok % 64).astype(np.float32)
    inv = (10000.0 ** (-np.arange(0, 32, 2, dtype=np.float32) / 32.0)).astype(np.float32)
    C = np.zeros((64, T), np.float32)
    S = np.zeros((64, T), np.float32)
    for dd in range(64):
        blk = dd // 16
        pos = row if blk < 2 else colp
        ang = pos * inv[dd % 16]
        C[dd] = np.cos(ang)
        S[dd] = -np.sin(ang) if blk % 2 == 0 else np.sin(ang)
    ropeC_s, ropeS_s = np.tile(C, (2, 1)), np.tile(S, (2, 1))
    ropeC_p, ropeS_p = np.ones((128, T), np.float32), np.zeros((128, T), np.float32)
    maps = []
    for core in range(8):
        m = dict(shared)
        if core < 4:
            b = core
            xt = x_sample[b]
            m["cond"] = np.ascontiguousarray(cvec[b].reshape(KC, 128).T)
            fl = np.zeros((128, 4), np.float32)
            fl[:, 0] = 1.0
            fl[:, 1] = 1.0
            m["ropeC"], m["ropeS"] = ropeC_s, ropeS_s
            kc = cache_k[b, 0]
            m["kcT"] = np.ascontiguousarray(np.stack([np.tile(kc[:, g, :].T, (2, 1)) for g in range(2)]))
            m["vc"] = np.ascontiguousarray(cache_v[b, 0].reshape(PAST, 128))
            m["state"] = np.ascontiguousarray(state[b, 0])
        else:
            p0 = (core - 4) * 8
            xt = x_prompt[p0:p0 + 8].reshape(T, D)
            m["cond"] = np.ascontiguousarray(c_ctx.reshape(KC, 128).T)
            fl = np.zeros((128, 4), np.float32)
            fl[:, 2] = -1.0
            m["ropeC"], m["ropeS"] = ropeC_p, ropeS_p
            m["kcT"] = np.zeros((2, 128, PAST), np.float32)
            m["vc"] = np.zeros((PAST, 128), np.float32)
            m["state"] = np.zeros((2, 8, 128, 128), np.float32)
        m["flags"] = fl
        m["xT"] = np.ascontiguousarray(xt.T)
        maps.append(m)
    return maps


_CACHE = {}


def kernel(**inputs):
    maps = prep_inputs(inputs)
    if "nc" not in _CACHE:
        _CACHE["nc"] = build()[0]
    nc = _CACHE["nc"]
    res = run_bass_kernel_spmd(nc, maps, core_ids=list(range(8)))
    r = res.results
    y_s = np.stack([r[c]["yT"].T for c in range(4)])
    y_p = np.concatenate([r[c]["yT"].T.reshape(8, SEG, D) for c in range(4, 8)])
    nk = np.concatenate([r[c]["koutT"].transpose(2, 0, 1).reshape(8, 1, SEG, 2, 64) for c in range(4, 8)])
    nv = np.concatenate([r[c]["vout"].reshape(8, 1, SEG, 2, 64) for c in range(4, 8)])
    ns = np.concatenate([r[c]["sout"].reshape(8, 1, 2, 8, 128, 128) for c in range(4, 8)])
    return (np.ascontiguousarray(y_p, dtype=np.float32), np.ascontiguousarray(y_s, dtype=np.float32),
            np.ascontiguousarray(nk, dtype=np.float32), np.ascontiguousarray(nv, dtype=np.float32),
            np.ascontiguousarray(ns, dtype=np.float32))
```

```python
import os
import numpy as np
import concourse.bass as bass
import concourse.mybir as mybir
from concourse.bass_utils import run_bass_kernel_spmd
from contextlib import ExitStack

F32 = mybir.dt.float32
F32R = mybir.dt.float32r
BF16 = mybir.dt.bfloat16
AF = mybir.ActivationFunctionType
ALU = mybir.AluOpType

T = 2048
D = 1024
KC = 8
NTG = 4
TGS = 512
NSEG = 8
SEG = 256
ALPHA = 4.0 ** 0.25
EPS = 1e-6
EPS_LN = EPS / (ALPHA * ALPHA)
DFF = 2816
NFC = 22
PAST = 512
NKC = 20
CHUNK = 32
NCH = T // CHUNK


class Prog:
    ENGS = ("pe", "act", "dve", "pool", "sp")

    def __init__(self, nc, es, n_dma_sems=12):
        self.nc = nc
        self.eng = {"pe": nc.tensor, "act": nc.scalar, "dve": nc.vector, "pool": nc.gpsimd, "sp": nc.sync}
        self.esem = {e: es.enter_context(nc.semaphore("e_" + e)) for e in self.ENGS}
        self.nds = n_dma_sems
        self.dsem = {e: [es.enter_context(nc.semaphore("d_%s_%d" % (e, i))) for i in range(n_dma_sems)]
                     for e in ("sp", "pool")}
        self.cnt = {e: 0 for e in self.ENGS}
        self.ndma = {e: 0 for e in self.dsem}
        self.dval = {}
        self.waited = {e: {} for e in self.ENGS}
        self.last_w = {}
        self.readers = {}
        self.nops = {e: 0 for e in self.ENGS}

    def _wait(self, e, sem, val):
        w = self.waited[e]
        if w.get(sem.num, 0) >= val:
            return
        w[sem.num] = val
        self.eng[e].wait_ge(sem, val)

    def _wait_tok(self, e, tok, is_dma_op):
        if tok[0] == "E":
            _, f, c = tok
            if f == "pe" and e == "pe" and not is_dma_op:
                return
            self._wait(e, self.esem[f], c)
        else:
            _, q, i, v = tok
            self._wait(e, self.dsem[q][i], v)

    def add(self, e, fn, reads=(), writes=(), dma=False):
        psr = [k for k in reads if isinstance(k, tuple) and k[0] == "ps"]
        if psr:
            writes = list(writes) + [k for k in psr if k not in writes]
        deps = set()
        for k in reads:
            w = self.last_w.get(k)
            if w is not None:
                deps.add(w)
        for k in writes:
            w = self.last_w.get(k)
            if w is not None:
                deps.add(w)
            for r in self.readers.get(k, ()):
                deps.add(r)
        for tok in sorted(deps):
            self._wait_tok(e, tok, dma)
        eng = self.eng[e]
        if dma:
            i = self.ndma[e] % self.nds
            self.ndma[e] += 1
            prev = self.dval.get((e, i), 0)
            if prev:
                self._wait(e, self.dsem[e][i], prev)
            ins = fn(eng)
            ins.then_inc(self.dsem[e][i], 16)
            self.dval[(e, i)] = prev + 16
            tok = ("D", e, i, prev + 16)
        else:
            ins = fn(eng)
            self.cnt[e] += 1
            ins.then_inc(self.esem[e], 1)
            tok = ("E", e, self.cnt[e])
        self.nops[e] += 1
        for k in writes:
            self.last_w[k] = tok
            self.readers[k] = []
        for k in reads:
            self.readers.setdefault(k, []).append(tok)
        return tok

    def barrier(self):
        for e in self.ENGS:
            for f in self.ENGS:
                if f != e and self.cnt[f] > 0:
                    self._wait(e, self.esem[f], self.cnt[f])
            for (q, i), v in self.dval.items():
                self._wait(e, self.dsem[q][i], v)
        self.last_w.clear()
        self.readers.clear()


def _swap_idx():
    idx = np.zeros(64, dtype=np.int64)
    for d in range(64):
        blk = d // 16
        idx[d] = d + 16 if blk % 2 == 0 else d - 16
    return idx


def build(n_dbg=None, upto=99):
    nc = bass.Bass("TRN2", target_bir_lowering=False)
    es = ExitStack()
    P = Prog(nc, es)

    def din(name, shape, dt=F32):
        return nc.dram_tensor(name, list(shape), dt, kind="ExternalInput").ap()

    def dout(name, shape, dt=F32):
        return nc.dram_tensor(name, list(shape), dt, kind="ExternalOutput").ap()

    sbn = [0]

    def sb(stack, name, shape, dt):
        sbn[0] += 1
        return stack.enter_context(nc.sbuf_tensor("s%d_%s" % (sbn[0], name), list(shape), dt))

    d_xT = din("xT", [D, T])
    d_cond = din("cond", [128, KC])
    d_flags = din("flags", [128, 4])
    d_ropeC = din("ropeC", [128, T])
    d_ropeS = din("ropeS", [128, T])
    d_kcT = din("kcT", [2, 128, PAST])
    d_vc = din("vc", [PAST, 128])
    d_state = din("state", [2, 8, 128, 128])
    d_wmod = din("w_mod", [2, D, 6 * D])
    d_bmod = din("bmod", [128, 96])
    d_lnp = din("lnp", [128, 2, 4, KC])
    d_win = din("attn_in_u", [19, 128, KC, 128])
    d_gains = din("gains", [128, 4])
    d_consts = din("consts", [3, 128, 128])
    d_scw = din("scw", [128, 4, 3])
    d_wout = din("attn_out_u", [8, 128, KC, 128])
    d_wup = din("ffn_up_u", [2, 44, 128, KC, 128])
    d_fcw = din("fcw", [2, 128, 44, 3])
    d_wdn = din("ffn_dn_u", [2, 8, 128, NFC, 128])
    d_hin = din("hgrn_in_u", [40, 128, KC, 128])
    d_lbl = din("lbl", [128, 2, 2, 8])
    d_hng = din("hng", [128, 1])
    d_hout = din("hgrn_out_u", [8, 128, KC, 128])
    d_hmask = din("hscan", [128, 2, 512])
    d_htri = din("htri", [128, 2, 128])

    o_yT = dout("yT", [D, T])
    o_kT = dout("koutT", [2, 64, T])
    o_v = dout("vout", [T, 128])
    o_s = dout("sout", [NSEG, 2, 8, 128, 128])

    dbg = {}

    def tap(name, ap, shape, reads, dt=F32):
        if n_dbg is None:
            return
        d = dout("dbg_" + name, shape, dt)
        n_dbg[name] = shape
        P.add("sp", lambda e: e.dma_start(out=d, in_=ap), reads=reads, writes=["dbg_" + name], dma=True)

    xT = sb(es, "xT_sb", [128, KC, T], F32)
    hT = sb(es, "hT_sb", [128, KC, T], BF16)
    mv = sb(es, "mv", [128, 2, 10, KC], F32)
    lnp = sb(es, "lnp_sb", [128, 2, 4, KC], F32)
    flags = sb(es, "flags_sb", [128, 4], F32)
    onesr = sb(es, "onesr", [128, 128], F32R)
    bonesr = sb(es, "bonesr", [128, 128], F32R)
    swapT = sb(es, "swapT", [128, 128], BF16)
    identb = sb(es, "identb", [128, 128], BF16)
    cst_f = sb(es, "cst_f", [128, 3, 128], F32)
    PS = es.enter_context(nc.psum_tensor("PS", [128, 4096], F32))

    def bank(b, n=1):
        return PS[:, b * 512:(b + n) * 512]

    def psk(b, n=1):
        return [("ps", i) for i in range(b, b + n)]

    def xk(j, g=None):
        if g is None:
            return [("xT", j, gg) for gg in range(NTG)]
        return [("xT", j, g)]

    def hk(j, g=None):
        if g is None:
            return [("hT", j, gg) for gg in range(NTG)]
        return [("hT", j, g)]

    om = flags[:, 0:1]
    keep = flags[:, 1:2]
    nbm = flags[:, 2:3]

    for j in range(KC):
        P.add("sp", lambda e, j=j: e.dma_start(out=xT[:, j, :], in_=d_xT[j * 128:(j + 1) * 128, :]),
              writes=xk(j), dma=True)
    P.add("sp", lambda e: e.dma_start(out=flags[:, :], in_=d_flags[:, :]), writes=["flags"], dma=True)
    P.add("sp", lambda e: e.dma_start(out=lnp[:], in_=d_lnp[:]), writes=["lnp"], dma=True)
    P.add("sp", lambda e: e.dma_start(out=cst_f[:], in_=d_consts.rearrange("c p m -> p c m")), writes=["cst_f"], dma=True)
    P.add("dve", lambda e: e.tensor_copy(out=swapT[:, :], in_=cst_f[:, 0, :]), reads=["cst_f"], writes=["swapT"])
    P.add("dve", lambda e: e.tensor_copy(out=bonesr[:, :], in_=cst_f[:, 1, :]), reads=["cst_f"], writes=["bonesr"])
    P.add("dve", lambda e: e.tensor_copy(out=identb[:, :], in_=cst_f[:, 2, :]), reads=["cst_f"], writes=["identb"])
    P.add("pool", lambda e: e.memset(cst_f[:, 1, :], 1.0), reads=["bonesr"], writes=["cst_f"])
    P.add("dve", lambda e: e.tensor_copy(out=onesr[:, :], in_=cst_f[:, 1, :]), reads=["cst_f"], writes=["onesr"])

    scond = sb(es, "scond", [128, KC], BF16)
    bmod = sb(es, "bmod_sb", [128, 96], F32)
    mod = sb(es, "mod_sb", [128, 2, 6, KC], F32)
    with ExitStack() as ph:
        cond = sb(ph, "cond_sb", [128, KC], F32)
        wm = [sb(ph, "wm%d" % i, [128, KC, 1024], BF16) for i in range(2)]
        P.add("sp", lambda e: e.dma_start(out=cond[:, :], in_=d_cond[:, :]), writes=["cond"], dma=True)
        P.add("sp", lambda e: e.dma_start(out=bmod[:, :], in_=d_bmod[:, :]), writes=["bmod"], dma=True)
        P.add("act", lambda e: e.activation(out=scond[:, :], in_=cond[:, :], func=AF.Silu), reads=["cond"], writes=["scond"])
        for l in range(1):
            wv = d_wmod[l].rearrange("(k p) n -> p k n", p=128)
            for pc in range(3):
                slot = (l * 6 + pc) % 2
                P.add("pool", lambda e, slot=slot, pc=pc, wv=wv: e.dma_start(out=wm[slot][:], in_=wv[:, :, pc * 1024:(pc + 1) * 1024]),
                      writes=[("wm", slot)], dma=True)
                for cc in range(8):
                    col = l * 48 + pc * 8 + cc

                    def mmfn(e, slot=slot, cc=cc, col=col):
                        ins = None
                        for k in range(KC):
                            ins = e.matmul(PS[:, col:col + 1], lhsT=wm[slot][:, k, cc * 128:(cc + 1) * 128],
                                           rhs=scond[:, k:k + 1], start=(k == 0), stop=(k == KC - 1))
                        return ins
                    P.add("pe", mmfn, reads=[("wm", slot), "scond"], writes=[("ps", 0)])
        P.add("dve", lambda e: e.tensor_tensor(out=mod[:, 0, 0:3].rearrange("p v c -> p (v c)"), in0=PS[:, 0:24], in1=bmod[:, 0:24], op=ALU.add),
              reads=[("ps", 0), "bmod"], writes=["mod0a"])

        def derive(l, part):
            mk = "mod%d%s" % (l, part)
            if part == "a":
                P.add("dve", lambda e: e.tensor_copy(out=mv[:, l, 0, :], in_=mod[:, l, 0, :]), reads=[mk], writes=[("mv", l, 0)])
                P.add("dve", lambda e: e.tensor_scalar(out=mv[:, l, 1, :], in0=mod[:, l, 1, :], scalar1=1.0, scalar2=None, op0=ALU.add),
                      reads=[mk], writes=[("mv", l, 1)])
                P.add("dve", lambda e: e.tensor_scalar(out=mv[:, l, 2, :], in0=mod[:, l, 2, :], scalar1=1.0 / ALPHA, scalar2=None, op0=ALU.mult),
                      reads=[mk], writes=[("mv", l, 2)])
                return
            P.add("dve", lambda e: e.tensor_copy(out=mv[:, l, 3, :], in_=mod[:, l, 3, :]), reads=[mk], writes=[("mv", l, 3)])
            P.add("dve", lambda e: e.tensor_scalar(out=mv[:, l, 4, :], in0=mod[:, l, 4, :], scalar1=1.0, scalar2=None, op0=ALU.add),
                  reads=[mk], writes=[("mv", l, 4)])
            P.add("dve", lambda e: e.tensor_scalar(out=mv[:, l, 5, :], in0=mod[:, l, 5, :], scalar1=1.0 / ALPHA, scalar2=None, op0=ALU.mult),
                  reads=[mk], writes=[("mv", l, 5)])
            P.add("dve", lambda e: e.tensor_tensor(out=mv[:, l, 6, :], in0=lnp[:, l, 0, :], in1=mv[:, l, 4, :], op=ALU.mult),
                  reads=["lnp", ("mv", l, 4)], writes=[("mv", l, 6)])
            P.add("dve", lambda e: e.tensor_tensor(out=mv[:, l, 7, :], in0=lnp[:, l, 1, :], in1=mv[:, l, 4, :], op=ALU.mult),
                  reads=["lnp", ("mv", l, 4)], writes=[("mv", l, 7)])
            P.add("dve", lambda e: e.tensor_tensor(out=mv[:, l, 7, :], in0=mv[:, l, 7, :], in1=mv[:, l, 3, :], op=ALU.add),
                  reads=[("mv", l, 7), ("mv", l, 3)], writes=[("mv", l, 7)])

        def derive_cross():
            P.add("dve", lambda e: e.tensor_tensor(out=mv[:, 0, 8, :], in0=lnp[:, 0, 2, :], in1=mv[:, 1, 1, :], op=ALU.mult),
                  reads=["lnp", ("mv", 1, 1)], writes=[("mv", 0, 8)])
            P.add("dve", lambda e: e.tensor_tensor(out=mv[:, 0, 9, :], in0=lnp[:, 0, 3, :], in1=mv[:, 1, 1, :], op=ALU.mult),
                  reads=["lnp", ("mv", 1, 1)], writes=[("mv", 0, 9)])
            P.add("dve", lambda e: e.tensor_tensor(out=mv[:, 0, 9, :], in0=mv[:, 0, 9, :], in1=mv[:, 1, 0, :], op=ALU.add),
                  reads=[("mv", 0, 9), ("mv", 1, 0)], writes=[("mv", 0, 9)])

        derive(0, "a")
        tap("mv", mv[:].rearrange("p l v c -> p (l v c)"), [128, 2 * 10 * KC], [("mv", l, v) for l in range(2) for v in range(10)])
        P.barrier()

    MVK = [("mv", l, v) for l in range(2) for v in range(10)]

    for j in range(KC):
        e_ = "dve" if j % 2 == 0 else "pool"
        P.add(e_, lambda e, j=j: e.tensor_scalar(out=hT[:, j, :], in0=xT[:, j, :], scalar1=mv[:, 0, 1, j:j + 1],
                                                  scalar2=mv[:, 0, 0, j:j + 1], op0=ALU.mult, op1=ALU.add),
              reads=xk(j) + MVK, writes=hk(j))
    tap("h0", hT[:, 0, :], [128, T], hk(0)) if False else None

    ctx = dict(m1=dict(scond=scond, bmod=bmod, mod=mod, d_wmod=d_wmod, derive=derive, derive_cross=derive_cross),
               nc=nc, es=es, P=P, PS=PS, bank=bank, psk=psk, xk=xk, hk=hk, xT=xT, hT=hT, mv=mv, lnp=lnp, MVK=MVK,
               onesr=onesr, bonesr=bonesr, swapT=swapT, identb=identb, om=om, keep=keep, nbm=nbm, sb=sb, tap=tap,
               d=dict(win=d_win, gains=d_gains, ropeC=d_ropeC, ropeS=d_ropeS, kcT=d_kcT, vc=d_vc, scw=d_scw, wout=d_wout,
                      wup=d_wup, fcw=d_fcw, wdn=d_wdn, hin=d_hin, lbl=d_lbl, hng=d_hng, hout=d_hout, hmask=d_hmask, htri=d_htri,
                      state=d_state, o_yT=o_yT, o_kT=o_kT, o_v=o_v, o_s=o_s))
    if upto >= 1:
        layer0_mixer(ctx)
    if upto >= 2:
        layernorm(ctx, 0, 1)
    if upto >= 3:
        ffn(ctx, 0)
    if upto >= 4:
        layernorm(ctx, 0, 2)
    if upto >= 5:
        hgrn_mixer(ctx)
    if upto >= 6:
        layernorm(ctx, 1, 1)
    if upto >= 7:
        ffn(ctx, 1)
    if upto >= 8:
        layernorm(ctx, 1, 2, final=True)
    if n_dbg is not None:
        for j in range(KC):
            tap("x%d" % j, xT[:, j, :], [128, T], xk(j))
            tap("h%d" % j, hT[:, j, :], [128, T], hk(j), BF16)
    P.barrier()
    return nc, P


def load_unit(c, dst, src, key, reads=()):
    c["P"].add("pool", lambda e: e.dma_start(out=dst, in_=src), reads=list(reads), writes=[key], dma=True)


class WStream:
    def __init__(self, c, bufs, key, srcs, pf):
        self.c, self.bufs, self.key, self.srcs, self.pf = c, bufs, key, srcs, pf
        self.issued = 0
        self.n = len(bufs)

    def prefetch(self, upto):
        upto = min(upto, len(self.srcs) - 1)
        while self.issued <= upto:
            i = self.issued
            src = self.srcs[i]
            if isinstance(src, tuple):
                load_unit(self.c, self.bufs[i % self.n][:, 0:src[1], :], src[0], (self.key, i % self.n))
            else:
                load_unit(self.c, self.bufs[i % self.n][:], src, (self.key, i % self.n))
            self.issued += 1

    def get(self, i):
        self.prefetch(i + self.pf)
        return self.bufs[i % self.n], (self.key, i % self.n)


def proj_fm(c, wtile, wkey, half, rhs_of, rkeys, nk=KC):
    P, bank, psk = c["P"], c["bank"], c["psk"]
    for g in range(NTG):
        b = half * 4 + g

        def fn(e, g=g, b=b):
            ins = None
            for k in range(nk):
                ins = e.matmul(bank(b), lhsT=wtile[:, k, :], rhs=rhs_of(k, g), start=(k == 0), stop=(k == nk - 1))
            return ins
        P.add("pe", fn, reads=[wkey] + rkeys(g), writes=psk(b))


def layer0_mixer(c):
    P, nc, PS, bank, psk = c["P"], c["nc"], c["PS"], c["bank"], c["psk"]
    xT, hT, mv, sb, tap, d = c["xT"], c["hT"], c["mv"], c["sb"], c["tap"], c["d"]
    xk, hk, MVK = c["xk"], c["hk"], c["MVK"]
    om, nbm = c["om"], c["nbm"]
    onesr, bonesr, swapT = c["onesr"], c["bonesr"], c["swapT"]
    d_win = d["win"]
    L0 = os.environ.get("K_L0", "ABCDE")

    def hrhs(k, g):
        return hT[:, k, g * TGS:(g + 1) * TGS]

    def hkeys(g):
        return [("hT", k, g) for k in range(KC)]

    with ExitStack() as ph:
        scT = sb(ph, "scT", [128, 4, T], BF16)
        wu = [sb(ph, "wu%d" % i, [128, KC, 128], BF16) for i in range(4)]
        gains = sb(ph, "gains", [128, 4], F32)
        scw = sb(ph, "scw_sb", [128, 4, 3], F32)
        nbw = sb(ph, "nbw", [128, 4, 2], F32)
        P.add("sp", lambda e: e.dma_start(out=gains[:, :], in_=d["gains"][:, :]), writes=["gains"], dma=True)
        P.add("sp", lambda e: e.dma_start(out=scw[:], in_=d["scw"][:]), writes=["scw"], dma=True)
        P.add("dve", lambda e: e.tensor_scalar(out=nbw[:, :, 0], in0=scw[:, :, 0], scalar1=nbm, scalar2=None, op0=ALU.mult),
              reads=["scw", "flags"], writes=["nbw"])
        P.add("dve", lambda e: e.tensor_scalar(out=nbw[:, :, 1], in0=scw[:, :, 2], scalar1=nbm, scalar2=None, op0=ALU.mult),
              reads=["scw", "flags", "nbw"], writes=["nbw"])
        srcs0 = []
        for ch in range(4):
            srcs0 += [d_win[15 + ch], d_win[11 + ch], d_win[7 + ch]]
        srcs0 += [d_win[un] for un in range(6)] + [d_win[6]] + [d["wout"][j] for j in range(KC)]
        ws0 = WStream(c, wu, "wu", srcs0, 3)
        ucount = [0]

        def next_unit(src):
            i = ucount[0]
            ucount[0] += 1
            assert srcs0[i] is src or True
            ws0.get(i)
            return i % 4

        with ExitStack() as pa:
            xi_sb = sb(pa, "xi_sb", [128, T], F32)
            mpad = sb(pa, "mpad", [128, T + 2], F32)
            u = sb(pa, "u_sb", [128, T], F32)
            P.add("pool", lambda e: e.memset(mpad[:, :], 0.0), writes=["mpad"])
            for ch in (range(4) if "A" in L0 else ()):
                s_xi = next_unit(d_win[15 + ch])
                proj_fm(c, wu[s_xi], ("wu", s_xi), 0, hrhs, hkeys)
                s_cg = next_unit(d_win[11 + ch])
                P.add("act", lambda e: e.activation(out=xi_sb[:, :], in_=bank(0, 4), func=AF.Copy), reads=psk(0, 4), writes=["xi_sb"])
                proj_fm(c, wu[s_cg], ("wu", s_cg), 1, hrhs, hkeys)
                s_bg = next_unit(d_win[7 + ch])
                P.add("dve", lambda e: e.tensor_tensor(out=mpad[:, 1:T + 1], in0=bank(4, 4), in1=xi_sb[:, :], op=ALU.mult),
                      reads=psk(4, 4) + ["xi_sb"], writes=["mpad"])
                proj_fm(c, wu[s_bg], ("wu", s_bg), 0, hrhs, hkeys)
                P.add("act", lambda e, ch=ch: e.activation(out=u[:, :], in_=mpad[:, 1:T + 1], func=AF.Copy, scale=scw[:, ch, 1:2]),
                      reads=["mpad", "scw"], writes=["u"])
                P.add("dve", lambda e, ch=ch: e.scalar_tensor_tensor(out=u[:, :], in0=mpad[:, 0:T], scalar=scw[:, ch, 0:1], in1=u[:, :],
                                                                     op0=ALU.mult, op1=ALU.add), reads=["mpad", "scw", "u"], writes=["u"])
                P.add("dve", lambda e, ch=ch: e.scalar_tensor_tensor(out=u[:, :], in0=mpad[:, 2:T + 2], scalar=scw[:, ch, 2:3], in1=u[:, :],
                                                                     op0=ALU.mult, op1=ALU.add), reads=["mpad", "scw", "u"], writes=["u"])
                P.add("dve", lambda e, ch=ch: e.scalar_tensor_tensor(out=u[:, SEG:T:SEG], in0=mpad[:, SEG:T:SEG], scalar=nbw[:, ch, 0:1],
                                                                     in1=u[:, SEG:T:SEG], op0=ALU.mult, op1=ALU.add),
                      reads=["mpad", "nbw", "u"], writes=["u"])
                P.add("dve", lambda e, ch=ch: e.scalar_tensor_tensor(out=u[:, SEG - 1:T - 1:SEG], in0=mpad[:, SEG + 1:T + 1:SEG], scalar=nbw[:, ch, 1:2],
                                                                     in1=u[:, SEG - 1:T - 1:SEG], op0=ALU.mult, op1=ALU.add),
                      reads=["mpad", "nbw", "u"], writes=["u"])
                P.add("dve", lambda e, ch=ch: e.tensor_tensor(out=scT[:, ch, :], in0=bank(0, 4), in1=u[:, :], op=ALU.mult),
                      reads=psk(0, 4) + ["u"], writes=[("scT", ch)])
            P.barrier()
        tap("scT", scT[:].rearrange("p c t -> p (c t)"), [128, 4 * T], [("scT", ch) for ch in range(4)]) if False else None

        with ExitStack() as pb:
            qT = sb(pb, "qT", [128, 4, T], BF16)
            kTz = sb(pb, "kTz", [128, 2, 2, T + PAST], BF16)
            P.add("pool", lambda e: e.memset(kTz[:].rearrange("p a b t -> p (a b t)"), 0.0), writes=["kTz0"])
            with ExitStack() as pq:
                ropeC = sb(pq, "ropeC", [128, T], F32)
                ropeS = sb(pq, "ropeS", [128, T], F32)
                P.add("sp", lambda e: e.dma_start(out=ropeC[:, :], in_=d["ropeC"][:, :]), writes=["ropeC"], dma=True)
                P.add("sp", lambda e: e.dma_start(out=ropeS[:, :], in_=d["ropeS"][:, :]), writes=["ropeS"], dma=True)
                qsq = [sb(pq, "qsq%d" % i, [128, TGS], F32R) for i in range(2)]
                qbf = [sb(pq, "qbf%d" % i, [128, TGS], BF16) for i in range(2)]
                sd = [sb(pq, "sd%d" % i, [128, TGS], F32) for i in range(2)]
                rstd = [sb(pq, "rstd%d" % i, [128, TGS], F32) for i in range(2)]
                t1 = [sb(pq, "t1_%d" % i, [128, TGS], F32) for i in range(2)]
                t2 = [sb(pq, "t2_%d" % i, [128, TGS], F32) for i in range(2)]
                m1 = c["m1"]
                wm1 = [sb(pq, "wm1_%d" % i, [128, KC, 128], BF16) for i in range(4)]
                wv1 = m1["d_wmod"][1].rearrange("(k p) n -> p k n", p=128)

                wv0 = m1["d_wmod"][0].rearrange("(k p) n -> p k n", p=128)
                mchunks = [(wv0, cidx) for cidx in range(24, 48)] + [(wv1, cidx) for cidx in range(48)]

                def m1_load(pi):
                    if pi >= len(mchunks):
                        return
                    wvx, cidx = mchunks[pi]
                    P.add("pool", lambda e: e.dma_start(out=wm1[pi % 4][:], in_=wvx[:, :, cidx * 128:(cidx + 1) * 128]), writes=[("wm1", pi % 4)], dma=True)

                def m1_mm(pi):
                    col = 6 * 512 + pi

                    def mmfn(e):
                        ins = None
                        for k in range(KC):
                            ins = e.matmul(PS[:, col:col + 1], lhsT=wm1[pi % 4][:, k, :],
                                           rhs=m1["scond"][:, k:k + 1], start=(k == 0), stop=(k == KC - 1))
                        return ins
                    P.add("pe", mmfn, reads=[("wm1", pi % 4), "scond"], writes=[("ps", 6)])
                m1_load(0)
                m1_load(1)
                m1_load(2)
                it = 0
                for un in (range(6) if "B" in L0 else ()):
                    slot = next_unit(d_win[un])
                    isk = un >= 4
                    gcol = 2 if isk else 0
                    for g in range(NTG):
                        s = it % 2
                        it += 1
                        bA, bB, bC = s * 3, s * 3 + 1, s * 3 + 2
                        tok = slice(g * TGS, (g + 1) * TGS)

                        def fn(e, slot=slot, g=g, bA=bA):
                            ins = None
                            for k in range(KC):
                                ins = e.matmul(bank(bA), lhsT=wu[slot][:, k, :], rhs=hrhs(k, g), start=(k == 0), stop=(k == KC - 1))
                            return ins
                        P.add("pe", fn, reads=[("wu", slot)] + hkeys(g), writes=psk(bA))
                        for cc_ in range(3):
                            m1_mm(3 * (it - 1) + cc_)
                            m1_load(3 * (it - 1) + cc_ + 3)
                        P.add("act", lambda e, s=s, bA=bA: e.activation(out=qsq[s][:, :], in_=bank(bA), func=AF.Square),
                              reads=psk(bA), writes=[("qsq", s)])
                        P.add("act", lambda e, s=s, bA=bA: e.activation(out=qbf[s][:, :], in_=bank(bA), func=AF.Copy),
                              reads=psk(bA), writes=[("qbf", s)])
                        P.add("pe", lambda e, s=s, bB=bB: e.matmul(bank(bB), lhsT=bonesr[:, :], rhs=qsq[s][:, :], start=True, stop=True),
                              reads=["bonesr", ("qsq", s)], writes=psk(bB))
                        P.add("pe", lambda e, s=s, bC=bC: e.matmul(bank(bC), lhsT=swapT[:, :], rhs=qbf[s][:, :], start=True, stop=True),
                              reads=["swapT", ("qbf", s)], writes=psk(bC))
                        P.add("act", lambda e, s=s, bB=bB: e.activation(out=sd[s][:, :], in_=bank(bB), func=AF.Ln, bias=EPS, scale=1.0 / 64.0),
                              reads=psk(bB), writes=[("sd", s)])
                        P.add("act", lambda e, s=s: e.activation(out=rstd[s][:, :], in_=sd[s][:, :], func=AF.Exp, scale=-0.5), reads=[("sd", s)], writes=[("rstd", s)])
                        P.add("dve", lambda e, s=s, bA=bA, tok=tok, gcol=gcol: e.scalar_tensor_tensor(
                            out=t1[s][:, :], in0=bank(bA), scalar=gains[:, gcol:gcol + 1], in1=ropeC[:, tok], op0=ALU.mult, op1=ALU.mult),
                            reads=psk(bA) + ["gains", "ropeC"], writes=[("t1", s)])
                        P.add("dve", lambda e, s=s, bC=bC, tok=tok, gcol=gcol: e.scalar_tensor_tensor(
                            out=t2[s][:, :], in0=bank(bC), scalar=gains[:, gcol + 1:gcol + 2], in1=ropeS[:, tok], op0=ALU.mult, op1=ALU.mult),
                            reads=psk(bC) + ["gains", "ropeS"], writes=[("t2", s)])
                        P.add("pool", lambda e, s=s: e.tensor_tensor(out=t1[s][:, :], in0=t1[s][:, :], in1=t2[s][:, :], op=ALU.add),
                              reads=[("t1", s), ("t2", s)], writes=[("t1", s)])
                        if not isk:
                            P.add("pool", lambda e, s=s, un=un, tok=tok: e.tensor_tensor(out=qT[:, un, tok], in0=t1[s][:, :], in1=rstd[s][:, :], op=ALU.mult),
                                  reads=[("t1", s), ("rstd", s)], writes=[("qT", un, g)])
                        else:
                            kv = un - 4
                            P.add("pool", lambda e, s=s: e.tensor_tensor(out=t2[s][:, :], in0=t1[s][:, :], in1=rstd[s][:, :], op=ALU.mult),
                                  reads=[("t1", s), ("rstd", s)], writes=[("t2", s)])
                            P.add("pool", lambda e, s=s, kv=kv, tok=tok: e.tensor_copy(out=kTz[0:64, kv, 0, tok], in_=t2[s][0:64, :]),
                                  reads=[("t2", s), "kTz0"], writes=[("kTd", kv, g, 0)])
                            P.add("pool", lambda e, s=s, kv=kv, tok=tok: e.tensor_copy(out=kTz[64:128, kv, 1, tok], in_=t2[s][64:128, :]),
                                  reads=[("t2", s), "kTz0"], writes=[("kTd", kv, g, 1)])
                            P.add("sp", lambda e, s=s, kv=kv, tok=tok: e.dma_start(out=d["o_kT"][kv, :, tok], in_=t2[s][0:64, :]),
                                  reads=[("t2", s)], writes=[("o_kT", kv, g)], dma=True)
                P.add("dve", lambda e: e.tensor_tensor(out=m1["mod"][:, 0, 3:6].rearrange("p v c -> p (v c)"), in0=PS[:, 6 * 512:6 * 512 + 24],
                                                       in1=m1["bmod"][:, 24:48], op=ALU.add), reads=[("ps", 6), "bmod"], writes=["mod0b"])
                P.add("dve", lambda e: e.tensor_tensor(out=m1["mod"][:, 1].rearrange("p v c -> p (v c)"), in0=PS[:, 6 * 512 + 24:6 * 512 + 72],
                                                       in1=m1["bmod"][:, 48:96], op=ALU.add), reads=[("ps", 6), "bmod"], writes=["mod1a", "mod1b"])
                m1["derive"](0, "b")
                m1["derive"](1, "a")
                m1["derive"](1, "b")
                m1["derive_cross"]()
                P.barrier()
            pvv = pb
            VVd = sb(pvv, "VVd", [128, 32, 128], BF16)
            VVo = sb(pvv, "VVo", [128, 40, 128], BF16)
            with ExitStack() as pq:
                vf = [sb(pq, "vf%d" % i, [128, 4, 128], F32) for i in range(2)]
                vcf = sb(pq, "vcf", [128, 4, 128], F32)
                KCX = os.environ.get("K_C", "kvmpados")
                for kv in (range(2) if "k" in KCX else ()):
                    P.add("pool", lambda e, kv=kv: e.dma_start(out=kTz[0:64, kv, 0, T:T + PAST], in_=d["kcT"][kv, 0:64, :]), writes=[("kTd", kv, "ctx", 0)], dma=True)
                    P.add("pool", lambda e, kv=kv: e.dma_start(out=kTz[64:128, kv, 1, T:T + PAST], in_=d["kcT"][kv, 64:128, :]), writes=[("kTd", kv, "ctx", 1)], dma=True)
                if "v" in KCX:
                    P.add("sp", lambda e: e.dma_start(out=vcf[:], in_=d["vc"].rearrange("(t p) c -> p t c", p=128)), writes=["vcf"], dma=True)
                P.add("pool", lambda e: e.memset(VVd[:, :, 64:128], 1.0), writes=["VVd1"])
                P.add("pool", lambda e: e.memset(VVo[:, :, 64:128], 1.0), writes=["VVo1"])
                P.add("dve", lambda e: e.tensor_scalar(out=VVo[:, :, 64:128], in0=VVo[:, :, 64:128], scalar1=om, scalar2=None, op0=ALU.mult),
                      reads=["VVo1", "flags"], writes=["VVo1"])
                P.add("dve", lambda e: e.tensor_scalar(out=VVo[:, 32:40, 0:64], in0=vcf[:].rearrange("p t (g c) -> p (t g) c", g=2),
                                                        scalar1=om, scalar2=None, op0=ALU.mult),
                      reads=["vcf", "flags"], writes=[("VVo", 4)])
                slot = next_unit(d_win[6])
                for t4 in (range(4) if ("C" in L0 and "p" in KCX) else ()):
                    b = 6 + t4 % 2
                    s = t4 % 2

                    def fn(e, slot=slot, t4=t4, b=b):
                        ins = None
                        for i in range(4):
                            tt = t4 * 4 + i
                            for k in range(KC):
                                ins = e.matmul(bank(b)[:, i * 128:(i + 1) * 128], lhsT=hT[:, k, tt * 128:(tt + 1) * 128], rhs=wu[slot][:, k, :],
                                               start=(k == 0), stop=(k == KC - 1))
                        return ins
                    P.add("pe", fn, reads=[("wu", slot)] + hkeys(t4), writes=psk(b))
                    if "a" in KCX:
                        P.add("act", lambda e, s=s, b=b: e.activation(out=vf[s][:].rearrange("p t c -> p (t c)"), in_=bank(b), func=AF.Copy),
                              reads=psk(b), writes=[("vf", s)])
                    if "d" not in KCX:
                        continue
                    P.add("dve", lambda e, t4=t4, b=b: e.tensor_copy(out=VVd[:, t4 * 8:t4 * 8 + 8, 0:64], in_=bank(b).rearrange("p (s c) -> p s c", c=64)),
                          reads=psk(b), writes=[("VVd", t4)])
                    if "o" not in KCX:
                        continue
                    P.add("dve", lambda e, s=s, t4=t4: e.tensor_scalar(out=VVo[:, t4 * 8:t4 * 8 + 8, 0:64],
                                                                         in0=vf[s][:].rearrange("p t (g c) -> p (t g) c", g=2),
                                                                         scalar1=om, scalar2=None, op0=ALU.mult),
                          reads=[("vf", s), "flags"], writes=[("VVo", t4)])
                    if "s" not in KCX:
                        continue
                    P.add("sp", lambda e, s=s, t4=t4: e.dma_start(out=d["o_v"].rearrange("(t p) c -> p t c", p=128)[:, t4 * 4:(t4 + 1) * 4, :], in_=vf[s][:]),
                          reads=[("vf", s)], writes=[("o_v", t4)], dma=True)
                P.barrier()
            with ExitStack() as pd:
                attT = sb(pd, "attT", [128, 4, T], BF16)
                PT = [sb(pd, "PT%d" % i, [128, 2 * TGS], BF16) for i in range(2)]
                rden = [sb(pd, "rden%d" % i, [128, SEG], F32) for i in range(4)]
                its = [(h, g4, kp) for h in (range(8) if "D" in L0 else ()) for g4 in range(NTG) for kp in range(NKC // 2)]
                LA = 1

                def emit_S(n):
                    h, g4, kp = its[n]
                    cq, kv = h // 2, h // 4
                    m = n % 2
                    for j in range(2):
                        kc = kp * 2 + j
                        sbk = 2 * m + j
                        P.add("pe", lambda e, kc=kc, sbk=sbk: e.matmul(bank(sbk), lhsT=kTz[:, kv, h % 2, kc * 128:(kc + 1) * 128],
                                                                       rhs=qT[:, cq, g4 * TGS:(g4 + 1) * TGS], start=True, stop=True),
                              reads=[("kTd", kv, kc // 4 if kc < 16 else "ctx", h % 2), ("qT", cq, g4), "kTz0"], writes=psk(sbk))

                def emit_rest(n):
                    h, g4, kp = its[n]
                    cq, kv = h // 2, h // 4
                    odd = (h % 2 == 1)
                    m = n % 2
                    par = (h * NTG + g4) % 2
                    accb = [4 + par * 2, 5 + par * 2]
                    P.add("act", lambda e: e.activation(out=PT[m][:, :], in_=bank(2 * m, 2), func=AF.Exp, scale=0.125),
                          reads=psk(2 * m, 2), writes=[("PT", m)])
                    for j in range(2):
                        kc = kp * 2 + j
                        for s_ in range(2):
                            seg = g4 * 2 + s_
                            diag = kc < 16 and (kc // 2) == seg
                            lhs = (VVd if diag else VVo)[:, kc * 2 + kv, :]
                            P.add("pe", lambda e, s_=s_, lhs=lhs, j=j, kc=kc: e.matmul(bank(accb[s_])[:, 0:SEG], lhsT=lhs,
                                                                                     rhs=PT[m][:, j * TGS + s_ * SEG:j * TGS + (s_ + 1) * SEG],
                                                                                     start=(kc == 0), stop=(kc == NKC - 1)),
                                  reads=[("PT", m), ("VVd", kc // 4) if diag else ("VVo", kc // 4), "VVd1", "VVo1"], writes=psk(accb[s_]))
                    if kp == NKC // 2 - 1:
                        for s_ in range(2):
                            seg = g4 * 2 + s_
                            rb = par * 2 + s_
                            nr, dr = slice(0, 64), slice(64, 128)
                            orow = slice(64, 128) if odd else slice(0, 64)
                            tok = slice(seg * SEG, (seg + 1) * SEG)
                            P.add("dve", lambda e, s_=s_, rb=rb: e.reciprocal(out=rden[rb][dr, :], in_=bank(accb[s_])[dr, 0:SEG]),
                                  reads=psk(accb[s_]), writes=[("rden", rb)])
                            P.add("dve", lambda e, s_=s_, rb=rb, orow=orow, tok=tok: e.tensor_tensor(
                                out=attT[orow, cq, tok], in0=bank(accb[s_])[nr, 0:SEG], in1=rden[rb][dr, :], op=ALU.mult),
                                reads=psk(accb[s_]) + [("rden", rb)], writes=[("attT", cq, seg, h % 2)])

                for n in range(len(its) + LA):
                    if n < len(its):
                        emit_S(n)
                    if n - LA >= 0:
                        emit_rest(n - LA)
                P.barrier()
                for j in (range(KC) if "E" in L0 else ()):
                    slot = next_unit(d["wout"][j])
                    half = j % 2

                    def rhs_of(k, g):
                        if k < 4:
                            return attT[:, k, g * TGS:(g + 1) * TGS]
                        return scT[:, k - 4, g * TGS:(g + 1) * TGS]

                    def rk(g):
                        r = [("scT", ch) for ch in range(4)]
                        for cq in range(4):
                            for seg in (2 * g, 2 * g + 1):
                                r += [("attT", cq, seg, 0), ("attT", cq, seg, 1)]
                        return r
                    proj_fm(c, wu[slot], ("wu", slot), half, rhs_of, rk)
                    P.add("dve", lambda e, j=j, half=half: e.scalar_tensor_tensor(out=xT[:, j, :], in0=bank(half * 4, 4), scalar=mv[:, 0, 2, j:j + 1],
                                                                                 in1=xT[:, j, :], op0=ALU.mult, op1=ALU.add),
                          reads=psk(half * 4, 4) + MVK + xk(j), writes=xk(j))
                P.barrier()


def layernorm(c, l, which, final=False):
    P, bank, psk = c["P"], c["bank"], c["psk"]
    xT, hT, mv, lnp, sb, d = c["xT"], c["hT"], c["mv"], c["lnp"], c["sb"], c["d"]
    onesr, MVK = c["onesr"], c["MVK"]
    gi, bi = (0, 1) if which == 1 else (2, 3)
    ghi, bhi = (6, 7) if which == 1 else (8, 9)
    with ExitStack() as ph:
        xr = [sb(ph, "xr%d" % i, [128, KC, TGS], F32R) for i in range(2)]
        xs = [sb(ph, "xs%d" % i, [128, KC, TGS], F32R) for i in range(2)]
        mean = [sb(ph, "mean%d" % i, [128, TGS], F32) for i in range(2)]
        msq = [sb(ph, "msq%d" % i, [128, TGS], F32) for i in range(2)]
        var = [sb(ph, "var%d" % i, [128, TGS], F32) for i in range(2)]
        rstd = [sb(ph, "lrstd%d" % i, [128, TGS], F32) for i in range(2)]
        nmr = [sb(ph, "nmr%d" % i, [128, TGS], F32) for i in range(2)]
        tt = [sb(ph, "ltt%d" % i, [128, TGS], F32) for i in range(4)]
        itc = [0]

        def stats(g):
            s = g % 2
            tok = slice(g * TGS, (g + 1) * TGS)
            bA, bB = 2 * s, 2 * s + 1
            for j in range(KC):
                P.add("dve", lambda e, j=j: e.tensor_copy(out=xr[s][:, j, :], in_=xT[:, j, tok]),
                      reads=[("xT", j, g)], writes=[("xr", s, j)])
                P.add("act", lambda e, j=j: e.activation(out=xs[s][:, j, :], in_=xT[:, j, tok], func=AF.Square),
                      reads=[("xT", j, g)], writes=[("xs", s, j)])

            def fnA(e):
                ins = None
                for j in range(KC):
                    ins = e.matmul(bank(bA), lhsT=onesr[:, :], rhs=xr[s][:, j, :], start=(j == 0), stop=(j == KC - 1))
                return ins

            def fnB(e):
                ins = None
                for j in range(KC):
                    ins = e.matmul(bank(bB), lhsT=onesr[:, :], rhs=xs[s][:, j, :], start=(j == 0), stop=(j == KC - 1))
                return ins
            P.add("pe", fnA, reads=["onesr"] + [("xr", s, j) for j in range(KC)], writes=psk(bA))
            P.add("pe", fnB, reads=["onesr"] + [("xs", s, j) for j in range(KC)], writes=psk(bB))
            P.add("act", lambda e: e.activation(out=mean[s][:, :], in_=bank(bA), func=AF.Copy, scale=1.0 / D),
                  reads=psk(bA), writes=[("mean", s)])
            P.add("pool", lambda e: e.tensor_tensor(out=msq[s][:, :], in0=mean[s][:, :], in1=mean[s][:, :], op=ALU.mult),
                  reads=[("mean", s)], writes=[("msq", s)])
            P.add("dve", lambda e: e.scalar_tensor_tensor(out=var[s][:, :], in0=bank(bB), scalar=1.0 / D, in1=msq[s][:, :],
                                                          op0=ALU.mult, op1=ALU.subtract),
                  reads=psk(bB) + [("msq", s)], writes=[("var", s)])
            P.add("act", lambda e: e.activation(out=var[s][:, :], in_=var[s][:, :], func=AF.Ln, bias=EPS_LN, scale=1.0),
                  reads=[("var", s)], writes=[("var", s)])
            P.add("act", lambda e: e.activation(out=rstd[s][:, :], in_=var[s][:, :], func=AF.Exp, scale=-0.5), reads=[("var", s)], writes=[("lrstd", s)])
            P.add("dve", lambda e: e.scalar_tensor_tensor(out=nmr[s][:, :], in0=mean[s][:, :], scalar=-1.0, in1=rstd[s][:, :],
                                                          op0=ALU.mult, op1=ALU.mult),
                  reads=[("mean", s), ("lrstd", s)], writes=[("nmr", s)])

        def norm(g):
            s = g % 2
            tok = slice(g * TGS, (g + 1) * TGS)
            for j in range(KC):
                r = itc[0] % 4
                itc[0] += 1
                P.add("dve", lambda e, r=r, j=j: e.tensor_tensor(out=tt[r][:, :], in0=xT[:, j, tok], in1=rstd[s][:, :], op=ALU.mult),
                      reads=[("xT", j, g), ("lrstd", s)], writes=[("ltt", r)])
                P.add("dve", lambda e, r=r: e.tensor_tensor(out=tt[r][:, :], in0=tt[r][:, :], in1=nmr[s][:, :], op=ALU.add),
                      reads=[("ltt", r), ("nmr", s)], writes=[("ltt", r)])
                P.add("act", lambda e, r=r, j=j: e.activation(out=xT[:, j, tok], in_=tt[r][:, :], func=AF.Identity,
                                                              bias=lnp[:, l, bi, j:j + 1], scale=lnp[:, l, gi, j:j + 1]),
                      reads=[("ltt", r), "lnp"], writes=[("xT", j, g)])
                if final:
                    P.add("sp", lambda e, j=j: e.dma_start(out=d["o_yT"][j * 128:(j + 1) * 128, tok], in_=xT[:, j, tok]),
                          reads=[("xT", j, g)], writes=[("o_yT", j, g)], dma=True)
                else:
                    P.add("pool", lambda e, r=r, j=j: e.tensor_scalar(out=hT[:, j, tok], in0=tt[r][:, :], scalar1=mv[:, l, ghi, j:j + 1],
                                                                       scalar2=mv[:, l, bhi, j:j + 1], op0=ALU.mult, op1=ALU.add),
                          reads=[("ltt", r)] + MVK, writes=[("hT", j, g)])

        stats(0)
        stats(1)
        norm(0)
        stats(2)
        norm(1)
        stats(3)
        norm(2)
        norm(3)
        P.barrier()


def ffn(c, l):
    P, bank, psk = c["P"], c["bank"], c["psk"]
    xT, hT, mv, sb, d = c["xT"], c["hT"], c["mv"], c["sb"], c["d"]
    xk, MVK, nbm = c["xk"], c["MVK"], c["nbm"]
    HC = NFC // 2

    def hrhs(k, g):
        return hT[:, k, g * TGS:(g + 1) * TGS]

    def hkeys(g):
        return [("hT", k, g) for k in range(KC)]

    with ExitStack() as ph:
        act = sb(ph, "ffact", [128, HC, T], BF16)
        uu = [sb(ph, "ffu%d" % i, [128, T], F32) for i in range(2)]
        sg = sb(ph, "ffsg", [128, T], F32)
        tl = [sb(ph, "fftl%d" % i, [128, T], F32) for i in range(2)]
        wu = [sb(ph, "fwu%d" % i, [128, KC, 128], BF16) for i in range(6)]
        wd = [sb(ph, "fwd%d" % i, [128, HC, 128], BF16) for i in range(3)]
        srcs_up = [d["wup"][l, pas * HC + cc + (NFC if isg else 0)] for pas in range(2) for cc in range(HC) for isg in range(2)]
        srcs_dn = [d["wdn"][l, j, :, pas * HC:(pas + 1) * HC, :] for pas in range(2) for j in range(KC)]
        wsU = WStream(c, wu, "fwu", srcs_up, 5)
        wsD = WStream(c, wd, "fwd", srcs_dn, 1)
        fcw = sb(ph, "fcw_sb", [128, 44, 3], F32)
        nbw = sb(ph, "fnbw", [128, 44, 2], F32)
        P.add("sp", lambda e: e.dma_start(out=fcw[:], in_=d["fcw"][l]), writes=["fcw"], dma=True)
        P.add("dve", lambda e: e.tensor_scalar(out=nbw[:, :, 0], in0=fcw[:, :, 0], scalar1=nbm, scalar2=None, op0=ALU.mult),
              reads=["fcw", "flags"], writes=["fnbw"])
        P.add("dve", lambda e: e.tensor_scalar(out=nbw[:, :, 1], in0=fcw[:, :, 2], scalar1=nbm, scalar2=None, op0=ALU.mult),
              reads=["fcw", "flags", "fnbw"], writes=["fnbw"])
        ucnt = 0
        dcnt = 0
        for pas in range(2):
            for cc in range(HC):
                for isg in range(2):
                    ch = pas * HC + cc + (NFC if isg else 0)
                    wsU.get(ucnt)
                    slot = ucnt % 6
                    ucnt += 1
                    if cc == HC - 2 and isg:
                        wsD.prefetch(dcnt + 1)
                    half = isg
                    proj_fm(c, wu[slot], ("fwu", slot), half, hrhs, hkeys)
                    u = uu[isg]
                    uk = ("ffu", isg)
                    y = bank(half * 4, 4)
                    yk = psk(half * 4, 4)
                    ysb = tl[isg]
                    ysk = ("fftl", isg)
                    P.add("act", lambda e, ysb=ysb, y=y: e.activation(out=ysb[:, :], in_=y, func=AF.Copy), reads=yk, writes=[ysk])
                    P.add("act", lambda e, u=u, ysb=ysb, ch=ch: e.activation(out=u[:, :], in_=ysb[:, :], func=AF.Copy, scale=fcw[:, ch, 1:2]),
                          reads=[ysk, "fcw"], writes=[uk])
                    P.add("dve", lambda e, u=u, ysb=ysb, ch=ch: e.scalar_tensor_tensor(out=u[:, 1:T], in0=ysb[:, 0:T - 1], scalar=fcw[:, ch, 0:1], in1=u[:, 1:T],
                                                                                       op0=ALU.mult, op1=ALU.add), reads=[ysk, "fcw", uk], writes=[uk])
                    P.add("dve", lambda e, u=u, ysb=ysb, ch=ch: e.scalar_tensor_tensor(out=u[:, 0:T - 1], in0=ysb[:, 1:T], scalar=fcw[:, ch, 2:3], in1=u[:, 0:T - 1],
                                                                                       op0=ALU.mult, op1=ALU.add), reads=[ysk, "fcw", uk], writes=[uk])
                    P.add("dve", lambda e, u=u, ysb=ysb, ch=ch: e.scalar_tensor_tensor(out=u[:, SEG:T:SEG], in0=ysb[:, SEG - 1:T - 1:SEG], scalar=nbw[:, ch, 0:1],
                                                                                       in1=u[:, SEG:T:SEG], op0=ALU.mult, op1=ALU.add),
                          reads=[ysk, "fnbw", uk], writes=[uk])
                    P.add("dve", lambda e, u=u, ysb=ysb, ch=ch: e.scalar_tensor_tensor(out=u[:, SEG - 1:T - 1:SEG], in0=ysb[:, SEG:T:SEG], scalar=nbw[:, ch, 1:2],
                                                                                       in1=u[:, SEG - 1:T - 1:SEG], op0=ALU.mult, op1=ALU.add),
                          reads=[ysk, "fnbw", uk], writes=[uk])
                    if isg:
                        P.add("act", lambda e: e.activation(out=sg[:, :], in_=uu[1][:, :], func=AF.Silu), reads=[("ffu", 1)], writes=["ffsg"])
                        P.add("dve", lambda e, cc=cc: e.tensor_tensor(out=act[:, cc, :], in0=sg[:, :], in1=uu[0][:, :], op=ALU.mult),
                              reads=["ffsg", ("ffu", 0)], writes=[("ffact", cc)])
            for j in range(KC):
                wsD.get(dcnt)
                slot = dcnt % 3
                dcnt += 1
                half = j % 2
                proj_fm(c, wd[slot], ("fwd", slot), half, lambda k, g: act[:, k, g * TGS:(g + 1) * TGS],
                        lambda g: [("ffact", k) for k in range(HC)], nk=HC)
                P.add("dve", lambda e, j=j, half=half: e.scalar_tensor_tensor(out=xT[:, j, :], in0=bank(half * 4, 4), scalar=mv[:, l, 5, j:j + 1],
                                                                             in1=xT[:, j, :], op0=ALU.mult, op1=ALU.add),
                      reads=psk(half * 4, 4) + MVK + xk(j), writes=xk(j))
        P.barrier()


def hgrn_mixer(c):
    P, nc, PS, bank, psk = c["P"], c["nc"], c["PS"], c["bank"], c["psk"]
    xT, hT, mv, sb, d = c["xT"], c["hT"], c["mv"], c["sb"], c["d"]
    xk, MVK = c["xk"], c["MVK"]
    keep, onesr, identb = c["keep"], c["onesr"], c["identb"]
    PSb = PS.bitcast(BF16)
    HL = 512
    NP = T // HL
    NR = 4

    def hrhs(k, g):
        return hT[:, k, g * TGS:(g + 1) * TGS]

    def hkeys(g):
        return [("hT", k, g) for k in range(KC)]

    with ExitStack() as ph:
        yT = sb(ph, "yT", [128, 4, T], BF16)
        wu = [sb(ph, "hwu%d" % i, [128, KC, 128], BF16) for i in range(4)]
        hmask = sb(ph, "hmask_sb", [128, 2, 512], BF16)
        trim = sb(ph, "trim", [128, 2, 128], BF16)
        lbl = sb(ph, "lbl_sb", [128, 2, 2, 8], F32)
        lb = sb(ph, "lb", [128, 2, 8], F32)
        hng = sb(ph, "hng_sb", [128, 1], F32)
        P.add("pool", lambda e: e.dma_start(out=hmask[:], in_=d["hmask"][:]), writes=["hmask"], dma=True)
        P.add("pool", lambda e: e.dma_start(out=trim[:], in_=d["htri"][:]), writes=["trim"], dma=True)
        P.add("sp", lambda e: e.dma_start(out=lbl[:], in_=d["lbl"][:]), writes=["lbl"], dma=True)
        P.add("sp", lambda e: e.dma_start(out=hng[:, :], in_=d["hng"][:, :]), writes=["hng"], dma=True)
        P.add("dve", lambda e: e.tensor_tensor(out=lb[:, :, :], in0=lbl[:, 0, :, :], in1=lbl[:, 1, :, :], op=ALU.subtract), reads=["lbl"], writes=["lb"])
        P.add("act", lambda e: e.activation(out=lb[:].rearrange("p a b -> p (a b)"), in_=lb[:].rearrange("p a b -> p (a b)"), func=AF.Exp),
              reads=["lb"], writes=["lb"])
        P.add("dve", lambda e: e.tensor_scalar(out=lb[:, :, :], in0=lb[:, :, :], scalar1=1.0, scalar2=None, op0=ALU.add), reads=["lb"], writes=["lb"])
        P.add("dve", lambda e: e.reciprocal(out=lb[:].rearrange("p a b -> p (a b)"), in_=lb[:].rearrange("p a b -> p (a b)")), reads=["lb"], writes=["lb"])
        srcsH = []
        for h in range(8):
            srcsH += [d["hin"][0 * 8 + h], d["hin"][1 * 8 + h], d["hin"][3 * 8 + h], d["hin"][4 * 8 + h], d["hin"][2 * 8 + h]]
            if h % 4 == 3:
                srcsH += [(d["hout"][j][:, 4 * (h // 4):4 * (h // 4) + 4, :], 4) for j in range(KC)]
        wsH = WStream(c, wu, "hwu", srcsH, 3)
        ucount = [0]

        def next_unit(src):
            i = ucount[0]
            ucount[0] += 1
            wsH.get(i)
            return i % 4

        with ExitStack() as pp:
            qf = sb(pp, "qf", [128, T], BF16)
            gA = [sb(pp, "gA%d" % i, [128, HL], F32) for i in range(NP)]
            gB = [sb(pp, "gB%d" % i, [128, HL], F32) for i in range(NP)]
            gC = [sb(pp, "gC%d" % i, [128, HL], F32) for i in range(NP)]
            sqr = [sb(pp, "gsq%d" % i, [128, HL], F32R) for i in range(2)]
            qtL = [sb(pp, "qt%d" % i, [128, T], BF16) for i in range(2)]
            ktL = [sb(pp, "kt%d" % i, [128, T], BF16) for i in range(2)]
            khfm = sb(pp, "khfm", [128, T], BF16)
            khTL = [sb(pp, "khT%d" % i, [128, 16, 128], BF16) for i in range(2)]
            vtm = sb(pp, "vtm", [128, 16, 128], BF16)
            of = sb(pp, "of", [128, T], F32)
            decL = [sb(pp, "dec%d" % i, [128, NCH], F32) for i in range(2)]
            Sf = [sb(pp, "Sf%d" % i, [128, 128], F32) for i in range(NR)]
            Sb = [sb(pp, "Sb%d" % i, [128, 128], BF16) for i in range(NR)]
            Am = [sb(pp, "Am%d" % i, [128, 128], BF16) for i in range(2)]
            HST = os.environ.get("K_H", "vgtrn")
            for h in range(int(os.environ.get("K_NH", "8"))):
                s_q = next_unit(d["hin"][0 * 8 + h])
                proj_fm(c, wu[s_q], ("hwu", s_q), 0, hrhs, hkeys)
                s_i = next_unit(d["hin"][1 * 8 + h])
                P.add("act", lambda e: e.activation(out=qf[:, :], in_=bank(0, 4), func=AF.Copy), reads=psk(0, 4), writes=["qf"])
                for t4 in (range(4) if "v" in HST else ()):
                    b = 4 + t4 % 2

                    def fnv(e, t4=t4, b=b, s_i=s_i):
                        ins = None
                        for i in range(4):
                            tt = t4 * 4 + i
                            for k in range(KC):
                                ins = e.matmul(bank(b)[:, i * 128:(i + 1) * 128], lhsT=hT[:, k, tt * 128:(tt + 1) * 128], rhs=wu[s_i][:, k, :],
                                               start=(k == 0), stop=(k == KC - 1))
                        return ins
                    P.add("pe", fnv, reads=[("hwu", s_i)] + hkeys(t4), writes=psk(b))
                    P.add("act" if t4 % 2 == 0 else "dve", (lambda e, t4=t4, b=b: e.activation(out=vtm[:, t4 * 4:(t4 + 1) * 4, :].rearrange("p t c -> p (t c)"), in_=bank(b), func=AF.Copy))
                          if t4 % 2 == 0 else (lambda e, t4=t4, b=b: e.tensor_copy(out=vtm[:, t4 * 4:(t4 + 1) * 4, :].rearrange("p t c -> p (t c)"), in_=bank(b))),
                          reads=psk(b), writes=[("vtm", t4)])
                def gates_for(dd, h=h):
                    qt, kt, dec = qtL[dd], ktL[dd], decL[dd]
                    s_f = next_unit(d["hin"][(3 + dd) * 8 + h])
                    proj_fm(c, wu[s_f], ("hwu", s_f), 0, hrhs, hkeys)
                    lbh = lb[:, dd, h:h + 1]
                    lastoff = CHUNK - 1 if dd == 0 else 0
                    NB = HL // CHUNK

                    def g1(hf):
                        P.add("act", lambda e: e.activation(out=gA[hf][:, :], in_=bank(hf * (HL // 512), HL // 512), func=AF.Exp, scale=-1.0), reads=psk(hf * (HL // 512), HL // 512), writes=[("gA", hf)])

                    def g2(hf):
                        P.add("act", lambda e: e.activation(out=gB[hf][:, :], in_=gA[hf][:, :], func=AF.Ln, bias=1.0, scale=lbh), reads=[("gA", hf), "lb"], writes=[("gB", hf)])

                    def g3(hf):
                        P.add("act", lambda e: e.activation(out=gA[hf][:, :], in_=gA[hf][:, :], func=AF.Ln, bias=1.0, scale=1.0), reads=[("gA", hf)], writes=[("gA", hf)])

                    def g4_(hf):
                        P.add("dve", lambda e: e.tensor_tensor(out=gB[hf][:, :], in0=gB[hf][:, :], in1=gA[hf][:, :], op=ALU.subtract),
                              reads=[("gA", hf), ("gB", hf)], writes=[("gB", hf)])

                    def g5(hf):
                        P.add("act", lambda e: e.activation(out=gA[hf][:, :], in_=gB[hf][:, :], func=AF.Exp), reads=[("gB", hf)], writes=[("gA", hf)])

                    def g6(hf):
                        P.add("dve", lambda e: e.tensor_scalar(out=gA[hf][:, :], in0=gA[hf][:, :], scalar1=-1.0, scalar2=1.0, op0=ALU.mult, op1=ALU.add),
                              reads=[("gA", hf)], writes=[("gA", hf)])

                    def g7(hf):
                        for blk in range(HL // 512):
                            bs = slice(blk * 512, (blk + 1) * 512)
                            if dd == 0:
                                P.add("dve", lambda e, bs=bs: e.tensor_tensor_scan(out=gC[hf][:, bs], data0=hmask[:, 0, :], data1=gB[hf][:, bs], initial=0.0,
                                                                                    op0=ALU.mult, op1=ALU.add), reads=[("gB", hf), "hmask"], writes=[("gC", hf, blk)])
                            else:
                                P.add("dve", lambda e, bs=bs: e.tensor_tensor_scan(out=gC[hf][:, bs][:, ::-1], data0=hmask[:, 1, ::-1], data1=gB[hf][:, bs][:, ::-1],
                                                                                    initial=0.0, op0=ALU.mult, op1=ALU.add),
                                      reads=[("gB", hf), "hmask"], writes=[("gC", hf, blk)])

                    def gck(hf):
                        return [("gC", hf, blk) for blk in range(HL // 512)]

                    def g8(hf):
                        P.add("act", lambda e: e.activation(out=gB[hf][:, :], in_=gC[hf][:, :], func=AF.Exp), reads=gck(hf) + [("gB", hf)], writes=[("gB", hf)])

                    def g9(hf):
                        P.add("act", lambda e: e.activation(out=gC[hf][:, :], in_=gC[hf][:, :], func=AF.Exp, scale=-1.0), reads=gck(hf), writes=gck(hf))

                    def g10(hf):
                        tok = slice(hf * HL, (hf + 1) * HL)
                        P.add("dve", lambda e: e.tensor_tensor(out=qt[:, tok], in0=qf[:, tok], in1=gB[hf][:, :], op=ALU.mult),
                              reads=["qf", ("gB", hf)], writes=[("qt", dd, hf)])

                    def g11(hf):
                        P.add("dve", lambda e: e.tensor_tensor(out=gA[hf][:, :], in0=gA[hf][:, :], in1=gC[hf][:, :], op=ALU.mult),
                              reads=[("gA", hf)] + gck(hf), writes=[("gA", hf)])

                    def g12(hf):
                        tok = slice(hf * HL, (hf + 1) * HL)
                        P.add("pool", lambda e: e.tensor_copy(out=kt[:, tok], in_=gA[hf][:, :]), reads=[("gA", hf)], writes=[("kt", dd, hf)])

                    def g13(hf):
                        tok = slice(hf * HL, (hf + 1) * HL)
                        ebl = gB[hf][:, lastoff::CHUNK]
                        P.add("dve", lambda e: e.tensor_tensor(out=khfm[:, tok].rearrange("p (c l) -> p c l", l=CHUNK),
                                                               in0=gA[hf][:, :].rearrange("p (c l) -> p c l", l=CHUNK),
                                                               in1=ebl.unsqueeze(2).to_broadcast([128, NB, CHUNK]), op=ALU.mult),
                              reads=[("gA", hf), ("gB", hf)], writes=[("khfm", hf)])

                    def g14(hf):
                        ebl = gB[hf][:, lastoff::CHUNK]
                        P.add("pool", lambda e: e.tensor_copy(out=dec[:, hf * NB:(hf + 1) * NB], in_=ebl), reads=[("gB", hf)], writes=[("dec", dd, hf)])

                    return [g1, g2, g3, g4_, g5, g6, g7, g8, g9, g10, g11, g12, g13, g14]

                def transposes_for(dd):
                    khT = khTL[dd]
                    for t8 in (range(2) if "t" in HST else ()):
                        def fnt(e, t8=t8):
                            ins = None
                            for i in range(8):
                                tt = t8 * 8 + i
                                ins = e.transpose(out=PSb[:, (6 + t8) * 1024 + i * 128:(6 + t8) * 1024 + (i + 1) * 128], in_=khfm[:, tt * 128:(tt + 1) * 128],
                                                  identity=identb[:, :])
                            return ins
                        P.add("pe", fnt, reads=[("khfm", hf) for hf in range((t8 * 1024) // HL, ((t8 + 1) * 1024) // HL)] + ["identb"], writes=psk(6 + t8))
                        P.add("act", lambda e, t8=t8: e.activation(out=khT[:, t8 * 8:(t8 + 1) * 8, :].rearrange("p t c -> p (t c)"),
                                                                    in_=PSb[:, (6 + t8) * 1024:(7 + t8) * 1024], func=AF.Copy),
                              reads=psk(6 + t8), writes=[("khT", dd, t8)])

                def recurrence_for(dd, inject, h=h):
                    qt, kt, dec, khT = qtL[dd], ktL[dd], decL[dd], khTL[dd]
                    cur = 0
                    P.add("sp", lambda e, dd=dd, h=h: e.dma_start(out=Sf[0][:, :], in_=d["state"][dd, h]), writes=[("Sf", 0)], dma=True)
                    P.add("act", lambda e: e.activation(out=Sb[0][:, :], in_=Sf[0][:, :], func=AF.Copy), reads=[("Sf", 0)], writes=[("Sb", 0)])
                    tiles = list(range(16)) if dd == 0 else list(range(15, -1, -1))
                    order = list(range(4)) if dd == 0 else list(range(3, -1, -1))

                    def emit_scores(ti, tt):
                        bS = ti % 2
                        tsl = slice(tt * 128, (tt + 1) * 128)
                        P.add("pe", lambda e: e.matmul(bank(bS)[:, 0:128], lhsT=kt[:, tsl], rhs=qt[:, tsl], start=True, stop=True),
                              reads=[("kt", dd, (tt * 128) // HL), ("qt", dd, (tt * 128) // HL)], writes=psk(bS))

                    def emit_U(tt, i):
                        kw = dict(tile_position=(96, 0)) if i == 3 else {}
                        P.add("pe", lambda e: e.matmul(bank(4 + i)[:, 0:128], lhsT=khT[32 * i:32 * i + 32, tt, :],
                                                       rhs=vtm[32 * i:32 * i + 32, tt, :], start=True, stop=True, **kw),
                              reads=[("khT", dd, tt // 8), ("vtm", tt // 4)], writes=psk(4 + i))

                    emit_scores(0, tiles[0])
                    for i in order:
                        emit_U(tiles[0], i)
                    for ti, tt in enumerate(tiles):
                        par = ti % 2
                        bS, bO = par, 2 + par
                        tsl = slice(tt * 128, (tt + 1) * 128)
                        hfk = (tt * 128) // HL
                        nxt_t = tiles[ti + 1] if ti + 1 < len(tiles) else None
                        P.add("dve", lambda e, par=par, bS=bS, dd=dd: e.tensor_tensor(out=Am[par][:, :], in0=bank(bS)[:, 0:128], in1=trim[:, dd, :], op=ALU.mult),
                              reads=psk(bS) + ["trim"], writes=[("Am", par)])
                        P.add("pe", lambda e, par=par, bO=bO, tt=tt: e.matmul(bank(bO)[:, 0:128], lhsT=vtm[:, tt, :], rhs=Am[par][:, :], start=True, stop=False),
                              reads=[("Am", par), ("vtm", tt // 4)], writes=psk(bO))
                        if nxt_t is not None:
                            emit_scores(ti + 1, nxt_t)
                        for oi, i in enumerate(order):
                            cch = tt * 4 + i
                            csl = slice(cch * CHUNK, (cch + 1) * CHUNK)
                            P.add("pe", lambda e, bO=bO, i=i, cur=cur, csl=csl, oi=oi: e.matmul(bank(bO)[:, i * CHUNK:(i + 1) * CHUNK], lhsT=Sb[cur][:, :], rhs=qt[:, csl],
                                                                                                 start=False, stop=(oi == 3)),
                                  reads=[("Sb", cur), ("qt", dd, hfk)], writes=psk(bO))
                            nxt = (cur + 1) % NR
                            P.add("dve", lambda e, nxt=nxt, cur=cur, cch=cch, i=i: e.scalar_tensor_tensor(
                                out=Sf[nxt][:, :], in0=Sf[cur][:, :], scalar=dec[:, cch:cch + 1], in1=bank(4 + i)[:, 0:128], op0=ALU.mult, op1=ALU.add),
                                reads=[("Sf", cur), ("dec", dd, cch // (HL // CHUNK))] + psk(4 + i), writes=[("Sf", nxt)])
                            cur = nxt
                            seg_end = (cch % 8 == 7) if dd == 0 else (cch % 8 == 0)
                            if seg_end:
                                seg = cch // 8
                                P.add("sp", lambda e, cur=cur, seg=seg, dd=dd, h=h: e.dma_start(out=d["o_s"][seg, dd, h], in_=Sf[cur][:, :]),
                                      reads=[("Sf", cur)], writes=[("o_s", seg, dd, h)], dma=True)
                                last = (cch == NCH - 1) if dd == 0 else (cch == 0)
                                if not last:
                                    nxt = (cur + 1) % NR
                                    P.add("dve", lambda e, nxt=nxt, cur=cur: e.tensor_scalar(out=Sf[nxt][:, :], in0=Sf[cur][:, :], scalar1=keep, scalar2=None, op0=ALU.mult),
                                          reads=[("Sf", cur), "flags"], writes=[("Sf", nxt)])
                                    cur = nxt
                            P.add("act", lambda e, cur=cur: e.activation(out=Sb[cur][:, :], in_=Sf[cur][:, :], func=AF.Copy), reads=[("Sf", cur)], writes=[("Sb", cur)])
                            if nxt_t is not None:
                                emit_U(nxt_t, i)
                            if inject is not None:
                                inject()
                        if dd == 0:
                            P.add("act", lambda e, bO=bO, tsl=tsl: e.activation(out=of[:, tsl], in_=bank(bO)[:, 0:128], func=AF.Copy), reads=psk(bO), writes=[("of", tt)])
                        else:
                            P.add("dve", lambda e, bO=bO, tsl=tsl: e.tensor_tensor(out=of[:, tsl], in0=bank(bO)[:, 0:128], in1=of[:, tsl], op=ALU.add),
                                  reads=psk(bO) + [("of", tt)], writes=[("of", tt)])

                st0 = gates_for(0)
                for st in st0:
                    for hf in range(NP):
                        st(hf)
                transposes_for(0)
                st1 = gates_for(1)
                for hf in range(NP):
                    st1[0](hf)
                pend = [(st, hf) for st in st1[1:] for hf in range(NP)]

                def inject():
                    for _ in range(1):
                        if pend:
                            st_, hf_ = pend.pop(0)
                            st_(hf_)
                recurrence_for(0, inject)
                while pend:
                    st_, hf_ = pend.pop(0)
                    st_(hf_)
                transposes_for(1)
                recurrence_for(1, None)
                if "n" not in HST:
                    continue
                s_g = next_unit(d["hin"][2 * 8 + h])
                proj_fm(c, wu[s_g], ("hwu", s_g), 1, hrhs, hkeys)
                TPP = HL // 128

                def n1(p):
                    tok = slice(p * HL, (p + 1) * HL)
                    ofk = [("of", tt) for tt in range(p * TPP, (p + 1) * TPP)]
                    P.add("act", lambda e: e.activation(out=sqr[p % 2][:, :], in_=of[:, tok], func=AF.Square), reads=ofk, writes=[("gsq", p % 2)])

                def n2(p):
                    P.add("pe", lambda e: e.matmul(bank(p), lhsT=onesr[:, :], rhs=sqr[p % 2][:, :], start=True, stop=True),
                          reads=[("gsq", p % 2), "onesr"], writes=psk(p))

                def n3(p):
                    P.add("act", lambda e: e.activation(out=gA[p][:, :], in_=bank(p), func=AF.Ln, bias=EPS, scale=1.0 / 128.0), reads=psk(p), writes=[("gA", p)])

                def n4(p):
                    P.add("act", lambda e: e.activation(out=gA[p][:, :], in_=gA[p][:, :], func=AF.Exp, scale=-0.5), reads=[("gA", p)], writes=[("gA", p)])

                def n5(p):
                    P.add("act", lambda e: e.activation(out=gB[p][:, :], in_=bank(4 + p), func=AF.Silu), reads=psk(4 + p), writes=[("gB", p)])

                def n6(p):
                    tok = slice(p * HL, (p + 1) * HL)
                    ofk = [("of", tt) for tt in range(p * TPP, (p + 1) * TPP)]
                    P.add("dve", lambda e: e.scalar_tensor_tensor(out=gA[p][:, :], in0=of[:, tok], scalar=hng[:, 0:1], in1=gA[p][:, :], op0=ALU.mult, op1=ALU.mult),
                          reads=ofk + [("gA", p), "hng"], writes=[("gA", p)])

                def n7(p, h=h):
                    tok = slice(p * HL, (p + 1) * HL)
                    P.add("pool", lambda e: e.tensor_tensor(out=yT[:, h % 4, tok], in0=gA[p][:, :], in1=gB[p][:, :], op=ALU.mult),
                          reads=[("gA", p), ("gB", p)], writes=[("yT", h % 4, p)])

                for p in range(NP):
                    n1(p)
                    n2(p)
                for st in (n3, n4, n5, n6, n7):
                    for p in range(NP):
                        st(p)
                if h % 4 == 3:
                    for j in range(KC):
                        slot = next_unit(None)
                        half = j % 2
                        proj_fm(c, wu[slot], ("hwu", slot), half, lambda k, g: yT[:, k, g * TGS:(g + 1) * TGS],
                                lambda g: [("yT", k, g) for k in range(4)], nk=4)
                        P.add("dve", lambda e, j=j, half=half: e.scalar_tensor_tensor(out=xT[:, j, :], in0=bank(half * 4, 4), scalar=mv[:, 1, 2, j:j + 1],
                                                                                     in1=xT[:, j, :], op0=ALU.mult, op1=ALU.add),
                              reads=psk(half * 4, 4) + MVK + xk(j), writes=xk(j))
            P.barrier()
        P.barrier()


def _units(W, cols_list):
    K = W.shape[0]
    out = np.empty((len(cols_list), 128, K // 128, 128), dtype=np.float32)
    for i, cols in enumerate(cols_list):
        blk = W[:, cols]
        out[i] = blk.reshape(K // 128, 128, 128).transpose(1, 0, 2)
    return out


def prep_inputs(inp):
    f = lambda a: np.ascontiguousarray(np.asarray(a, dtype=np.float32))
    x_prompt, x_sample = f(inp["x_prompt"]), f(inp["x_sample"])
    cache_k, cache_v, state = f(inp["cache_k"]), f(inp["cache_v"]), f(inp["state_hgrn"])
    cvec, c_ctx = f(inp["c"]), f(inp["c_ctx"])
    shared = {}
    shared["w_mod"] = f(inp["w_mod"])
    bm = f(inp["b_mod"])
    shared["bmod"] = np.ascontiguousarray(np.concatenate([bm[l].reshape(48, 128).T for l in range(2)], axis=1))
    lnp = np.empty((128, 2, 4, KC), np.float32)
    for l in range(2):
        for i, nm in enumerate(("ln1_g", "ln1_b", "ln2_g", "ln2_b")):
            lnp[:, l, i, :] = f(inp[nm])[l].reshape(KC, 128).T
    shared["lnp"] = lnp
    Win = f(inp["attn_w_in"])[0]
    ar = np.arange
    cols = [ar(c * 128, (c + 1) * 128) for c in range(4)]
    cols += [np.concatenate([ar(512, 576), ar(512, 576)]), np.concatenate([ar(576, 640), ar(576, 640)])]
    cols += [ar(640, 768)]
    cols += [ar(768 + c * 128, 768 + (c + 1) * 128) for c in range(12)]
    shared["attn_in_u"] = _units(Win, cols)
    sw = _swap_idx()
    qg, kg = f(inp["attn_q_gain"])[0], f(inp["attn_k_gain"])[0]
    gains = np.stack([np.tile(qg, 2), np.tile(qg[sw], 2), np.tile(kg, 2), np.tile(kg[sw], 2)], axis=1)
    shared["gains"] = np.ascontiguousarray(gains.astype(np.float32))
    swapT = np.zeros((128, 128), np.float32)
    for m in range(128):
        p = (m // 64) * 64 + sw[m % 64]
        swapT[p, m] = 1.0
    bones = np.zeros((128, 128), np.float32)
    bones[:64, :64] = 1.0
    bones[64:, 64:] = 1.0
    shared["consts"] = np.stack([swapT, bones, np.eye(128, dtype=np.float32)])
    scw = f(inp["sconv_w"])[0]
    shared["scw"] = np.ascontiguousarray(scw.reshape(3, 4, 128).transpose(2, 1, 0))
    shared["attn_out_u"] = _units(f(inp["attn_w_out"])[0], [ar(j * 128, (j + 1) * 128) for j in range(8)])
    wup = f(inp["ffn_w_up"])
    shared["ffn_up_u"] = np.stack([_units(wup[l], [ar(c * 128, (c + 1) * 128) for c in range(44)]) for l in range(2)])
    fcw = f(inp["ffn_conv_w"])
    shared["fcw"] = np.ascontiguousarray(fcw.reshape(2, 3, 44, 128).transpose(0, 3, 2, 1))
    wdn = f(inp["ffn_w_down"])
    shared["ffn_dn_u"] = np.stack([_units(wdn[l], [ar(j * 128, (j + 1) * 128) for j in range(8)]) for l in range(2)])
    hin = f(inp["hgrn_w_in"])[0]
    shared["hgrn_in_u"] = _units(hin, [ar(kind * 1024 + h * 128, kind * 1024 + (h + 1) * 128) for kind in range(5) for h in range(8)])
    lbl = f(inp["hgrn_lb_logits"])
    shared["lbl"] = np.ascontiguousarray(lbl.reshape(2, 2, 8, 128).transpose(3, 0, 1, 2))
    shared["hng"] = np.ascontiguousarray(f(inp["hgrn_norm_g"])[0].reshape(128, 1))
    shared["hgrn_out_u"] = _units(f(inp["hgrn_w_out"])[0], [ar(j * 128, (j + 1) * 128) for j in range(8)])
    hm = np.zeros((128, 2, 512), np.float32)
    t = np.arange(512)
    hm[:, 0, :] = (t % CHUNK != 0).astype(np.float32)[None, :]
    hm[:, 1, :] = (t % CHUNK != CHUNK - 1).astype(np.float32)[None, :]
    sidx = np.arange(128)[:, None]
    tidx = np.arange(128)[None, :]
    same = (sidx // CHUNK) == (tidx // CHUNK)
    ht = np.zeros((128, 2, 128), np.float32)
    ht[:, 0, :] = (same & (sidx <= tidx)).astype(np.float32)
    ht[:, 1, :] = (same & (sidx >= tidx)).astype(np.float32)
    shared["hscan"] = hm
    shared["htri"] = ht
    tok = np.arange(T)
    row = (tok // 64).astype(np.float32)
    colp = (tok % 64).astype(np.float32)
    inv = (10000.0 ** (-np.arange(0, 32, 2, dtype=np.float32) / 32.0)).astype(np.float32)
    C = np.zeros((64, T), np.float32)
    S = np.zeros((64, T), np.float32)
    for dd in range(64):
        blk = dd // 16
        pos = row if blk < 2 else colp
        ang = pos * inv[dd % 16]
        C[dd] = np.cos(ang)
        S[dd] = -np.sin(ang) if blk % 2 == 0 else np.sin(ang)
    ropeC_s, ropeS_s = np.tile(C, (2, 1)), np.tile(S, (2, 1))
    ropeC_p, ropeS_p = np.ones((128, T), np.float32), np.zeros((128, T), np.float32)
    maps = []
    for core in range(8):
        m = dict(shared)
        if core < 4:
            b = core
            xt = x_sample[b]
            m["cond"] = np.ascontiguousarray(cvec[b].reshape(KC, 128).T)
            fl = np.zeros((128, 4), np.float32)
            fl[:, 0] = 1.0
            fl[:, 1] = 1.0
            m["ropeC"], m["ropeS"] = ropeC_s, ropeS_s
            kc = cache_k[b, 0]
            m["kcT"] = np.ascontiguousarray(np.stack([np.tile(kc[:, g, :].T, (2, 1)) for g in range(2)]))
            m["vc"] = np.ascontiguousarray(cache_v[b, 0].reshape(PAST, 128))
            m["state"] = np.ascontiguousarray(state[b, 0])
        else:
            p0 = (core - 4) * 8
            xt = x_prompt[p0:p0 + 8].reshape(T, D)
            m["cond"] = np.ascontiguousarray(c_ctx.reshape(KC, 128).T)
            fl = np.zeros((128, 4), np.float32)
            fl[:, 2] = -1.0
            m["ropeC"], m["ropeS"] = ropeC_p, ropeS_p
            m["kcT"] = np.zeros((2, 128, PAST), np.float32)
            m["vc"] = np.zeros((PAST, 128), np.float32)
            m["state"] = np.zeros((2, 8, 128, 128), np.float32)
        m["flags"] = fl
        m["xT"] = np.ascontiguousarray(xt.T)
        maps.append(m)
    return maps


_CACHE = {}


def kernel(**inputs):
    maps = prep_inputs(inputs)
    if "nc" not in _CACHE:
        _CACHE["nc"] = build()[0]
    nc = _CACHE["nc"]
    res = run_bass_kernel_spmd(nc, maps, core_ids=list(range(8)))
    r = res.results
    y_s = np.stack([r[c]["yT"].T for c in range(4)])
    y_p = np.concatenate([r[c]["yT"].T.reshape(8, SEG, D) for c in range(4, 8)])
    nk = np.concatenate([r[c]["koutT"].transpose(2, 0, 1).reshape(8, 1, SEG, 2, 64) for c in range(4, 8)])
    nv = np.concatenate([r[c]["vout"].reshape(8, 1, SEG, 2, 64) for c in range(4, 8)])
    ns = np.concatenate([r[c]["sout"].reshape(8, 1, 2, 8, 128, 128) for c in range(4, 8)])
    return (np.ascontiguousarray(y_p, dtype=np.float32), np.ascontiguousarray(y_s, dtype=np.float32),
            np.ascontiguousarray(nk, dtype=np.float32), np.ascontiguousarray(nv, dtype=np.float32),
            np.ascontiguousarray(ns, dtype=np.float32))
```

```python
import os
import numpy as np
import concourse.bass as bass
import concourse.mybir as mybir
from concourse.bass_utils import run_bass_kernel_spmd
from contextlib import ExitStack

F32 = mybir.dt.float32
F32R = mybir.dt.float32r
BF16 = mybir.dt.bfloat16
AF = mybir.ActivationFunctionType
ALU = mybir.AluOpType

T = 2048
D = 1024
KC = 8
NTG = 4
TGS = 512
NSEG = 8
SEG = 256
ALPHA = 4.0 ** 0.25
EPS = 1e-6
EPS_LN = EPS / (ALPHA * ALPHA)
DFF = 2816
NFC = 22
PAST = 512
NKC = 20
CHUNK = 32
NCH = T // CHUNK


class Prog:
    ENGS = ("pe", "act", "dve", "pool", "sp")

    def __init__(self, nc, es, n_dma_sems=12):
        self.nc = nc
        self.eng = {"pe": nc.tensor, "act": nc.scalar, "dve": nc.vector, "pool": nc.gpsimd, "sp": nc.sync}
        self.esem = {e: es.enter_context(nc.semaphore("e_" + e)) for e in self.ENGS}
        self.nds = n_dma_sems
        self.dsem = {e: [es.enter_context(nc.semaphore("d_%s_%d" % (e, i))) for i in range(n_dma_sems)]
                     for e in ("sp", "pool")}
        self.cnt = {e: 0 for e in self.ENGS}
        self.ndma = {e: 0 for e in self.dsem}
        self.dval = {}
        self.waited = {e: {} for e in self.ENGS}
        self.last_w = {}
        self.readers = {}
        self.nops = {e: 0 for e in self.ENGS}

    def _wait(self, e, sem, val):
        w = self.waited[e]
        if w.get(sem.num, 0) >= val:
            return
        w[sem.num] = val
        self.eng[e].wait_ge(sem, val)

    def _wait_tok(self, e, tok, is_dma_op):
        if tok[0] == "E":
            _, f, c = tok
            if f == "pe" and e == "pe" and not is_dma_op:
                return
            self._wait(e, self.esem[f], c)
        else:
            _, q, i, v = tok
            self._wait(e, self.dsem[q][i], v)

    def add(self, e, fn, reads=(), writes=(), dma=False):
        psr = [k for k in reads if isinstance(k, tuple) and k[0] == "ps"]
        if psr:
            writes = list(writes) + [k for k in psr if k not in writes]
        deps = set()
        for k in reads:
            w = self.last_w.get(k)
            if w is not None:
                deps.add(w)
        for k in writes:
            w = self.last_w.get(k)
            if w is not None:
                deps.add(w)
            for r in self.readers.get(k, ()):
                deps.add(r)
        for tok in sorted(deps):
            self._wait_tok(e, tok, dma)
        eng = self.eng[e]
        if dma:
            i = self.ndma[e] % self.nds
            self.ndma[e] += 1
            prev = self.dval.get((e, i), 0)
            if prev:
                self._wait(e, self.dsem[e][i], prev)
            ins = fn(eng)
            ins.then_inc(self.dsem[e][i], 16)
            self.dval[(e, i)] = prev + 16
            tok = ("D", e, i, prev + 16)
        else:
            ins = fn(eng)
            self.cnt[e] += 1
            ins.then_inc(self.esem[e], 1)
            tok = ("E", e, self.cnt[e])
        self.nops[e] += 1
        for k in writes:
            self.last_w[k] = tok
            self.readers[k] = []
        for k in reads:
            self.readers.setdefault(k, []).append(tok)
        return tok

    def barrier(self):
        for e in self.ENGS:
            for f in self.ENGS:
                if f != e and self.cnt[f] > 0:
                    self._wait(e, self.esem[f], self.cnt[f])
            for (q, i), v in self.dval.items():
                self._wait(e, self.dsem[q][i], v)
        self.last_w.clear()
        self.readers.clear()


def _swap_idx():
    idx = np.zeros(64, dtype=np.int64)
    for d in range(64):
        blk = d // 16
        idx[d] = d + 16 if blk % 2 == 0 else d - 16
    return idx


def build(n_dbg=None, upto=99):
    nc = bass.Bass("TRN2", target_bir_lowering=False)
    es = ExitStack()
    P = Prog(nc, es)

    def din(name, shape, dt=F32):
        return nc.dram_tensor(name, list(shape), dt, kind="ExternalInput").ap()

    def dout(name, shape, dt=F32):
        return nc.dram_tensor(name, list(shape), dt, kind="ExternalOutput").ap()

    sbn = [0]

    def sb(stack, name, shape, dt):
        sbn[0] += 1
        return stack.enter_context(nc.sbuf_tensor("s%d_%s" % (sbn[0], name), list(shape), dt))

    d_xT = din("xT", [D, T])
    d_cond = din("cond", [128, KC])
    d_flags = din("flags", [128, 4])
    d_ropeC = din("ropeC", [128, T])
    d_ropeS = din("ropeS", [128, T])
    d_kcT = din("kcT", [2, 128, PAST])
    d_vc = din("vc", [PAST, 128])
    d_state = din("state", [2, 8, 128, 128])
    d_wmod = din("w_mod", [2, D, 6 * D])
    d_bmod = din("bmod", [128, 96])
    d_lnp = din("lnp", [128, 2, 4, KC])
    d_win = din("attn_in_u", [19, 128, KC, 128])
    d_gains = din("gains", [128, 4])
    d_consts = din("consts", [3, 128, 128])
    d_scw = din("scw", [128, 4, 3])
    d_wout = din("attn_out_u", [8, 128, KC, 128])
    d_wup = din("ffn_up_u", [2, 44, 128, KC, 128])
    d_fcw = din("fcw", [2, 128, 44, 3])
    d_wdn = din("ffn_dn_u", [2, 8, 128, NFC, 128])
    d_hin = din("hgrn_in_u", [40, 128, KC, 128])
    d_lbl = din("lbl", [128, 2, 2, 8])
    d_hng = din("hng", [128, 1])
    d_hout = din("hgrn_out_u", [8, 128, KC, 128])
    d_hmask = din("hscan", [128, 2, 512])
    d_htri = din("htri", [128, 2, 128])

    o_yT = dout("yT", [D, T])
    o_kT = dout("koutT", [2, 64, T])
    o_v = dout("vout", [T, 128])
    o_s = dout("sout", [NSEG, 2, 8, 128, 128])

    dbg = {}

    def tap(name, ap, shape, reads, dt=F32):
        if n_dbg is None:
            return
        d = dout("dbg_" + name, shape, dt)
        n_dbg[name] = shape
        P.add("sp", lambda e: e.dma_start(out=d, in_=ap), reads=reads, writes=["dbg_" + name], dma=True)

    xT = sb(es, "xT_sb", [128, KC, T], F32)
    hT = sb(es, "hT_sb", [128, KC, T], BF16)
    mv = sb(es, "mv", [128, 2, 10, KC], F32)
    lnp = sb(es, "lnp_sb", [128, 2, 4, KC], F32)
    flags = sb(es, "flags_sb", [128, 4], F32)
    onesr = sb(es, "onesr", [128, 128], F32R)
    bonesr = sb(es, "bonesr", [128, 128], F32R)
    swapT = sb(es, "swapT", [128, 128], BF16)
    identb = sb(es, "identb", [128, 128], BF16)
    cst_f = sb(es, "cst_f", [128, 3, 128], F32)
    PS = es.enter_context(nc.psum_tensor("PS", [128, 4096], F32))

    def bank(b, n=1):
        return PS[:, b * 512:(b + n) * 512]

    def psk(b, n=1):
        return [("ps", i) for i in range(b, b + n)]

    def xk(j, g=None):
        if g is None:
            return [("xT", j, gg) for gg in range(NTG)]
        return [("xT", j, g)]

    def hk(j, g=None):
        if g is None:
            return [("hT", j, gg) for gg in range(NTG)]
        return [("hT", j, g)]

    om = flags[:, 0:1]
    keep = flags[:, 1:2]
    nbm = flags[:, 2:3]

    for j in range(KC):
        P.add("sp", lambda e, j=j: e.dma_start(out=xT[:, j, :], in_=d_xT[j * 128:(j + 1) * 128, :]),
              writes=xk(j), dma=True)
    P.add("sp", lambda e: e.dma_start(out=flags[:, :], in_=d_flags[:, :]), writes=["flags"], dma=True)
    P.add("sp", lambda e: e.dma_start(out=lnp[:], in_=d_lnp[:]), writes=["lnp"], dma=True)
    P.add("sp", lambda e: e.dma_start(out=cst_f[:], in_=d_consts.rearrange("c p m -> p c m")), writes=["cst_f"], dma=True)
    P.add("dve", lambda e: e.tensor_copy(out=swapT[:, :], in_=cst_f[:, 0, :]), reads=["cst_f"], writes=["swapT"])
    P.add("dve", lambda e: e.tensor_copy(out=bonesr[:, :], in_=cst_f[:, 1, :]), reads=["cst_f"], writes=["bonesr"])
    P.add("dve", lambda e: e.tensor_copy(out=identb[:, :], in_=cst_f[:, 2, :]), reads=["cst_f"], writes=["identb"])
    P.add("pool", lambda e: e.memset(cst_f[:, 1, :], 1.0), reads=["bonesr"], writes=["cst_f"])
    P.add("dve", lambda e: e.tensor_copy(out=onesr[:, :], in_=cst_f[:, 1, :]), reads=["cst_f"], writes=["onesr"])

    scond = sb(es, "scond", [128, KC], BF16)
    bmod = sb(es, "bmod_sb", [128, 96], F32)
    mod = sb(es, "mod_sb", [128, 2, 6, KC], F32)
    with ExitStack() as ph:
        cond = sb(ph, "cond_sb", [128, KC], F32)
        wm = [sb(ph, "wm%d" % i, [128, KC, 1024], BF16) for i in range(2)]
        P.add("sp", lambda e: e.dma_start(out=cond[:, :], in_=d_cond[:, :]), writes=["cond"], dma=True)
        P.add("sp", lambda e: e.dma_start(out=bmod[:, :], in_=d_bmod[:, :]), writes=["bmod"], dma=True)
        P.add("act", lambda e: e.activation(out=scond[:, :], in_=cond[:, :], func=AF.Silu), reads=["cond"], writes=["scond"])
        for l in range(1):
            wv = d_wmod[l].rearrange("(k p) n -> p k n", p=128)
            for pc in range(3):
                slot = (l * 6 + pc) % 2
                P.add("pool", lambda e, slot=slot, pc=pc, wv=wv: e.dma_start(out=wm[slot][:], in_=wv[:, :, pc * 1024:(pc + 1) * 1024]),
                      writes=[("wm", slot)], dma=True)
                for cc in range(8):
                    col = l * 48 + pc * 8 + cc

                    def mmfn(e, slot=slot, cc=cc, col=col):
                        ins = None
                        for k in range(KC):
                            ins = e.matmul(PS[:, col:col + 1], lhsT=wm[slot][:, k, cc * 128:(cc + 1) * 128],
                                           rhs=scond[:, k:k + 1], start=(k == 0), stop=(k == KC - 1))
                        return ins
                    P.add("pe", mmfn, reads=[("wm", slot), "scond"], writes=[("ps", 0)])
        P.add("dve", lambda e: e.tensor_tensor(out=mod[:, 0, 0:3].rearrange("p v c -> p (v c)"), in0=PS[:, 0:24], in1=bmod[:, 0:24], op=ALU.add),
              reads=[("ps", 0), "bmod"], writes=["mod0a"])

        def derive(l, part):
            mk = "mod%d%s" % (l, part)
            if part == "a":
                P.add("dve", lambda e: e.tensor_copy(out=mv[:, l, 0, :], in_=mod[:, l, 0, :]), reads=[mk], writes=[("mv", l, 0)])
                P.add("dve", lambda e: e.tensor_scalar(out=mv[:, l, 1, :], in0=mod[:, l, 1, :], scalar1=1.0, scalar2=None, op0=ALU.add),
                      reads=[mk], writes=[("mv", l, 1)])
                P.add("dve", lambda e: e.tensor_scalar(out=mv[:, l, 2, :], in0=mod[:, l, 2, :], scalar1=1.0 / ALPHA, scalar2=None, op0=ALU.mult),
                      reads=[mk], writes=[("mv", l, 2)])
                return
            P.add("dve", lambda e: e.tensor_copy(out=mv[:, l, 3, :], in_=mod[:, l, 3, :]), reads=[mk], writes=[("mv", l, 3)])
            P.add("dve", lambda e: e.tensor_scalar(out=mv[:, l, 4, :], in0=mod[:, l, 4, :], scalar1=1.0, scalar2=None, op0=ALU.add),
                  reads=[mk], writes=[("mv", l, 4)])
            P.add("dve", lambda e: e.tensor_scalar(out=mv[:, l, 5, :], in0=mod[:, l, 5, :], scalar1=1.0 / ALPHA, scalar2=None, op0=ALU.mult),
                  reads=[mk], writes=[("mv", l, 5)])
            P.add("dve", lambda e: e.tensor_tensor(out=mv[:, l, 6, :], in0=lnp[:, l, 0, :], in1=mv[:, l, 4, :], op=ALU.mult),
                  reads=["lnp", ("mv", l, 4)], writes=[("mv", l, 6)])
            P.add("dve", lambda e: e.tensor_tensor(out=mv[:, l, 7, :], in0=lnp[:, l, 1, :], in1=mv[:, l, 4, :], op=ALU.mult),
                  reads=["lnp", ("mv", l, 4)], writes=[("mv", l, 7)])
            P.add("dve", lambda e: e.tensor_tensor(out=mv[:, l, 7, :], in0=mv[:, l, 7, :], in1=mv[:, l, 3, :], op=ALU.add),
                  reads=[("mv", l, 7), ("mv", l, 3)], writes=[("mv", l, 7)])

        def derive_cross():
            P.add("dve", lambda e: e.tensor_tensor(out=mv[:, 0, 8, :], in0=lnp[:, 0, 2, :], in1=mv[:, 1, 1, :], op=ALU.mult),
                  reads=["lnp", ("mv", 1, 1)], writes=[("mv", 0, 8)])
            P.add("dve", lambda e: e.tensor_tensor(out=mv[:, 0, 9, :], in0=lnp[:, 0, 3, :], in1=mv[:, 1, 1, :], op=ALU.mult),
                  reads=["lnp", ("mv", 1, 1)], writes=[("mv", 0, 9)])
            P.add("dve", lambda e: e.tensor_tensor(out=mv[:, 0, 9, :], in0=mv[:, 0, 9, :], in1=mv[:, 1, 0, :], op=ALU.add),
                  reads=[("mv", 0, 9), ("mv", 1, 0)], writes=[("mv", 0, 9)])

        derive(0, "a")
        tap("mv", mv[:].rearrange("p l v c -> p (l v c)"), [128, 2 * 10 * KC], [("mv", l, v) for l in range(2) for v in range(10)])
        P.barrier()

    MVK = [("mv", l, v) for l in range(2) for v in range(10)]

    for j in range(KC):
        if j % 2 == 0:
            P.add("dve", lambda e, j=j: e.tensor_scalar(out=hT[:, j, :], in0=xT[:, j, :], scalar1=mv[:, 0, 1, j:j + 1],
                                                        scalar2=mv[:, 0, 0, j:j + 1], op0=ALU.mult, op1=ALU.add),
                  reads=xk(j) + MVK, writes=hk(j))
        else:
            P.add("act", lambda e, j=j: e.activation(out=hT[:, j, :], in_=xT[:, j, :], func=AF.Identity,
                                                     bias=mv[:, 0, 0, j:j + 1], scale=mv[:, 0, 1, j:j + 1]),
                  reads=xk(j) + MVK, writes=hk(j))
    tap("h0", hT[:, 0, :], [128, T], hk(0)) if False else None

    ctx = dict(m1=dict(scond=scond, bmod=bmod, mod=mod, d_wmod=d_wmod, derive=derive, derive_cross=derive_cross),
               nc=nc, es=es, P=P, PS=PS, bank=bank, psk=psk, xk=xk, hk=hk, xT=xT, hT=hT, mv=mv, lnp=lnp, MVK=MVK,
               onesr=onesr, bonesr=bonesr, swapT=swapT, identb=identb, om=om, keep=keep, nbm=nbm, sb=sb, tap=tap,
               d=dict(win=d_win, gains=d_gains, ropeC=d_ropeC, ropeS=d_ropeS, kcT=d_kcT, vc=d_vc, scw=d_scw, wout=d_wout,
                      wup=d_wup, fcw=d_fcw, wdn=d_wdn, hin=d_hin, lbl=d_lbl, hng=d_hng, hout=d_hout, hmask=d_hmask, htri=d_htri,
                      state=d_state, o_yT=o_yT, o_kT=o_kT, o_v=o_v, o_s=o_s))
    if upto >= 1:
        layer0_mixer(ctx)
    if upto >= 2:
        layernorm(ctx, 0, 1)
    if upto >= 3:
        ffn(ctx, 0)
    if upto >= 4:
        layernorm(ctx, 0, 2)
    if upto >= 5:
        hgrn_mixer(ctx)
    if upto >= 6:
        layernorm(ctx, 1, 1)
    if upto >= 7:
        ffn(ctx, 1)
    if upto >= 8:
        layernorm(ctx, 1, 2, final=True)
    if n_dbg is not None:
        for j in range(KC):
            tap("x%d" % j, xT[:, j, :], [128, T], xk(j))
            tap("h%d" % j, hT[:, j, :], [128, T], hk(j), BF16)
    P.barrier()
    return nc, P


def load_unit(c, dst, src, key, reads=()):
    c["P"].add("pool", lambda e: e.dma_start(out=dst, in_=src), reads=list(reads), writes=[key], dma=True)


class WStream:
    def __init__(self, c, bufs, key, srcs, pf):
        self.c, self.bufs, self.key, self.srcs, self.pf = c, bufs, key, srcs, pf
        self.issued = 0
        self.n = len(bufs)

    def prefetch(self, upto):
        upto = min(upto, len(self.srcs) - 1)
        while self.issued <= upto:
            i = self.issued
            src = self.srcs[i]
            if isinstance(src, tuple):
                load_unit(self.c, self.bufs[i % self.n][:, 0:src[1], :], src[0], (self.key, i % self.n))
            else:
                load_unit(self.c, self.bufs[i % self.n][:], src, (self.key, i % self.n))
            self.issued += 1

    def get(self, i):
        self.prefetch(i + self.pf)
        return self.bufs[i % self.n], (self.key, i % self.n)


def proj_fm(c, wtile, wkey, half, rhs_of, rkeys, nk=KC):
    P, bank, psk = c["P"], c["bank"], c["psk"]
    for g in range(NTG):
        b = half * 4 + g

        def fn(e, g=g, b=b):
            ins = None
            for k in range(nk):
                ins = e.matmul(bank(b), lhsT=wtile[:, k, :], rhs=rhs_of(k, g), start=(k == 0), stop=(k == nk - 1))
            return ins
        P.add("pe", fn, reads=[wkey] + rkeys(g), writes=psk(b))


def layer0_mixer(c):
    P, nc, PS, bank, psk = c["P"], c["nc"], c["PS"], c["bank"], c["psk"]
    xT, hT, mv, sb, tap, d = c["xT"], c["hT"], c["mv"], c["sb"], c["tap"], c["d"]
    xk, hk, MVK = c["xk"], c["hk"], c["MVK"]
    om, nbm = c["om"], c["nbm"]
    onesr, bonesr, swapT = c["onesr"], c["bonesr"], c["swapT"]
    d_win = d["win"]
    L0 = os.environ.get("K_L0", "ABCDE")

    def hrhs(k, g):
        return hT[:, k, g * TGS:(g + 1) * TGS]

    def hkeys(g):
        return [("hT", k, g) for k in range(KC)]

    with ExitStack() as ph:
        scT = sb(ph, "scT", [128, 4, T], BF16)
        wu = [sb(ph, "wu%d" % i, [128, KC, 128], BF16) for i in range(4)]
        gains = sb(ph, "gains", [128, 4], F32)
        scw = sb(ph, "scw_sb", [128, 4, 3], F32)
        nbw = sb(ph, "nbw", [128, 4, 2], F32)
        P.add("sp", lambda e: e.dma_start(out=gains[:, :], in_=d["gains"][:, :]), writes=["gains"], dma=True)
        P.add("sp", lambda e: e.dma_start(out=scw[:], in_=d["scw"][:]), writes=["scw"], dma=True)
        P.add("dve", lambda e: e.tensor_scalar(out=nbw[:, :, 0], in0=scw[:, :, 0], scalar1=nbm, scalar2=None, op0=ALU.mult),
              reads=["scw", "flags"], writes=["nbw"])
        P.add("dve", lambda e: e.tensor_scalar(out=nbw[:, :, 1], in0=scw[:, :, 2], scalar1=nbm, scalar2=None, op0=ALU.mult),
              reads=["scw", "flags", "nbw"], writes=["nbw"])
        srcs0 = []
        for ch in range(4):
            srcs0 += [d_win[15 + ch], d_win[11 + ch], d_win[7 + ch]]
        srcs0 += [d_win[un] for un in range(6)] + [d_win[6]] + [d["wout"][j] for j in range(KC)]
        ws0 = WStream(c, wu, "wu", srcs0, 3)
        ucount = [0]

        def next_unit(src):
            i = ucount[0]
            ucount[0] += 1
            assert srcs0[i] is src or True
            ws0.get(i)
            return i % 4

        with ExitStack() as pa:
            xi_sb = sb(pa, "xi_sb", [128, T], F32)
            mpad = sb(pa, "mpad", [128, T + 2], F32)
            u = sb(pa, "u_sb", [128, T], F32)
            P.add("pool", lambda e: e.memset(mpad[:, :], 0.0), writes=["mpad"])
            for ch in (range(4) if "A" in L0 else ()):
                s_xi = next_unit(d_win[15 + ch])
                proj_fm(c, wu[s_xi], ("wu", s_xi), 0, hrhs, hkeys)
                s_cg = next_unit(d_win[11 + ch])
                P.add("act", lambda e: e.activation(out=xi_sb[:, :], in_=bank(0, 4), func=AF.Copy), reads=psk(0, 4), writes=["xi_sb"])
                proj_fm(c, wu[s_cg], ("wu", s_cg), 1, hrhs, hkeys)
                s_bg = next_unit(d_win[7 + ch])
                P.add("dve", lambda e: e.tensor_tensor(out=mpad[:, 1:T + 1], in0=bank(4, 4), in1=xi_sb[:, :], op=ALU.mult),
                      reads=psk(4, 4) + ["xi_sb"], writes=["mpad"])
                proj_fm(c, wu[s_bg], ("wu", s_bg), 0, hrhs, hkeys)
                P.add("act", lambda e, ch=ch: e.activation(out=u[:, :], in_=mpad[:, 1:T + 1], func=AF.Copy, scale=scw[:, ch, 1:2]),
                      reads=["mpad", "scw"], writes=["u"])
                P.add("dve", lambda e, ch=ch: e.scalar_tensor_tensor(out=u[:, :], in0=mpad[:, 0:T], scalar=scw[:, ch, 0:1], in1=u[:, :],
                                                                     op0=ALU.mult, op1=ALU.add), reads=["mpad", "scw", "u"], writes=["u"])
                P.add("dve", lambda e, ch=ch: e.scalar_tensor_tensor(out=u[:, :], in0=mpad[:, 2:T + 2], scalar=scw[:, ch, 2:3], in1=u[:, :],
                                                                     op0=ALU.mult, op1=ALU.add), reads=["mpad", "scw", "u"], writes=["u"])
                P.add("dve", lambda e, ch=ch: e.scalar_tensor_tensor(out=u[:, SEG:T:SEG], in0=mpad[:, SEG:T:SEG], scalar=nbw[:, ch, 0:1],
                                                                     in1=u[:, SEG:T:SEG], op0=ALU.mult, op1=ALU.add),
                      reads=["mpad", "nbw", "u"], writes=["u"])
                P.add("dve", lambda e, ch=ch: e.scalar_tensor_tensor(out=u[:, SEG - 1:T - 1:SEG], in0=mpad[:, SEG + 1:T + 1:SEG], scalar=nbw[:, ch, 1:2],
                                                                     in1=u[:, SEG - 1:T - 1:SEG], op0=ALU.mult, op1=ALU.add),
                      reads=["mpad", "nbw", "u"], writes=["u"])
                P.add("dve", lambda e, ch=ch: e.tensor_tensor(out=scT[:, ch, :], in0=bank(0, 4), in1=u[:, :], op=ALU.mult),
                      reads=psk(0, 4) + ["u"], writes=[("scT", ch)])
            P.barrier()
        tap("scT", scT[:].rearrange("p c t -> p (c t)"), [128, 4 * T], [("scT", ch) for ch in range(4)]) if False else None

        with ExitStack() as pb:
            qT = sb(pb, "qT", [128, 4, T], BF16)
            kTz = sb(pb, "kTz", [128, 2, 2, T + PAST], BF16)
            P.add("dve", lambda e: e.memset(kTz[:].rearrange("p a b t -> p (a b t)"), 0.0), writes=["kTz0"])
            with ExitStack() as pq:
                ropeC = sb(pq, "ropeC", [128, T], F32)
                ropeS = sb(pq, "ropeS", [128, T], F32)
                P.add("sp", lambda e: e.dma_start(out=ropeC[:, :], in_=d["ropeC"][:, :]), writes=["ropeC"], dma=True)
                P.add("sp", lambda e: e.dma_start(out=ropeS[:, :], in_=d["ropeS"][:, :]), writes=["ropeS"], dma=True)
                qsq = [sb(pq, "qsq%d" % i, [128, TGS], F32R) for i in range(2)]
                qbf = [sb(pq, "qbf%d" % i, [128, TGS], BF16) for i in range(2)]
                sd = [sb(pq, "sd%d" % i, [128, TGS], F32) for i in range(2)]
                rstd = [sb(pq, "rstd%d" % i, [128, TGS], F32) for i in range(2)]
                t1 = [sb(pq, "t1_%d" % i, [128, TGS], F32) for i in range(2)]
                t2 = [sb(pq, "t2_%d" % i, [128, TGS], F32) for i in range(2)]
                m1 = c["m1"]
                wm1 = [sb(pq, "wm1_%d" % i, [128, KC, 128], BF16) for i in range(4)]
                wv1 = m1["d_wmod"][1].rearrange("(k p) n -> p k n", p=128)

                wv0 = m1["d_wmod"][0].rearrange("(k p) n -> p k n", p=128)
                mchunks = [(wv0, cidx) for cidx in range(24, 48)] + [(wv1, cidx) for cidx in range(48)]

                def m1_load(pi):
                    if pi >= len(mchunks):
                        return
                    wvx, cidx = mchunks[pi]
                    P.add("pool", lambda e: e.dma_start(out=wm1[pi % 4][:], in_=wvx[:, :, cidx * 128:(cidx + 1) * 128]), writes=[("wm1", pi % 4)], dma=True)

                def m1_mm(pi):
                    col = 6 * 512 + pi

                    def mmfn(e):
                        ins = None
                        for k in range(KC):
                            ins = e.matmul(PS[:, col:col + 1], lhsT=wm1[pi % 4][:, k, :],
                                           rhs=m1["scond"][:, k:k + 1], start=(k == 0), stop=(k == KC - 1))
                        return ins
                    P.add("pe", mmfn, reads=[("wm1", pi % 4), "scond"], writes=[("ps", 6)])
                m1_load(0)
                m1_load(1)
                m1_load(2)
                it = 0
                for un in (range(6) if "B" in L0 else ()):
                    slot = next_unit(d_win[un])
                    isk = un >= 4
                    gcol = 2 if isk else 0
                    for g in range(NTG):
                        s = it % 2
                        it += 1
                        bA, bB, bC = s * 3, s * 3 + 1, s * 3 + 2
                        tok = slice(g * TGS, (g + 1) * TGS)

                        def fn(e, slot=slot, g=g, bA=bA):
                            ins = None
                            for k in range(KC):
                                ins = e.matmul(bank(bA), lhsT=wu[slot][:, k, :], rhs=hrhs(k, g), start=(k == 0), stop=(k == KC - 1))
                            return ins
                        P.add("pe", fn, reads=[("wu", slot)] + hkeys(g), writes=psk(bA))
                        for cc_ in range(3):
                            m1_mm(3 * (it - 1) + cc_)
                            m1_load(3 * (it - 1) + cc_ + 3)
                        P.add("act", lambda e, s=s, bA=bA: e.activation(out=qsq[s][:, :], in_=bank(bA), func=AF.Square),
                              reads=psk(bA), writes=[("qsq", s)])
                        P.add("act", lambda e, s=s, bA=bA: e.activation(out=qbf[s][:, :], in_=bank(bA), func=AF.Copy),
                              reads=psk(bA), writes=[("qbf", s)])
                        P.add("pe", lambda e, s=s, bB=bB: e.matmul(bank(bB), lhsT=bonesr[:, :], rhs=qsq[s][:, :], start=True, stop=True),
                              reads=["bonesr", ("qsq", s)], writes=psk(bB))
                        P.add("pe", lambda e, s=s, bC=bC: e.matmul(bank(bC), lhsT=swapT[:, :], rhs=qbf[s][:, :], start=True, stop=True),
                              reads=["swapT", ("qbf", s)], writes=psk(bC))
                        P.add("act", lambda e, s=s, bB=bB: e.activation(out=sd[s][:, :], in_=bank(bB), func=AF.Ln, bias=EPS, scale=1.0 / 64.0),
                              reads=psk(bB), writes=[("sd", s)])
                        P.add("act", lambda e, s=s: e.activation(out=rstd[s][:, :], in_=sd[s][:, :], func=AF.Exp, scale=-0.5), reads=[("sd", s)], writes=[("rstd", s)])
                        P.add("dve", lambda e, s=s, bA=bA, tok=tok, gcol=gcol: e.scalar_tensor_tensor(
                            out=t1[s][:, :], in0=bank(bA), scalar=gains[:, gcol:gcol + 1], in1=ropeC[:, tok], op0=ALU.mult, op1=ALU.mult),
                            reads=psk(bA) + ["gains", "ropeC"], writes=[("t1", s)])
                        P.add("dve", lambda e, s=s, bC=bC, tok=tok, gcol=gcol: e.scalar_tensor_tensor(
                            out=t2[s][:, :], in0=bank(bC), scalar=gains[:, gcol + 1:gcol + 2], in1=ropeS[:, tok], op0=ALU.mult, op1=ALU.mult),
                            reads=psk(bC) + ["gains", "ropeS"], writes=[("t2", s)])
                        P.add("pool", lambda e, s=s: e.tensor_tensor(out=t1[s][:, :], in0=t1[s][:, :], in1=t2[s][:, :], op=ALU.add),
                              reads=[("t1", s), ("t2", s)], writes=[("t1", s)])
                        if not isk:
                            P.add("pool", lambda e, s=s, un=un, tok=tok: e.tensor_tensor(out=qT[:, un, tok], in0=t1[s][:, :], in1=rstd[s][:, :], op=ALU.mult),
                                  reads=[("t1", s), ("rstd", s)], writes=[("qT", un, g)])
                        else:
                            kv = un - 4
                            P.add("pool", lambda e, s=s: e.tensor_tensor(out=t2[s][:, :], in0=t1[s][:, :], in1=rstd[s][:, :], op=ALU.mult),
                                  reads=[("t1", s), ("rstd", s)], writes=[("t2", s)])
                            P.add("pool", lambda e, s=s, kv=kv, tok=tok: e.tensor_copy(out=kTz[0:64, kv, 0, tok], in_=t2[s][0:64, :]),
                                  reads=[("t2", s), "kTz0"], writes=[("kTd", kv, g, 0)])
                            P.add("pool", lambda e, s=s, kv=kv, tok=tok: e.tensor_copy(out=kTz[64:128, kv, 1, tok], in_=t2[s][64:128, :]),
                                  reads=[("t2", s), "kTz0"], writes=[("kTd", kv, g, 1)])
                            P.add("sp", lambda e, s=s, kv=kv, tok=tok: e.dma_start(out=d["o_kT"][kv, :, tok], in_=t2[s][0:64, :]),
                                  reads=[("t2", s)], writes=[("o_kT", kv, g)], dma=True)
                P.add("dve", lambda e: e.tensor_tensor(out=m1["mod"][:, 0, 3:6].rearrange("p v c -> p (v c)"), in0=PS[:, 6 * 512:6 * 512 + 24],
                                                       in1=m1["bmod"][:, 24:48], op=ALU.add), reads=[("ps", 6), "bmod"], writes=["mod0b"])
                P.add("dve", lambda e: e.tensor_tensor(out=m1["mod"][:, 1].rearrange("p v c -> p (v c)"), in0=PS[:, 6 * 512 + 24:6 * 512 + 72],
                                                       in1=m1["bmod"][:, 48:96], op=ALU.add), reads=[("ps", 6), "bmod"], writes=["mod1a", "mod1b"])
                m1["derive"](0, "b")
                m1["derive"](1, "a")
                m1["derive"](1, "b")
                m1["derive_cross"]()
                P.barrier()
            pvv = pb
            VVd = sb(pvv, "VVd", [128, 32, 128], BF16)
            VVo = sb(pvv, "VVo", [128, 40, 128], BF16)
            with ExitStack() as pq:
                vf = [sb(pq, "vf%d" % i, [128, 4, 128], F32) for i in range(2)]
                vcf = sb(pq, "vcf", [128, 4, 128], F32)
                KCX = os.environ.get("K_C", "kvmpados")
                for kv in (range(2) if "k" in KCX else ()):
                    P.add("pool", lambda e, kv=kv: e.dma_start(out=kTz[0:64, kv, 0, T:T + PAST], in_=d["kcT"][kv, 0:64, :]), writes=[("kTd", kv, "ctx", 0)], dma=True)
                    P.add("pool", lambda e, kv=kv: e.dma_start(out=kTz[64:128, kv, 1, T:T + PAST], in_=d["kcT"][kv, 64:128, :]), writes=[("kTd", kv, "ctx", 1)], dma=True)
                if "v" in KCX:
                    P.add("sp", lambda e: e.dma_start(out=vcf[:], in_=d["vc"].rearrange("(t p) c -> p t c", p=128)), writes=["vcf"], dma=True)
                P.add("pool", lambda e: e.memset(VVd[:, :, 64:128], 1.0), writes=["VVd1"])
                P.add("pool", lambda e: e.memset(VVo[:, :, 64:128], 1.0), writes=["VVo1"])
                P.add("dve", lambda e: e.tensor_scalar(out=VVo[:, :, 64:128], in0=VVo[:, :, 64:128], scalar1=om, scalar2=None, op0=ALU.mult),
                      reads=["VVo1", "flags"], writes=["VVo1"])
                P.add("dve", lambda e: e.tensor_scalar(out=VVo[:, 32:40, 0:64], in0=vcf[:].rearrange("p t (g c) -> p (t g) c", g=2),
                                                        scalar1=om, scalar2=None, op0=ALU.mult),
                      reads=["vcf", "flags"], writes=[("VVo", 4)])
                slot = next_unit(d_win[6])
                for t4 in (range(4) if ("C" in L0 and "p" in KCX) else ()):
                    b = 6 + t4 % 2
                    s = t4 % 2

                    def fn(e, slot=slot, t4=t4, b=b):
                        ins = None
                        for i in range(4):
                            tt = t4 * 4 + i
                            for k in range(KC):
                                ins = e.matmul(bank(b)[:, i * 128:(i + 1) * 128], lhsT=hT[:, k, tt * 128:(tt + 1) * 128], rhs=wu[slot][:, k, :],
                                               start=(k == 0), stop=(k == KC - 1))
                        return ins
                    P.add("pe", fn, reads=[("wu", slot)] + hkeys(t4), writes=psk(b))
                    if "a" in KCX:
                        P.add("act", lambda e, s=s, b=b: e.activation(out=vf[s][:].rearrange("p t c -> p (t c)"), in_=bank(b), func=AF.Copy),
                              reads=psk(b), writes=[("vf", s)])
                    if "d" not in KCX:
                        continue
                    P.add("dve", lambda e, t4=t4, b=b: e.tensor_copy(out=VVd[:, t4 * 8:t4 * 8 + 8, 0:64], in_=bank(b).rearrange("p (s c) -> p s c", c=64)),
                          reads=psk(b), writes=[("VVd", t4)])
                    if "o" not in KCX:
                        continue
                    P.add("dve", lambda e, s=s, t4=t4: e.tensor_scalar(out=VVo[:, t4 * 8:t4 * 8 + 8, 0:64],
                                                                         in0=vf[s][:].rearrange("p t (g c) -> p (t g) c", g=2),
                                                                         scalar1=om, scalar2=None, op0=ALU.mult),
                          reads=[("vf", s), "flags"], writes=[("VVo", t4)])
                    if "s" not in KCX:
                        continue
                    P.add("sp", lambda e, s=s, t4=t4: e.dma_start(out=d["o_v"].rearrange("(t p) c -> p t c", p=128)[:, t4 * 4:(t4 + 1) * 4, :], in_=vf[s][:]),
                          reads=[("vf", s)], writes=[("o_v", t4)], dma=True)
                P.barrier()
            with ExitStack() as pd:
                attT = sb(pd, "attT", [128, 4, T], BF16)
                PT = [sb(pd, "PT%d" % i, [128, 2 * TGS], BF16) for i in range(2)]
                rden = [sb(pd, "rden%d" % i, [128, SEG], F32) for i in range(4)]
                its = [(h, g4, kp) for h in (range(8) if "D" in L0 else ()) for g4 in range(NTG) for kp in range(NKC // 2)]
                LA = 1

                def emit_S(n):
                    h, g4, kp = its[n]
                    cq, kv = h // 2, h // 4
                    m = n % 2
                    for j in range(2):
                        kc = kp * 2 + j
                        sbk = 2 * m + j
                        P.add("pe", lambda e, kc=kc, sbk=sbk: e.matmul(bank(sbk), lhsT=kTz[:, kv, h % 2, kc * 128:(kc + 1) * 128],
                                                                       rhs=qT[:, cq, g4 * TGS:(g4 + 1) * TGS], start=True, stop=True),
                              reads=[("kTd", kv, kc // 4 if kc < 16 else "ctx", h % 2), ("qT", cq, g4), "kTz0"], writes=psk(sbk))

                def emit_rest(n):
                    h, g4, kp = its[n]
                    cq, kv = h // 2, h // 4
                    odd = (h % 2 == 1)
                    m = n % 2
                    par = (h * NTG + g4) % 2
                    accb = [4 + par * 2, 5 + par * 2]
                    P.add("act", lambda e: e.activation(out=PT[m][:, :], in_=bank(2 * m, 2), func=AF.Exp, scale=0.125),
                          reads=psk(2 * m, 2), writes=[("PT", m)])
                    for j in range(2):
                        kc = kp * 2 + j
                        for s_ in range(2):
                            seg = g4 * 2 + s_
                            diag = kc < 16 and (kc // 2) == seg
                            lhs = (VVd if diag else VVo)[:, kc * 2 + kv, :]
                            P.add("pe", lambda e, s_=s_, lhs=lhs, j=j, kc=kc: e.matmul(bank(accb[s_])[:, 0:SEG], lhsT=lhs,
                                                                                     rhs=PT[m][:, j * TGS + s_ * SEG:j * TGS + (s_ + 1) * SEG],
                                                                                     start=(kc == 0), stop=(kc == NKC - 1)),
                                  reads=[("PT", m), ("VVd", kc // 4) if diag else ("VVo", kc // 4), "VVd1", "VVo1"], writes=psk(accb[s_]))
                    if kp == NKC // 2 - 1:
                        for s_ in range(2):
                            seg = g4 * 2 + s_
                            rb = par * 2 + s_
                            nr, dr = slice(0, 64), slice(64, 128)
                            orow = slice(64, 128) if odd else slice(0, 64)
                            tok = slice(seg * SEG, (seg + 1) * SEG)
                            P.add("dve", lambda e, s_=s_, rb=rb: e.reciprocal(out=rden[rb][dr, :], in_=bank(accb[s_])[dr, 0:SEG]),
                                  reads=psk(accb[s_]), writes=[("rden", rb)])
                            P.add("dve", lambda e, s_=s_, rb=rb, orow=orow, tok=tok: e.tensor_tensor(
                                out=attT[orow, cq, tok], in0=bank(accb[s_])[nr, 0:SEG], in1=rden[rb][dr, :], op=ALU.mult),
                                reads=psk(accb[s_]) + [("rden", rb)], writes=[("attT", cq, seg, h % 2)])

                for n in range(len(its) + LA):
                    if n < len(its):
                        emit_S(n)
                    if n - LA >= 0:
                        emit_rest(n - LA)
                P.barrier()
                for j in (range(KC) if "E" in L0 else ()):
                    slot = next_unit(d["wout"][j])
                    half = j % 2

                    def rhs_of(k, g):
                        if k < 4:
                            return attT[:, k, g * TGS:(g + 1) * TGS]
                        return scT[:, k - 4, g * TGS:(g + 1) * TGS]

                    def rk(g):
                        r = [("scT", ch) for ch in range(4)]
                        for cq in range(4):
                            for seg in (2 * g, 2 * g + 1):
                                r += [("attT", cq, seg, 0), ("attT", cq, seg, 1)]
                        return r
                    proj_fm(c, wu[slot], ("wu", slot), half, rhs_of, rk)
                    P.add("dve", lambda e, j=j, half=half: e.scalar_tensor_tensor(out=xT[:, j, :], in0=bank(half * 4, 4), scalar=mv[:, 0, 2, j:j + 1],
                                                                                 in1=xT[:, j, :], op0=ALU.mult, op1=ALU.add),
                          reads=psk(half * 4, 4) + MVK + xk(j), writes=xk(j))
                P.barrier()


def layernorm(c, l, which, final=False):
    P, bank, psk = c["P"], c["bank"], c["psk"]
    xT, hT, mv, lnp, sb, d = c["xT"], c["hT"], c["mv"], c["lnp"], c["sb"], c["d"]
    onesr, MVK = c["onesr"], c["MVK"]
    gi, bi = (0, 1) if which == 1 else (2, 3)
    ghi, bhi = (6, 7) if which == 1 else (8, 9)
    with ExitStack() as ph:
        xr = [sb(ph, "xr%d" % i, [128, KC, TGS], F32R) for i in range(2)]
        xs = [sb(ph, "xs%d" % i, [128, KC, TGS], F32R) for i in range(2)]
        mean = [sb(ph, "mean%d" % i, [128, TGS], F32) for i in range(2)]
        msq = [sb(ph, "msq%d" % i, [128, TGS], F32) for i in range(2)]
        var = [sb(ph, "var%d" % i, [128, TGS], F32) for i in range(2)]
        rstd = [sb(ph, "lrstd%d" % i, [128, TGS], F32) for i in range(2)]
        nmr = [sb(ph, "nmr%d" % i, [128, TGS], F32) for i in range(2)]
        tt = [sb(ph, "ltt%d" % i, [128, TGS], F32) for i in range(4)]
        itc = [0]

        def stats(g):
            s = g % 2
            tok = slice(g * TGS, (g + 1) * TGS)
            bA, bB = 2 * s, 2 * s + 1
            for j in range(KC):
                P.add("dve", lambda e, j=j: e.tensor_copy(out=xr[s][:, j, :], in_=xT[:, j, tok]),
                      reads=[("xT", j, g)], writes=[("xr", s, j)])
                P.add("act", lambda e, j=j: e.activation(out=xs[s][:, j, :], in_=xT[:, j, tok], func=AF.Square),
                      reads=[("xT", j, g)], writes=[("xs", s, j)])

            def fnA(e):
                ins = None
                for j in range(KC):
                    ins = e.matmul(bank(bA), lhsT=onesr[:, :], rhs=xr[s][:, j, :], start=(j == 0), stop=(j == KC - 1))
                return ins

            def fnB(e):
                ins = None
                for j in range(KC):
                    ins = e.matmul(bank(bB), lhsT=onesr[:, :], rhs=xs[s][:, j, :], start=(j == 0), stop=(j == KC - 1))
                return ins
            P.add("pe", fnA, reads=["onesr"] + [("xr", s, j) for j in range(KC)], writes=psk(bA))
            P.add("pe", fnB, reads=["onesr"] + [("xs", s, j) for j in range(KC)], writes=psk(bB))
            P.add("act", lambda e: e.activation(out=mean[s][:, :], in_=bank(bA), func=AF.Copy, scale=1.0 / D),
                  reads=psk(bA), writes=[("mean", s)])
            P.add("pool", lambda e: e.tensor_tensor(out=msq[s][:, :], in0=mean[s][:, :], in1=mean[s][:, :], op=ALU.mult),
                  reads=[("mean", s)], writes=[("msq", s)])
            P.add("dve", lambda e: e.scalar_tensor_tensor(out=var[s][:, :], in0=bank(bB), scalar=1.0 / D, in1=msq[s][:, :],
                                                          op0=ALU.mult, op1=ALU.subtract),
                  reads=psk(bB) + [("msq", s)], writes=[("var", s)])
            P.add("act", lambda e: e.activation(out=var[s][:, :], in_=var[s][:, :], func=AF.Ln, bias=EPS_LN, scale=1.0),
                  reads=[("var", s)], writes=[("var", s)])
            P.add("act", lambda e: e.activation(out=rstd[s][:, :], in_=var[s][:, :], func=AF.Exp, scale=-0.5), reads=[("var", s)], writes=[("lrstd", s)])
            P.add("dve", lambda e: e.scalar_tensor_tensor(out=nmr[s][:, :], in0=mean[s][:, :], scalar=-1.0, in1=rstd[s][:, :],
                                                          op0=ALU.mult, op1=ALU.mult),
                  reads=[("mean", s), ("lrstd", s)], writes=[("nmr", s)])

        def norm(g):
            s = g % 2
            tok = slice(g * TGS, (g + 1) * TGS)
            for j in range(KC):
                r = itc[0] % 4
                itc[0] += 1
                P.add("dve", lambda e, r=r, j=j: e.tensor_tensor(out=tt[r][:, :], in0=xT[:, j, tok], in1=rstd[s][:, :], op=ALU.mult),
                      reads=[("xT", j, g), ("lrstd", s)], writes=[("ltt", r)])
                P.add("dve", lambda e, r=r: e.tensor_tensor(out=tt[r][:, :], in0=tt[r][:, :], in1=nmr[s][:, :], op=ALU.add),
                      reads=[("ltt", r), ("nmr", s)], writes=[("ltt", r)])
                P.add("act", lambda e, r=r, j=j: e.activation(out=xT[:, j, tok], in_=tt[r][:, :], func=AF.Identity,
                                                              bias=lnp[:, l, bi, j:j + 1], scale=lnp[:, l, gi, j:j + 1]),
                      reads=[("ltt", r), "lnp"], writes=[("xT", j, g)])
                if final:
                    P.add("sp", lambda e, j=j: e.dma_start(out=d["o_yT"][j * 128:(j + 1) * 128, tok], in_=xT[:, j, tok]),
                          reads=[("xT", j, g)], writes=[("o_yT", j, g)], dma=True)
                else:
                    P.add("pool", lambda e, r=r, j=j: e.tensor_scalar(out=hT[:, j, tok], in0=tt[r][:, :], scalar1=mv[:, l, ghi, j:j + 1],
                                                                       scalar2=mv[:, l, bhi, j:j + 1], op0=ALU.mult, op1=ALU.add),
                          reads=[("ltt", r)] + MVK, writes=[("hT", j, g)])

        stats(0)
        stats(1)
        norm(0)
        stats(2)
        norm(1)
        stats(3)
        norm(2)
        norm(3)
        P.barrier()


def ffn(c, l):
    P, bank, psk = c["P"], c["bank"], c["psk"]
    xT, hT, mv, sb, d = c["xT"], c["hT"], c["mv"], c["sb"], c["d"]
    xk, MVK, nbm = c["xk"], c["MVK"], c["nbm"]
    HC = NFC // 2

    def hrhs(k, g):
        return hT[:, k, g * TGS:(g + 1) * TGS]

    def hkeys(g):
        return [("hT", k, g) for k in range(KC)]

    with ExitStack() as ph:
        act = sb(ph, "ffact", [128, HC, T], BF16)
        uu = [sb(ph, "ffu%d" % i, [128, T], F32) for i in range(2)]
        sg = sb(ph, "ffsg", [128, T], F32)
        tl = [sb(ph, "fftl%d" % i, [128, T], F32) for i in range(2)]
        wu = [sb(ph, "fwu%d" % i, [128, KC, 128], BF16) for i in range(6)]
        wd = [sb(ph, "fwd%d" % i, [128, HC, 128], BF16) for i in range(3)]
        srcs_up = [d["wup"][l, pas * HC + cc + (NFC if isg else 0)] for pas in range(2) for cc in range(HC) for isg in range(2)]
        srcs_dn = [d["wdn"][l, j, :, pas * HC:(pas + 1) * HC, :] for pas in range(2) for j in range(KC)]
        wsU = WStream(c, wu, "fwu", srcs_up, 5)
        wsD = WStream(c, wd, "fwd", srcs_dn, 1)
        fcw = sb(ph, "fcw_sb", [128, 44, 3], F32)
        nbw = sb(ph, "fnbw", [128, 44, 2], F32)
        P.add("sp", lambda e: e.dma_start(out=fcw[:], in_=d["fcw"][l]), writes=["fcw"], dma=True)
        P.add("dve", lambda e: e.tensor_scalar(out=nbw[:, :, 0], in0=fcw[:, :, 0], scalar1=nbm, scalar2=None, op0=ALU.mult),
              reads=["fcw", "flags"], writes=["fnbw"])
        P.add("dve", lambda e: e.tensor_scalar(out=nbw[:, :, 1], in0=fcw[:, :, 2], scalar1=nbm, scalar2=None, op0=ALU.mult),
              reads=["fcw", "flags", "fnbw"], writes=["fnbw"])
        ucnt = 0
        dcnt = 0
        for pas in range(2):
            for cc in range(HC):
                for isg in range(2):
                    ch = pas * HC + cc + (NFC if isg else 0)
                    wsU.get(ucnt)
                    slot = ucnt % 6
                    ucnt += 1
                    if cc == HC - 2 and isg:
                        wsD.prefetch(dcnt + 1)
                    half = isg
                    proj_fm(c, wu[slot], ("fwu", slot), half, hrhs, hkeys)
                    u = uu[isg]
                    uk = ("ffu", isg)
                    y = bank(half * 4, 4)
                    yk = psk(half * 4, 4)
                    ysb = tl[isg]
                    ysk = ("fftl", isg)
                    P.add("act", lambda e, ysb=ysb, y=y: e.activation(out=ysb[:, :], in_=y, func=AF.Copy), reads=yk, writes=[ysk])
                    P.add("act", lambda e, u=u, ysb=ysb, ch=ch: e.activation(out=u[:, :], in_=ysb[:, :], func=AF.Copy, scale=fcw[:, ch, 1:2]),
                          reads=[ysk, "fcw"], writes=[uk])
                    P.add("dve", lambda e, u=u, ysb=ysb, ch=ch: e.scalar_tensor_tensor(out=u[:, 1:T], in0=ysb[:, 0:T - 1], scalar=fcw[:, ch, 0:1], in1=u[:, 1:T],
                                                                                       op0=ALU.mult, op1=ALU.add), reads=[ysk, "fcw", uk], writes=[uk])
                    P.add("dve", lambda e, u=u, ysb=ysb, ch=ch: e.scalar_tensor_tensor(out=u[:, 0:T - 1], in0=ysb[:, 1:T], scalar=fcw[:, ch, 2:3], in1=u[:, 0:T - 1],
                                                                                       op0=ALU.mult, op1=ALU.add), reads=[ysk, "fcw", uk], writes=[uk])
                    P.add("dve", lambda e, u=u, ysb=ysb, ch=ch: e.scalar_tensor_tensor(out=u[:, SEG:T:SEG], in0=ysb[:, SEG - 1:T - 1:SEG], scalar=nbw[:, ch, 0:1],
                                                                                       in1=u[:, SEG:T:SEG], op0=ALU.mult, op1=ALU.add),
                          reads=[ysk, "fnbw", uk], writes=[uk])
                    P.add("dve", lambda e, u=u, ysb=ysb, ch=ch: e.scalar_tensor_tensor(out=u[:, SEG - 1:T - 1:SEG], in0=ysb[:, SEG:T:SEG], scalar=nbw[:, ch, 1:2],
                                                                                       in1=u[:, SEG - 1:T - 1:SEG], op0=ALU.mult, op1=ALU.add),
                          reads=[ysk, "fnbw", uk], writes=[uk])
                    if isg:
                        P.add("act", lambda e: e.activation(out=sg[:, :], in_=uu[1][:, :], func=AF.Silu), reads=[("ffu", 1)], writes=["ffsg"])
                        P.add("dve", lambda e, cc=cc: e.tensor_tensor(out=act[:, cc, :], in0=sg[:, :], in1=uu[0][:, :], op=ALU.mult),
                              reads=["ffsg", ("ffu", 0)], writes=[("ffact", cc)])
            for j in range(KC):
                wsD.get(dcnt)
                slot = dcnt % 3
                dcnt += 1
                half = j % 2
                proj_fm(c, wd[slot], ("fwd", slot), half, lambda k, g: act[:, k, g * TGS:(g + 1) * TGS],
                        lambda g: [("ffact", k) for k in range(HC)], nk=HC)
                P.add("dve", lambda e, j=j, half=half: e.scalar_tensor_tensor(out=xT[:, j, :], in0=bank(half * 4, 4), scalar=mv[:, l, 5, j:j + 1],
                                                                             in1=xT[:, j, :], op0=ALU.mult, op1=ALU.add),
                      reads=psk(half * 4, 4) + MVK + xk(j), writes=xk(j))
        P.barrier()


def hgrn_mixer(c):
    P, nc, PS, bank, psk = c["P"], c["nc"], c["PS"], c["bank"], c["psk"]
    xT, hT, mv, sb, d = c["xT"], c["hT"], c["mv"], c["sb"], c["d"]
    xk, MVK = c["xk"], c["MVK"]
    keep, onesr, identb = c["keep"], c["onesr"], c["identb"]
    PSb = PS.bitcast(BF16)
    HL = 512
    NP = T // HL
    NR = 4

    def hrhs(k, g):
        return hT[:, k, g * TGS:(g + 1) * TGS]

    def hkeys(g):
        return [("hT", k, g) for k in range(KC)]

    with ExitStack() as ph:
        yT = sb(ph, "yT", [128, 4, T], BF16)
        wu = [sb(ph, "hwu%d" % i, [128, KC, 128], BF16) for i in range(4)]
        hmask = sb(ph, "hmask_sb", [128, 2, 512], BF16)
        trim = sb(ph, "trim", [128, 2, 128], BF16)
        lbl = sb(ph, "lbl_sb", [128, 2, 2, 8], F32)
        lb = sb(ph, "lb", [128, 2, 8], F32)
        hng = sb(ph, "hng_sb", [128, 1], F32)
        P.add("pool", lambda e: e.dma_start(out=hmask[:], in_=d["hmask"][:]), writes=["hmask"], dma=True)
        P.add("pool", lambda e: e.dma_start(out=trim[:], in_=d["htri"][:]), writes=["trim"], dma=True)
        P.add("sp", lambda e: e.dma_start(out=lbl[:], in_=d["lbl"][:]), writes=["lbl"], dma=True)
        P.add("sp", lambda e: e.dma_start(out=hng[:, :], in_=d["hng"][:, :]), writes=["hng"], dma=True)
        P.add("dve", lambda e: e.tensor_tensor(out=lb[:, :, :], in0=lbl[:, 0, :, :], in1=lbl[:, 1, :, :], op=ALU.subtract), reads=["lbl"], writes=["lb"])
        P.add("act", lambda e: e.activation(out=lb[:].rearrange("p a b -> p (a b)"), in_=lb[:].rearrange("p a b -> p (a b)"), func=AF.Exp),
              reads=["lb"], writes=["lb"])
        P.add("dve", lambda e: e.tensor_scalar(out=lb[:, :, :], in0=lb[:, :, :], scalar1=1.0, scalar2=None, op0=ALU.add), reads=["lb"], writes=["lb"])
        P.add("dve", lambda e: e.reciprocal(out=lb[:].rearrange("p a b -> p (a b)"), in_=lb[:].rearrange("p a b -> p (a b)")), reads=["lb"], writes=["lb"])
        srcsH = []
        for h in range(8):
            srcsH += [d["hin"][0 * 8 + h], d["hin"][1 * 8 + h], d["hin"][3 * 8 + h], d["hin"][4 * 8 + h], d["hin"][2 * 8 + h]]
            if h % 4 == 3:
                srcsH += [(d["hout"][j][:, 4 * (h // 4):4 * (h // 4) + 4, :], 4) for j in range(KC)]
        wsH = WStream(c, wu, "hwu", srcsH, 3)
        ucount = [0]

        def next_unit(src):
            i = ucount[0]
            ucount[0] += 1
            wsH.get(i)
            return i % 4

        with ExitStack() as pp:
            qf = sb(pp, "qf", [128, T], BF16)
            gA = [sb(pp, "gA%d" % i, [128, HL], F32) for i in range(NP)]
            gB = [sb(pp, "gB%d" % i, [128, HL], F32) for i in range(NP)]
            gC = [sb(pp, "gC%d" % i, [128, HL], F32) for i in range(NP)]
            sqr = [sb(pp, "gsq%d" % i, [128, HL], F32R) for i in range(2)]
            qtL = [sb(pp, "qt%d" % i, [128, T], BF16) for i in range(2)]
            ktL = [sb(pp, "kt%d" % i, [128, T], BF16) for i in range(2)]
            khfm = sb(pp, "khfm", [128, T], BF16)
            khTL = [sb(pp, "khT%d" % i, [128, 16, 128], BF16) for i in range(2)]
            vtm = sb(pp, "vtm", [128, 16, 128], BF16)
            of = sb(pp, "of", [128, T], F32)
            decL = [sb(pp, "dec%d" % i, [128, NCH], F32) for i in range(2)]
            Sf = [sb(pp, "Sf%d" % i, [128, 128], F32) for i in range(NR)]
            Sb = [sb(pp, "Sb%d" % i, [128, 128], BF16) for i in range(NR)]
            Am = [sb(pp, "Am%d" % i, [128, 128], BF16) for i in range(2)]
            HST = os.environ.get("K_H", "vgtrn")
            for h in range(int(os.environ.get("K_NH", "8"))):
                s_q = next_unit(d["hin"][0 * 8 + h])
                proj_fm(c, wu[s_q], ("hwu", s_q), 0, hrhs, hkeys)
                s_i = next_unit(d["hin"][1 * 8 + h])
                P.add("act", lambda e: e.activation(out=qf[:, :], in_=bank(0, 4), func=AF.Copy), reads=psk(0, 4), writes=["qf"])
                for t4 in (range(4) if "v" in HST else ()):
                    b = 4 + t4 % 2

                    def fnv(e, t4=t4, b=b, s_i=s_i):
                        ins = None
                        for i in range(4):
                            tt = t4 * 4 + i
                            for k in range(KC):
                                ins = e.matmul(bank(b)[:, i * 128:(i + 1) * 128], lhsT=hT[:, k, tt * 128:(tt + 1) * 128], rhs=wu[s_i][:, k, :],
                                               start=(k == 0), stop=(k == KC - 1))
                        return ins
                    P.add("pe", fnv, reads=[("hwu", s_i)] + hkeys(t4), writes=psk(b))
                    P.add("act" if t4 % 2 == 0 else "dve", (lambda e, t4=t4, b=b: e.activation(out=vtm[:, t4 * 4:(t4 + 1) * 4, :].rearrange("p t c -> p (t c)"), in_=bank(b), func=AF.Copy))
                          if t4 % 2 == 0 else (lambda e, t4=t4, b=b: e.tensor_copy(out=vtm[:, t4 * 4:(t4 + 1) * 4, :].rearrange("p t c -> p (t c)"), in_=bank(b))),
                          reads=psk(b), writes=[("vtm", t4)])
                def gates_for(dd, h=h):
                    qt, kt, dec = qtL[dd], ktL[dd], decL[dd]
                    s_f = next_unit(d["hin"][(3 + dd) * 8 + h])
                    proj_fm(c, wu[s_f], ("hwu", s_f), 0, hrhs, hkeys)
                    lbh = lb[:, dd, h:h + 1]
                    lastoff = CHUNK - 1 if dd == 0 else 0
                    NB = HL // CHUNK

                    def g1(hf):
                        P.add("act", lambda e: e.activation(out=gA[hf][:, :], in_=bank(hf * (HL // 512), HL // 512), func=AF.Exp, scale=-1.0), reads=psk(hf * (HL // 512), HL // 512), writes=[("gA", hf)])

                    def g2(hf):
                        P.add("act", lambda e: e.activation(out=gB[hf][:, :], in_=gA[hf][:, :], func=AF.Ln, bias=1.0, scale=lbh), reads=[("gA", hf), "lb"], writes=[("gB", hf)])

                    def g3(hf):
                        P.add("act", lambda e: e.activation(out=gA[hf][:, :], in_=gA[hf][:, :], func=AF.Ln, bias=1.0, scale=1.0), reads=[("gA", hf)], writes=[("gA", hf)])

                    def g4_(hf):
                        P.add("dve", lambda e: e.tensor_tensor(out=gB[hf][:, :], in0=gB[hf][:, :], in1=gA[hf][:, :], op=ALU.subtract),
                              reads=[("gA", hf), ("gB", hf)], writes=[("gB", hf)])

                    def g5(hf):
                        P.add("act", lambda e: e.activation(out=gA[hf][:, :], in_=gB[hf][:, :], func=AF.Exp), reads=[("gB", hf)], writes=[("gA", hf)])

                    def g6(hf):
                        P.add("dve", lambda e: e.tensor_scalar(out=gA[hf][:, :], in0=gA[hf][:, :], scalar1=-1.0, scalar2=1.0, op0=ALU.mult, op1=ALU.add),
                              reads=[("gA", hf)], writes=[("gA", hf)])

                    def g7(hf):
                        for blk in range(HL // 512):
                            bs = slice(blk * 512, (blk + 1) * 512)
                            if dd == 0:
                                P.add("dve", lambda e, bs=bs: e.tensor_tensor_scan(out=gC[hf][:, bs], data0=hmask[:, 0, :], data1=gB[hf][:, bs], initial=0.0,
                                                                                    op0=ALU.mult, op1=ALU.add), reads=[("gB", hf), "hmask"], writes=[("gC", hf, blk)])
                            else:
                                P.add("dve", lambda e, bs=bs: e.tensor_tensor_scan(out=gC[hf][:, bs][:, ::-1], data0=hmask[:, 1, ::-1], data1=gB[hf][:, bs][:, ::-1],
                                                                                    initial=0.0, op0=ALU.mult, op1=ALU.add),
                                      reads=[("gB", hf), "hmask"], writes=[("gC", hf, blk)])

                    def gck(hf):
                        return [("gC", hf, blk) for blk in range(HL // 512)]

                    def g8(hf):
                        P.add("act", lambda e: e.activation(out=gB[hf][:, :], in_=gC[hf][:, :], func=AF.Exp), reads=gck(hf) + [("gB", hf)], writes=[("gB", hf)])

                    def g9(hf):
                        P.add("act", lambda e: e.activation(out=gC[hf][:, :], in_=gC[hf][:, :], func=AF.Exp, scale=-1.0), reads=gck(hf), writes=gck(hf))

                    def g10(hf):
                        tok = slice(hf * HL, (hf + 1) * HL)
                        P.add("dve", lambda e: e.tensor_tensor(out=qt[:, tok], in0=qf[:, tok], in1=gB[hf][:, :], op=ALU.mult),
                              reads=["qf", ("gB", hf)], writes=[("qt", dd, hf)])

                    def g11(hf):
                        P.add("dve", lambda e: e.tensor_tensor(out=gA[hf][:, :], in0=gA[hf][:, :], in1=gC[hf][:, :], op=ALU.mult),
                              reads=[("gA", hf)] + gck(hf), writes=[("gA", hf)])

                    def g12(hf):
                        tok = slice(hf * HL, (hf + 1) * HL)
                        P.add("pool", lambda e: e.tensor_copy(out=kt[:, tok], in_=gA[hf][:, :]), reads=[("gA", hf)], writes=[("kt", dd, hf)])

                    def g13(hf):
                        tok = slice(hf * HL, (hf + 1) * HL)
                        ebl = gB[hf][:, lastoff::CHUNK]
                        P.add("dve", lambda e: e.tensor_tensor(out=khfm[:, tok].rearrange("p (c l) -> p c l", l=CHUNK),
                                                               in0=gA[hf][:, :].rearrange("p (c l) -> p c l", l=CHUNK),
                                                               in1=ebl.unsqueeze(2).to_broadcast([128, NB, CHUNK]), op=ALU.mult),
                              reads=[("gA", hf), ("gB", hf)], writes=[("khfm", hf)])

                    def g14(hf):
                        ebl = gB[hf][:, lastoff::CHUNK]
                        P.add("pool", lambda e: e.tensor_copy(out=dec[:, hf * NB:(hf + 1) * NB], in_=ebl), reads=[("gB", hf)], writes=[("dec", dd, hf)])

                    return [g1, g2, g3, g4_, g5, g6, g7, g8, g9, g10, g11, g12, g13, g14]

                def transposes_for(dd):
                    khT = khTL[dd]
                    for t8 in (range(2) if "t" in HST else ()):
                        def fnt(e, t8=t8):
                            ins = None
                            for i in range(8):
                                tt = t8 * 8 + i
                                ins = e.transpose(out=PSb[:, (6 + t8) * 1024 + i * 128:(6 + t8) * 1024 + (i + 1) * 128], in_=khfm[:, tt * 128:(tt + 1) * 128],
                                                  identity=identb[:, :])
                            return ins
                        P.add("pe", fnt, reads=[("khfm", hf) for hf in range((t8 * 1024) // HL, ((t8 + 1) * 1024) // HL)] + ["identb"], writes=psk(6 + t8))
                        P.add("act", lambda e, t8=t8: e.activation(out=khT[:, t8 * 8:(t8 + 1) * 8, :].rearrange("p t c -> p (t c)"),
                                                                    in_=PSb[:, (6 + t8) * 1024:(7 + t8) * 1024], func=AF.Copy),
                              reads=psk(6 + t8), writes=[("khT", dd, t8)])

                def recurrence_for(dd, inject, h=h):
                    qt, kt, dec, khT = qtL[dd], ktL[dd], decL[dd], khTL[dd]
                    cur = 0
                    P.add("sp", lambda e, dd=dd, h=h: e.dma_start(out=Sf[0][:, :], in_=d["state"][dd, h]), writes=[("Sf", 0)], dma=True)
                    P.add("act", lambda e: e.activation(out=Sb[0][:, :], in_=Sf[0][:, :], func=AF.Copy), reads=[("Sf", 0)], writes=[("Sb", 0)])
                    tiles = list(range(16)) if dd == 0 else list(range(15, -1, -1))
                    order = list(range(4)) if dd == 0 else list(range(3, -1, -1))

                    def emit_scores(ti, tt):
                        bS = ti % 2
                        tsl = slice(tt * 128, (tt + 1) * 128)
                        P.add("pe", lambda e: e.matmul(bank(bS)[:, 0:128], lhsT=kt[:, tsl], rhs=qt[:, tsl], start=True, stop=True),
                              reads=[("kt", dd, (tt * 128) // HL), ("qt", dd, (tt * 128) // HL)], writes=psk(bS))

                    def emit_U(tt, i):
                        kw = dict(tile_position=(96, 0)) if i == 3 else {}
                        P.add("pe", lambda e: e.matmul(bank(4 + i)[:, 0:128], lhsT=khT[32 * i:32 * i + 32, tt, :],
                                                       rhs=vtm[32 * i:32 * i + 32, tt, :], start=True, stop=True, **kw),
                              reads=[("khT", dd, tt // 8), ("vtm", tt // 4)], writes=psk(4 + i))

                    emit_scores(0, tiles[0])
                    for i in order:
                        emit_U(tiles[0], i)
                    for ti, tt in enumerate(tiles):
                        par = ti % 2
                        bS, bO = par, 2 + par
                        tsl = slice(tt * 128, (tt + 1) * 128)
                        hfk = (tt * 128) // HL
                        nxt_t = tiles[ti + 1] if ti + 1 < len(tiles) else None
                        P.add("dve", lambda e, par=par, bS=bS, dd=dd: e.tensor_tensor(out=Am[par][:, :], in0=bank(bS)[:, 0:128], in1=trim[:, dd, :], op=ALU.mult),
                              reads=psk(bS) + ["trim"], writes=[("Am", par)])
                        P.add("pe", lambda e, par=par, bO=bO, tt=tt: e.matmul(bank(bO)[:, 0:128], lhsT=vtm[:, tt, :], rhs=Am[par][:, :], start=True, stop=False),
                              reads=[("Am", par), ("vtm", tt // 4)], writes=psk(bO))
                        if nxt_t is not None:
                            emit_scores(ti + 1, nxt_t)
                        for oi, i in enumerate(order):
                            cch = tt * 4 + i
                            csl = slice(cch * CHUNK, (cch + 1) * CHUNK)
                            P.add("pe", lambda e, bO=bO, i=i, cur=cur, csl=csl, oi=oi: e.matmul(bank(bO)[:, i * CHUNK:(i + 1) * CHUNK], lhsT=Sb[cur][:, :], rhs=qt[:, csl],
                                                                                                 start=False, stop=(oi == 3)),
                                  reads=[("Sb", cur), ("qt", dd, hfk)], writes=psk(bO))
                            nxt = (cur + 1) % NR
                            P.add("dve", lambda e, nxt=nxt, cur=cur, cch=cch, i=i: e.scalar_tensor_tensor(
                                out=Sf[nxt][:, :], in0=Sf[cur][:, :], scalar=dec[:, cch:cch + 1], in1=bank(4 + i)[:, 0:128], op0=ALU.mult, op1=ALU.add),
                                reads=[("Sf", cur), ("dec", dd, cch // (HL // CHUNK))] + psk(4 + i), writes=[("Sf", nxt)])
                            cur = nxt
                            seg_end = (cch % 8 == 7) if dd == 0 else (cch % 8 == 0)
                            if seg_end:
                                seg = cch // 8
                                P.add("sp", lambda e, cur=cur, seg=seg, dd=dd, h=h: e.dma_start(out=d["o_s"][seg, dd, h], in_=Sf[cur][:, :]),
                                      reads=[("Sf", cur)], writes=[("o_s", seg, dd, h)], dma=True)
                                last = (cch == NCH - 1) if dd == 0 else (cch == 0)
                                if not last:
                                    nxt = (cur + 1) % NR
                                    P.add("dve", lambda e, nxt=nxt, cur=cur: e.tensor_scalar(out=Sf[nxt][:, :], in0=Sf[cur][:, :], scalar1=keep, scalar2=None, op0=ALU.mult),
                                          reads=[("Sf", cur), "flags"], writes=[("Sf", nxt)])
                                    cur = nxt
                            P.add("act", lambda e, cur=cur: e.activation(out=Sb[cur][:, :], in_=Sf[cur][:, :], func=AF.Copy), reads=[("Sf", cur)], writes=[("Sb", cur)])
                            if nxt_t is not None:
                                emit_U(nxt_t, i)
                            if inject is not None:
                                inject()
                        if dd == 0:
                            P.add("act", lambda e, bO=bO, tsl=tsl: e.activation(out=of[:, tsl], in_=bank(bO)[:, 0:128], func=AF.Copy), reads=psk(bO), writes=[("of", tt)])
                        else:
                            P.add("dve", lambda e, bO=bO, tsl=tsl: e.tensor_tensor(out=of[:, tsl], in0=bank(bO)[:, 0:128], in1=of[:, tsl], op=ALU.add),
                                  reads=psk(bO) + [("of", tt)], writes=[("of", tt)])

                st0 = gates_for(0)
                for st in st0:
                    for hf in range(NP):
                        st(hf)
                transposes_for(0)
                st1 = gates_for(1)
                for hf in range(NP):
                    st1[0](hf)
                pend = [(st, hf) for st in st1[1:] for hf in range(NP)]

                def inject():
                    for _ in range(1):
                        if pend:
                            st_, hf_ = pend.pop(0)
                            st_(hf_)
                recurrence_for(0, inject)
                while pend:
                    st_, hf_ = pend.pop(0)
                    st_(hf_)
                transposes_for(1)
                recurrence_for(1, None)
                if "n" not in HST:
                    continue
                s_g = next_unit(d["hin"][2 * 8 + h])
                proj_fm(c, wu[s_g], ("hwu", s_g), 1, hrhs, hkeys)
                TPP = HL // 128

                def n1(p):
                    tok = slice(p * HL, (p + 1) * HL)
                    ofk = [("of", tt) for tt in range(p * TPP, (p + 1) * TPP)]
                    P.add("act", lambda e: e.activation(out=sqr[p % 2][:, :], in_=of[:, tok], func=AF.Square), reads=ofk, writes=[("gsq", p % 2)])

                def n2(p):
                    P.add("pe", lambda e: e.matmul(bank(p), lhsT=onesr[:, :], rhs=sqr[p % 2][:, :], start=True, stop=True),
                          reads=[("gsq", p % 2), "onesr"], writes=psk(p))

                def n3(p):
                    P.add("act", lambda e: e.activation(out=gA[p][:, :], in_=bank(p), func=AF.Ln, bias=EPS, scale=1.0 / 128.0), reads=psk(p), writes=[("gA", p)])

                def n4(p):
                    P.add("act", lambda e: e.activation(out=gA[p][:, :], in_=gA[p][:, :], func=AF.Exp, scale=-0.5), reads=[("gA", p)], writes=[("gA", p)])

                def n5(p):
                    P.add("act", lambda e: e.activation(out=gB[p][:, :], in_=bank(4 + p), func=AF.Silu), reads=psk(4 + p), writes=[("gB", p)])

                def n6(p):
                    tok = slice(p * HL, (p + 1) * HL)
                    ofk = [("of", tt) for tt in range(p * TPP, (p + 1) * TPP)]
                    P.add("dve", lambda e: e.scalar_tensor_tensor(out=gA[p][:, :], in0=of[:, tok], scalar=hng[:, 0:1], in1=gA[p][:, :], op0=ALU.mult, op1=ALU.mult),
                          reads=ofk + [("gA", p), "hng"], writes=[("gA", p)])

                def n7(p, h=h):
                    tok = slice(p * HL, (p + 1) * HL)
                    P.add("pool", lambda e: e.tensor_tensor(out=yT[:, h % 4, tok], in0=gA[p][:, :], in1=gB[p][:, :], op=ALU.mult),
                          reads=[("gA", p), ("gB", p)], writes=[("yT", h % 4, p)])

                for p in range(NP):
                    n1(p)
                    n2(p)
                for st in (n3, n4, n5, n6, n7):
                    for p in range(NP):
                        st(p)
                if h % 4 == 3:
                    for j in range(KC):
                        slot = next_unit(None)
                        half = j % 2
                        proj_fm(c, wu[slot], ("hwu", slot), half, lambda k, g: yT[:, k, g * TGS:(g + 1) * TGS],
                                lambda g: [("yT", k, g) for k in range(4)], nk=4)
                        P.add("dve", lambda e, j=j, half=half: e.scalar_tensor_tensor(out=xT[:, j, :], in0=bank(half * 4, 4), scalar=mv[:, 1, 2, j:j + 1],
                                                                                     in1=xT[:, j, :], op0=ALU.mult, op1=ALU.add),
                              reads=psk(half * 4, 4) + MVK + xk(j), writes=xk(j))
            P.barrier()
        P.barrier()


def _units(W, cols_list):
    K = W.shape[0]
    out = np.empty((len(cols_list), 128, K // 128, 128), dtype=np.float32)
    for i, cols in enumerate(cols_list):
        blk = W[:, cols]
        out[i] = blk.reshape(K // 128, 128, 128).transpose(1, 0, 2)
    return out


def prep_inputs(inp):
    f = lambda a: np.ascontiguousarray(np.asarray(a, dtype=np.float32))
    x_prompt, x_sample = f(inp["x_prompt"]), f(inp["x_sample"])
    cache_k, cache_v, state = f(inp["cache_k"]), f(inp["cache_v"]), f(inp["state_hgrn"])
    cvec, c_ctx = f(inp["c"]), f(inp["c_ctx"])
    shared = {}
    shared["w_mod"] = f(inp["w_mod"])
    bm = f(inp["b_mod"])
    shared["bmod"] = np.ascontiguousarray(np.concatenate([bm[l].reshape(48, 128).T for l in range(2)], axis=1))
    lnp = np.empty((128, 2, 4, KC), np.float32)
    for l in range(2):
        for i, nm in enumerate(("ln1_g", "ln1_b", "ln2_g", "ln2_b")):
            lnp[:, l, i, :] = f(inp[nm])[l].reshape(KC, 128).T
    shared["lnp"] = lnp
    Win = f(inp["attn_w_in"])[0]
    ar = np.arange
    cols = [ar(c * 128, (c + 1) * 128) for c in range(4)]
    cols += [np.concatenate([ar(512, 576), ar(512, 576)]), np.concatenate([ar(576, 640), ar(576, 640)])]
    cols += [ar(640, 768)]
    cols += [ar(768 + c * 128, 768 + (c + 1) * 128) for c in range(12)]
    shared["attn_in_u"] = _units(Win, cols)
    sw = _swap_idx()
    qg, kg = f(inp["attn_q_gain"])[0], f(inp["attn_k_gain"])[0]
    gains = np.stack([np.tile(qg, 2), np.tile(qg[sw], 2), np.tile(kg, 2), np.tile(kg[sw], 2)], axis=1)
    shared["gains"] = np.ascontiguousarray(gains.astype(np.float32))
    swapT = np.zeros((128, 128), np.float32)
    for m in range(128):
        p = (m // 64) * 64 + sw[m % 64]
        swapT[p, m] = 1.0
    bones = np.zeros((128, 128), np.float32)
    bones[:64, :64] = 1.0
    bones[64:, 64:] = 1.0
    shared["consts"] = np.stack([swapT, bones, np.eye(128, dtype=np.float32)])
    scw = f(inp["sconv_w"])[0]
    shared["scw"] = np.ascontiguousarray(scw.reshape(3, 4, 128).transpose(2, 1, 0))
    shared["attn_out_u"] = _units(f(inp["attn_w_out"])[0], [ar(j * 128, (j + 1) * 128) for j in range(8)])
    wup = f(inp["ffn_w_up"])
    shared["ffn_up_u"] = np.stack([_units(wup[l], [ar(c * 128, (c + 1) * 128) for c in range(44)]) for l in range(2)])
    fcw = f(inp["ffn_conv_w"])
    shared["fcw"] = np.ascontiguousarray(fcw.reshape(2, 3, 44, 128).transpose(0, 3, 2, 1))
    wdn = f(inp["ffn_w_down"])
    shared["ffn_dn_u"] = np.stack([_units(wdn[l], [ar(j * 128, (j + 1) * 128) for j in range(8)]) for l in range(2)])
    hin = f(inp["hgrn_w_in"])[0]
    shared["hgrn_in_u"] = _units(hin, [ar(kind * 1024 + h * 128, kind * 1024 + (h + 1) * 128) for kind in range(5) for h in range(8)])
    lbl = f(inp["hgrn_lb_logits"])
    shared["lbl"] = np.ascontiguousarray(lbl.reshape(2, 2, 8, 128).transpose(3, 0, 1, 2))
    shared["hng"] = np.ascontiguousarray(f(inp["hgrn_norm_g"])[0].reshape(128, 1))
    shared["hgrn_out_u"] = _units(f(inp["hgrn_w_out"])[0], [ar(j * 128, (j + 1) * 128) for j in range(8)])
    hm = np.zeros((128, 2, 512), np.float32)
    t = np.arange(512)
    hm[:, 0, :] = (t % CHUNK != 0).astype(np.float32)[None, :]
    hm[:, 1, :] = (t % CHUNK != CHUNK - 1).astype(np.float32)[None, :]
    sidx = np.arange(128)[:, None]
    tidx = np.arange(128)[None, :]
    same = (sidx // CHUNK) == (tidx // CHUNK)
    ht = np.zeros((128, 2, 128), np.float32)
    ht[:, 0, :] = (same & (sidx <= tidx)).astype(np.float32)
    ht[:, 1, :] = (same & (sidx >= tidx)).astype(np.float32)
    shared["hscan"] = hm
    shared["htri"] = ht
    tok = np.arange(T)
    row = (tok // 64).astype(np.float32)
    colp = (tok % 64).astype(np.float32)
    inv = (10000.0 ** (-np.arange(0, 32, 2, dtype=np.float32) / 32.0)).astype(np.float32)
    C = np.zeros((64, T), np.float32)
    S = np.zeros((64, T), np.float32)
    for dd in range(64):
        blk = dd // 16
        pos = row if blk < 2 else colp
        ang = pos * inv[dd % 16]
        C[dd] = np.cos(ang)
        S[dd] = -np.sin(ang) if blk % 2 == 0 else np.sin(ang)
    ropeC_s, ropeS_s = np.tile(C, (2, 1)), np.tile(S, (2, 1))
    ropeC_p, ropeS_p = np.ones((128, T), np.float32), np.zeros((128, T), np.float32)
    maps = []
    for core in range(8):
        m = dict(shared)
        if core < 4:
            b = core
            xt = x_sample[b]
            m["cond"] = np.ascontiguousarray(cvec[b].reshape(KC, 128).T)
            fl = np.zeros((128, 4), np.float32)
            fl[:, 0] = 1.0
            fl[:, 1] = 1.0
            m["ropeC"], m["ropeS"] = ropeC_s, ropeS_s
            kc = cache_k[b, 0]
            m["kcT"] = np.ascontiguousarray(np.stack([np.tile(kc[:, g, :].T, (2, 1)) for g in range(2)]))
            m["vc"] = np.ascontiguousarray(cache_v[b, 0].reshape(PAST, 128))
            m["state"] = np.ascontiguousarray(state[b, 0])
        else:
            p0 = (core - 4) * 8
            xt = x_prompt[p0:p0 + 8].reshape(T, D)
            m["cond"] = np.ascontiguousarray(c_ctx.reshape(KC, 128).T)
            fl = np.zeros((128, 4), np.float32)
            fl[:, 2] = -1.0
            m["ropeC"], m["ropeS"] = ropeC_p, ropeS_p
            m["kcT"] = np.zeros((2, 128, PAST), np.float32)
            m["vc"] = np.zeros((PAST, 128), np.float32)
            m["state"] = np.zeros((2, 8, 128, 128), np.float32)
        m["flags"] = fl
        m["xT"] = np.ascontiguousarray(xt.T)
        maps.append(m)
    return maps


_CACHE = {}


def kernel(**inputs):
    maps = prep_inputs(inputs)
    if "nc" not in _CACHE:
        _CACHE["nc"] = build()[0]
    nc = _CACHE["nc"]
    res = run_bass_kernel_spmd(nc, maps, core_ids=list(range(8)))
    r = res.results
    y_s = np.stack([r[c]["yT"].T for c in range(4)])
    y_p = np.concatenate([r[c]["yT"].T.reshape(8, SEG, D) for c in range(4, 8)])
    nk = np.concatenate([r[c]["koutT"].transpose(2, 0, 1).reshape(8, 1, SEG, 2, 64) for c in range(4, 8)])
    nv = np.concatenate([r[c]["vout"].reshape(8, 1, SEG, 2, 64) for c in range(4, 8)])
    ns = np.concatenate([r[c]["sout"].reshape(8, 1, 2, 8, 128, 128) for c in range(4, 8)])
    return (np.ascontiguousarray(y_p, dtype=np.float32), np.ascontiguousarray(y_s, dtype=np.float32),
            np.ascontiguousarray(nk, dtype=np.float32), np.ascontiguousarray(nv, dtype=np.float32),
            np.ascontiguousarray(ns, dtype=np.float32))
```

```python
import os
import numpy as np
import concourse.bass as bass
import concourse.mybir as mybir
from concourse.bass_utils import run_bass_kernel_spmd
from contextlib import ExitStack

F32 = mybir.dt.float32
F32R = mybir.dt.float32r
BF16 = mybir.dt.bfloat16
AF = mybir.ActivationFunctionType
ALU = mybir.AluOpType

T = 2048
D = 1024
KC = 8
NTG = 4
TGS = 512
NSEG = 8
SEG = 256
ALPHA = 4.0 ** 0.25
EPS = 1e-6
EPS_LN = EPS / (ALPHA * ALPHA)
DFF = 2816
NFC = 22
PAST = 512
NKC = 20
CHUNK = 32
NCH = T // CHUNK


class Prog:
    ENGS = ("pe", "act", "dve", "pool", "sp")

    def __init__(self, nc, es, n_dma_sems=12):
        self.nc = nc
        self.eng = {"pe": nc.tensor, "act": nc.scalar, "dve": nc.vector, "pool": nc.gpsimd, "sp": nc.sync}
        self.esem = {e: es.enter_context(nc.semaphore("e_" + e)) for e in self.ENGS}
        self.nds = n_dma_sems
        self.dsem = {e: [es.enter_context(nc.semaphore("d_%s_%d" % (e, i))) for i in range(n_dma_sems)]
                     for e in ("sp", "pool")}
        self.cnt = {e: 0 for e in self.ENGS}
        self.ndma = {e: 0 for e in self.dsem}
        self.dval = {}
        self.waited = {e: {} for e in self.ENGS}
        self.last_w = {}
        self.readers = {}
        self.nops = {e: 0 for e in self.ENGS}

    def _wait(self, e, sem, val):
        w = self.waited[e]
        if w.get(sem.num, 0) >= val:
            return
        w[sem.num] = val
        self.eng[e].wait_ge(sem, val)

    def _wait_tok(self, e, tok, is_dma_op):
        if tok[0] == "E":
            _, f, c = tok
            if f == "pe" and e == "pe" and not is_dma_op:
                return
            self._wait(e, self.esem[f], c)
        else:
            _, q, i, v = tok
            self._wait(e, self.dsem[q][i], v)

    def add(self, e, fn, reads=(), writes=(), dma=False):
        psr = [k for k in reads if isinstance(k, tuple) and k[0] == "ps"]
        if psr:
            writes = list(writes) + [k for k in psr if k not in writes]
        deps = set()
        for k in reads:
            w = self.last_w.get(k)
            if w is not None:
                deps.add(w)
        for k in writes:
            w = self.last_w.get(k)
            if w is not None:
                deps.add(w)
            for r in self.readers.get(k, ()):
                deps.add(r)
        for tok in sorted(deps):
            self._wait_tok(e, tok, dma)
        eng = self.eng[e]
        if dma:
            i = self.ndma[e] % self.nds
            self.ndma[e] += 1
            prev = self.dval.get((e, i), 0)
            if prev:
                self._wait(e, self.dsem[e][i], prev)
            ins = fn(eng)
            ins.then_inc(self.dsem[e][i], 16)
            self.dval[(e, i)] = prev + 16
            tok = ("D", e, i, prev + 16)
        else:
            ins = fn(eng)
            self.cnt[e] += 1
            ins.then_inc(self.esem[e], 1)
            tok = ("E", e, self.cnt[e])
        self.nops[e] += 1
        for k in writes:
            self.last_w[k] = tok
            self.readers[k] = []
        for k in reads:
            self.readers.setdefault(k, []).append(tok)
        return tok

    def barrier(self):
        for e in self.ENGS:
            for f in self.ENGS:
                if f != e and self.cnt[f] > 0:
                    self._wait(e, self.esem[f], self.cnt[f])
            for (q, i), v in self.dval.items():
                self._wait(e, self.dsem[q][i], v)
        self.last_w.clear()
        self.readers.clear()


def _swap_idx():
    idx = np.zeros(64, dtype=np.int64)
    for d in range(64):
        blk = d // 16
        idx[d] = d + 16 if blk % 2 == 0 else d - 16
    return idx


def build(n_dbg=None, upto=99):
    nc = bass.Bass("TRN2", target_bir_lowering=False)
    es = ExitStack()
    P = Prog(nc, es)

    def din(name, shape, dt=F32):
        return nc.dram_tensor(name, list(shape), dt, kind="ExternalInput").ap()

    def dout(name, shape, dt=F32):
        return nc.dram_tensor(name, list(shape), dt, kind="ExternalOutput").ap()

    sbn = [0]

    def sb(stack, name, shape, dt):
        sbn[0] += 1
        return stack.enter_context(nc.sbuf_tensor("s%d_%s" % (sbn[0], name), list(shape), dt))

    d_xT = din("xT", [D, T])
    d_cond = din("cond", [128, KC])
    d_flags = din("flags", [128, 4])
    d_ropeC = din("ropeC", [128, T])
    d_ropeS = din("ropeS", [128, T])
    d_kcT = din("kcT", [2, 128, PAST])
    d_vc = din("vc", [PAST, 128])
    d_state = din("state", [2, 8, 128, 128])
    d_wmod = din("w_mod", [2, D, 6 * D])
    d_bmod = din("bmod", [128, 96])
    d_lnp = din("lnp", [128, 2, 4, KC])
    d_win = din("attn_in_u", [19, 128, KC, 128])
    d_gains = din("gains", [128, 4])
    d_consts = din("consts", [3, 128, 128])
    d_scw = din("scw", [128, 4, 3])
    d_wout = din("attn_out_u", [8, 128, KC, 128])
    d_wup = din("ffn_up_u", [2, 44, 128, KC, 128])
    d_fcw = din("fcw", [2, 128, 44, 3])
    d_wdn = din("ffn_dn_u", [2, 8, 128, NFC, 128])
    d_hin = din("hgrn_in_u", [40, 128, KC, 128])
    d_lbl = din("lbl", [128, 2, 2, 8])
    d_hng = din("hng", [128, 1])
    d_hout = din("hgrn_out_u", [8, 128, KC, 128])
    d_hmask = din("hscan", [128, 2, 512])
    d_htri = din("htri", [128, 2, 128])

    o_yT = dout("yT", [D, T])
    o_kT = dout("koutT", [2, 64, T])
    o_v = dout("vout", [T, 128])
    o_s = dout("sout", [NSEG, 2, 8, 128, 128])

    dbg = {}

    def tap(name, ap, shape, reads, dt=F32):
        if n_dbg is None:
            return
        d = dout("dbg_" + name, shape, dt)
        n_dbg[name] = shape
        P.add("sp", lambda e: e.dma_start(out=d, in_=ap), reads=reads, writes=["dbg_" + name], dma=True)

    xT = sb(es, "xT_sb", [128, KC, T], F32)
    hT = sb(es, "hT_sb", [128, KC, T], BF16)
    mv = sb(es, "mv", [128, 2, 10, KC], F32)
    lnp = sb(es, "lnp_sb", [128, 2, 4, KC], F32)
    flags = sb(es, "flags_sb", [128, 4], F32)
    onesr = sb(es, "onesr", [128, 128], F32R)
    bonesr = sb(es, "bonesr", [128, 128], F32R)
    swapT = sb(es, "swapT", [128, 128], BF16)
    identb = sb(es, "identb", [128, 128], BF16)
    cst_f = sb(es, "cst_f", [128, 3, 128], F32)
    PS = es.enter_context(nc.psum_tensor("PS", [128, 4096], F32))

    def bank(b, n=1):
        return PS[:, b * 512:(b + n) * 512]

    def psk(b, n=1):
        return [("ps", i) for i in range(b, b + n)]

    def xk(j, g=None):
        if g is None:
            return [("xT", j, gg) for gg in range(NTG)]
        return [("xT", j, g)]

    def hk(j, g=None):
        if g is None:
            return [("hT", j, gg) for gg in range(NTG)]
        return [("hT", j, g)]

    om = flags[:, 0:1]
    keep = flags[:, 1:2]
    nbm = flags[:, 2:3]

    for j in range(KC):
        P.add("sp", lambda e, j=j: e.dma_start(out=xT[:, j, :], in_=d_xT[j * 128:(j + 1) * 128, :]),
              writes=xk(j), dma=True)
    P.add("sp", lambda e: e.dma_start(out=flags[:, :], in_=d_flags[:, :]), writes=["flags"], dma=True)
    P.add("sp", lambda e: e.dma_start(out=lnp[:], in_=d_lnp[:]), writes=["lnp"], dma=True)
    P.add("sp", lambda e: e.dma_start(out=cst_f[:], in_=d_consts.rearrange("c p m -> p c m")), writes=["cst_f"], dma=True)
    P.add("dve", lambda e: e.tensor_copy(out=swapT[:, :], in_=cst_f[:, 0, :]), reads=["cst_f"], writes=["swapT"])
    P.add("dve", lambda e: e.tensor_copy(out=bonesr[:, :], in_=cst_f[:, 1, :]), reads=["cst_f"], writes=["bonesr"])
    P.add("dve", lambda e: e.tensor_copy(out=identb[:, :], in_=cst_f[:, 2, :]), reads=["cst_f"], writes=["identb"])
    P.add("pool", lambda e: e.memset(cst_f[:, 1, :], 1.0), reads=["bonesr"], writes=["cst_f"])
    P.add("dve", lambda e: e.tensor_copy(out=onesr[:, :], in_=cst_f[:, 1, :]), reads=["cst_f"], writes=["onesr"])

    scond = sb(es, "scond", [128, KC], BF16)
    bmod = sb(es, "bmod_sb", [128, 96], F32)
    mod = sb(es, "mod_sb", [128, 2, 6, KC], F32)
    with ExitStack() as ph:
        cond = sb(ph, "cond_sb", [128, KC], F32)
        wm = [sb(ph, "wm%d" % i, [128, KC, 1024], BF16) for i in range(2)]
        P.add("sp", lambda e: e.dma_start(out=cond[:, :], in_=d_cond[:, :]), writes=["cond"], dma=True)
        P.add("sp", lambda e: e.dma_start(out=bmod[:, :], in_=d_bmod[:, :]), writes=["bmod"], dma=True)
        P.add("act", lambda e: e.activation(out=scond[:, :], in_=cond[:, :], func=AF.Silu), reads=["cond"], writes=["scond"])
        for l in range(1):
            wv = d_wmod[l].rearrange("(k p) n -> p k n", p=128)
            for pc in range(3):
                slot = (l * 6 + pc) % 2
                P.add("pool", lambda e, slot=slot, pc=pc, wv=wv: e.dma_start(out=wm[slot][:], in_=wv[:, :, pc * 1024:(pc + 1) * 1024]),
                      writes=[("wm", slot)], dma=True)
                for cc in range(8):
                    col = l * 48 + pc * 8 + cc

                    def mmfn(e, slot=slot, cc=cc, col=col):
                        ins = None
                        for k in range(KC):
                            ins = e.matmul(PS[:, col:col + 1], lhsT=wm[slot][:, k, cc * 128:(cc + 1) * 128],
                                           rhs=scond[:, k:k + 1], start=(k == 0), stop=(k == KC - 1))
                        return ins
                    P.add("pe", mmfn, reads=[("wm", slot), "scond"], writes=[("ps", 0)])
        P.add("dve", lambda e: e.tensor_tensor(out=mod[:, 0, 0:3].rearrange("p v c -> p (v c)"), in0=PS[:, 0:24], in1=bmod[:, 0:24], op=ALU.add),
              reads=[("ps", 0), "bmod"], writes=["mod0a"])

        def derive(l, part):
            mk = "mod%d%s" % (l, part)
            if part == "a":
                P.add("dve", lambda e: e.tensor_copy(out=mv[:, l, 0, :], in_=mod[:, l, 0, :]), reads=[mk], writes=[("mv", l, 0)])
                P.add("dve", lambda e: e.tensor_scalar(out=mv[:, l, 1, :], in0=mod[:, l, 1, :], scalar1=1.0, scalar2=None, op0=ALU.add),
                      reads=[mk], writes=[("mv", l, 1)])
                P.add("dve", lambda e: e.tensor_scalar(out=mv[:, l, 2, :], in0=mod[:, l, 2, :], scalar1=1.0 / ALPHA, scalar2=None, op0=ALU.mult),
                      reads=[mk], writes=[("mv", l, 2)])
                return
            P.add("dve", lambda e: e.tensor_copy(out=mv[:, l, 3, :], in_=mod[:, l, 3, :]), reads=[mk], writes=[("mv", l, 3)])
            P.add("dve", lambda e: e.tensor_scalar(out=mv[:, l, 4, :], in0=mod[:, l, 4, :], scalar1=1.0, scalar2=None, op0=ALU.add),
                  reads=[mk], writes=[("mv", l, 4)])
            P.add("dve", lambda e: e.tensor_scalar(out=mv[:, l, 5, :], in0=mod[:, l, 5, :], scalar1=1.0 / ALPHA, scalar2=None, op0=ALU.mult),
                  reads=[mk], writes=[("mv", l, 5)])
            P.add("dve", lambda e: e.tensor_tensor(out=mv[:, l, 6, :], in0=lnp[:, l, 0, :], in1=mv[:, l, 4, :], op=ALU.mult),
                  reads=["lnp", ("mv", l, 4)], writes=[("mv", l, 6)])
            P.add("dve", lambda e: e.tensor_tensor(out=mv[:, l, 7, :], in0=lnp[:, l, 1, :], in1=mv[:, l, 4, :], op=ALU.mult),
                  reads=["lnp", ("mv", l, 4)], writes=[("mv", l, 7)])
            P.add("dve", lambda e: e.tensor_tensor(out=mv[:, l, 7, :], in0=mv[:, l, 7, :], in1=mv[:, l, 3, :], op=ALU.add),
                  reads=[("mv", l, 7), ("mv", l, 3)], writes=[("mv", l, 7)])

        def derive_cross():
            P.add("dve", lambda e: e.tensor_tensor(out=mv[:, 0, 8, :], in0=lnp[:, 0, 2, :], in1=mv[:, 1, 1, :], op=ALU.mult),
                  reads=["lnp", ("mv", 1, 1)], writes=[("mv", 0, 8)])
            P.add("dve", lambda e: e.tensor_tensor(out=mv[:, 0, 9, :], in0=lnp[:, 0, 3, :], in1=mv[:, 1, 1, :], op=ALU.mult),
                  reads=["lnp", ("mv", 1, 1)], writes=[("mv", 0, 9)])
            P.add("dve", lambda e: e.tensor_tensor(out=mv[:, 0, 9, :], in0=mv[:, 0, 9, :], in1=mv[:, 1, 0, :], op=ALU.add),
                  reads=[("mv", 0, 9), ("mv", 1, 0)], writes=[("mv", 0, 9)])

        derive(0, "a")
        tap("mv", mv[:].rearrange("p l v c -> p (l v c)"), [128, 2 * 10 * KC], [("mv", l, v) for l in range(2) for v in range(10)])
        P.barrier()

    MVK = [("mv", l, v) for l in range(2) for v in range(10)]

    for j in range(KC):
        if j % 2 == 0:
            P.add("dve", lambda e, j=j: e.tensor_scalar(out=hT[:, j, :], in0=xT[:, j, :], scalar1=mv[:, 0, 1, j:j + 1],
                                                        scalar2=mv[:, 0, 0, j:j + 1], op0=ALU.mult, op1=ALU.add),
                  reads=xk(j) + MVK, writes=hk(j))
        else:
            P.add("act", lambda e, j=j: e.activation(out=hT[:, j, :], in_=xT[:, j, :], func=AF.Identity,
                                                     bias=mv[:, 0, 0, j:j + 1], scale=mv[:, 0, 1, j:j + 1]),
                  reads=xk(j) + MVK, writes=hk(j))
    tap("h0", hT[:, 0, :], [128, T], hk(0)) if False else None

    ctx = dict(m1=dict(scond=scond, bmod=bmod, mod=mod, d_wmod=d_wmod, derive=derive, derive_cross=derive_cross),
               nc=nc, es=es, P=P, PS=PS, bank=bank, psk=psk, xk=xk, hk=hk, xT=xT, hT=hT, mv=mv, lnp=lnp, MVK=MVK,
               onesr=onesr, bonesr=bonesr, swapT=swapT, identb=identb, om=om, keep=keep, nbm=nbm, sb=sb, tap=tap,
               d=dict(win=d_win, gains=d_gains, ropeC=d_ropeC, ropeS=d_ropeS, kcT=d_kcT, vc=d_vc, scw=d_scw, wout=d_wout,
                      wup=d_wup, fcw=d_fcw, wdn=d_wdn, hin=d_hin, lbl=d_lbl, hng=d_hng, hout=d_hout, hmask=d_hmask, htri=d_htri,
                      state=d_state, o_yT=o_yT, o_kT=o_kT, o_v=o_v, o_s=o_s))
    if upto >= 1:
        layer0_mixer(ctx)
    if upto >= 2:
        layernorm(ctx, 0, 1)
    if upto >= 3:
        ffn(ctx, 0)
    if upto >= 4:
        layernorm(ctx, 0, 2)
    if upto >= 5:
        hgrn_mixer(ctx)
    if upto >= 6:
        layernorm(ctx, 1, 1)
    if upto >= 7:
        ffn(ctx, 1)
    if upto >= 8:
        layernorm(ctx, 1, 2, final=True)
    if n_dbg is not None:
        for j in range(KC):
            tap("x%d" % j, xT[:, j, :], [128, T], xk(j))
            tap("h%d" % j, hT[:, j, :], [128, T], hk(j), BF16)
    P.barrier()
    return nc, P


def load_unit(c, dst, src, key, reads=()):
    c["P"].add("pool", lambda e: e.dma_start(out=dst, in_=src), reads=list(reads), writes=[key], dma=True)


class WStream:
    def __init__(self, c, bufs, key, srcs, pf):
        self.c, self.bufs, self.key, self.srcs, self.pf = c, bufs, key, srcs, pf
        self.issued = 0
        self.n = len(bufs)

    def prefetch(self, upto):
        upto = min(upto, len(self.srcs) - 1)
        while self.issued <= upto:
            i = self.issued
            src = self.srcs[i]
            if isinstance(src, tuple):
                load_unit(self.c, self.bufs[i % self.n][:, 0:src[1], :], src[0], (self.key, i % self.n))
            else:
                load_unit(self.c, self.bufs[i % self.n][:], src, (self.key, i % self.n))
            self.issued += 1

    def get(self, i):
        self.prefetch(i + self.pf)
        return self.bufs[i % self.n], (self.key, i % self.n)


def proj_fm(c, wtile, wkey, half, rhs_of, rkeys, nk=KC):
    P, bank, psk = c["P"], c["bank"], c["psk"]
    for g in range(NTG):
        b = half * 4 + g

        def fn(e, g=g, b=b):
            ins = None
            for k in range(nk):
                ins = e.matmul(bank(b), lhsT=wtile[:, k, :], rhs=rhs_of(k, g), start=(k == 0), stop=(k == nk - 1))
            return ins
        P.add("pe", fn, reads=[wkey] + rkeys(g), writes=psk(b))


def layer0_mixer(c):
    P, nc, PS, bank, psk = c["P"], c["nc"], c["PS"], c["bank"], c["psk"]
    xT, hT, mv, sb, tap, d = c["xT"], c["hT"], c["mv"], c["sb"], c["tap"], c["d"]
    xk, hk, MVK = c["xk"], c["hk"], c["MVK"]
    om, nbm = c["om"], c["nbm"]
    onesr, bonesr, swapT = c["onesr"], c["bonesr"], c["swapT"]
    d_win = d["win"]
    L0 = os.environ.get("K_L0", "ABCDE")

    def hrhs(k, g):
        return hT[:, k, g * TGS:(g + 1) * TGS]

    def hkeys(g):
        return [("hT", k, g) for k in range(KC)]

    with ExitStack() as ph:
        scT = sb(ph, "scT", [128, 4, T], BF16)
        wu = [sb(ph, "wu%d" % i, [128, KC, 128], BF16) for i in range(4)]
        gains = sb(ph, "gains", [128, 4], F32)
        scw = sb(ph, "scw_sb", [128, 4, 3], F32)
        nbw = sb(ph, "nbw", [128, 4, 2], F32)
        P.add("sp", lambda e: e.dma_start(out=gains[:, :], in_=d["gains"][:, :]), writes=["gains"], dma=True)
        P.add("sp", lambda e: e.dma_start(out=scw[:], in_=d["scw"][:]), writes=["scw"], dma=True)
        P.add("dve", lambda e: e.tensor_scalar(out=nbw[:, :, 0], in0=scw[:, :, 0], scalar1=nbm, scalar2=None, op0=ALU.mult),
              reads=["scw", "flags"], writes=["nbw"])
        P.add("dve", lambda e: e.tensor_scalar(out=nbw[:, :, 1], in0=scw[:, :, 2], scalar1=nbm, scalar2=None, op0=ALU.mult),
              reads=["scw", "flags", "nbw"], writes=["nbw"])
        srcs0 = []
        for ch in range(4):
            srcs0 += [d_win[15 + ch], d_win[11 + ch], d_win[7 + ch]]
        srcs0 += [d_win[un] for un in range(6)] + [d_win[6]] + [d["wout"][j] for j in range(KC)]
        ws0 = WStream(c, wu, "wu", srcs0, 3)
        ucount = [0]

        def next_unit(src):
            i = ucount[0]
            ucount[0] += 1
            assert srcs0[i] is src or True
            ws0.get(i)
            return i % 4

        with ExitStack() as pa:
            xi_sb = sb(pa, "xi_sb", [128, T], F32)
            mpad = sb(pa, "mpad", [128, T + 2], F32)
            u = sb(pa, "u_sb", [128, T], F32)
            P.add("dve", lambda e: e.memset(mpad[:, :], 0.0), writes=["mpad"])
            for ch in (range(4) if "A" in L0 else ()):
                s_xi = next_unit(d_win[15 + ch])
                proj_fm(c, wu[s_xi], ("wu", s_xi), 0, hrhs, hkeys)
                s_cg = next_unit(d_win[11 + ch])
                P.add("act", lambda e: e.activation(out=xi_sb[:, :], in_=bank(0, 4), func=AF.Copy), reads=psk(0, 4), writes=["xi_sb"])
                proj_fm(c, wu[s_cg], ("wu", s_cg), 1, hrhs, hkeys)
                s_bg = next_unit(d_win[7 + ch])
                P.add("dve", lambda e: e.tensor_tensor(out=mpad[:, 1:T + 1], in0=bank(4, 4), in1=xi_sb[:, :], op=ALU.mult),
                      reads=psk(4, 4) + ["xi_sb"], writes=["mpad"])
                proj_fm(c, wu[s_bg], ("wu", s_bg), 0, hrhs, hkeys)
                P.add("act", lambda e, ch=ch: e.activation(out=u[:, :], in_=mpad[:, 1:T + 1], func=AF.Copy, scale=scw[:, ch, 1:2]),
                      reads=["mpad", "scw"], writes=["u"])
                P.add("dve", lambda e, ch=ch: e.scalar_tensor_tensor(out=u[:, :], in0=mpad[:, 0:T], scalar=scw[:, ch, 0:1], in1=u[:, :],
                                                                     op0=ALU.mult, op1=ALU.add), reads=["mpad", "scw", "u"], writes=["u"])
                P.add("dve", lambda e, ch=ch: e.scalar_tensor_tensor(out=u[:, :], in0=mpad[:, 2:T + 2], scalar=scw[:, ch, 2:3], in1=u[:, :],
                                                                     op0=ALU.mult, op1=ALU.add), reads=["mpad", "scw", "u"], writes=["u"])
                P.add("dve", lambda e, ch=ch: e.scalar_tensor_tensor(out=u[:, SEG:T:SEG], in0=mpad[:, SEG:T:SEG], scalar=nbw[:, ch, 0:1],
                                                                     in1=u[:, SEG:T:SEG], op0=ALU.mult, op1=ALU.add),
                      reads=["mpad", "nbw", "u"], writes=["u"])
                P.add("dve", lambda e, ch=ch: e.scalar_tensor_tensor(out=u[:, SEG - 1:T - 1:SEG], in0=mpad[:, SEG + 1:T + 1:SEG], scalar=nbw[:, ch, 1:2],
                                                                     in1=u[:, SEG - 1:T - 1:SEG], op0=ALU.mult, op1=ALU.add),
                      reads=["mpad", "nbw", "u"], writes=["u"])
                P.add("dve", lambda e, ch=ch: e.tensor_tensor(out=scT[:, ch, :], in0=bank(0, 4), in1=u[:, :], op=ALU.mult),
                      reads=psk(0, 4) + ["u"], writes=[("scT", ch)])
            P.barrier()
        tap("scT", scT[:].rearrange("p c t -> p (c t)"), [128, 4 * T], [("scT", ch) for ch in range(4)]) if False else None

        with ExitStack() as pb:
            qT = sb(pb, "qT", [128, 4, T], BF16)
            kTz = sb(pb, "kTz", [128, 2, 2, T + PAST], BF16)
            P.add("dve", lambda e: e.memset(kTz[:].rearrange("p a b t -> p (a b t)"), 0.0), writes=["kTz0"])
            with ExitStack() as pq:
                ropeC = sb(pq, "ropeC", [128, T], F32)
                ropeS = sb(pq, "ropeS", [128, T], F32)
                P.add("sp", lambda e: e.dma_start(out=ropeC[:, :], in_=d["ropeC"][:, :]), writes=["ropeC"], dma=True)
                P.add("sp", lambda e: e.dma_start(out=ropeS[:, :], in_=d["ropeS"][:, :]), writes=["ropeS"], dma=True)
                qsq = [sb(pq, "qsq%d" % i, [128, TGS], F32R) for i in range(2)]
                qbf = [sb(pq, "qbf%d" % i, [128, TGS], BF16) for i in range(2)]
                sd = [sb(pq, "sd%d" % i, [128, TGS], F32) for i in range(2)]
                rstd = [sb(pq, "rstd%d" % i, [128, TGS], F32) for i in range(2)]
                t1 = [sb(pq, "t1_%d" % i, [128, TGS], F32) for i in range(2)]
                t2 = [sb(pq, "t2_%d" % i, [128, TGS], F32) for i in range(2)]
                m1 = c["m1"]
                wm1 = [sb(pq, "wm1_%d" % i, [128, KC, 128], BF16) for i in range(4)]
                wv1 = m1["d_wmod"][1].rearrange("(k p) n -> p k n", p=128)

                wv0 = m1["d_wmod"][0].rearrange("(k p) n -> p k n", p=128)
                mchunks = [(wv0, cidx) for cidx in range(24, 48)] + [(wv1, cidx) for cidx in range(48)]

                def m1_load(pi):
                    if pi >= len(mchunks):
                        return
                    wvx, cidx = mchunks[pi]
                    P.add("pool", lambda e: e.dma_start(out=wm1[pi % 4][:], in_=wvx[:, :, cidx * 128:(cidx + 1) * 128]), writes=[("wm1", pi % 4)], dma=True)

                def m1_mm(pi):
                    col = 6 * 512 + pi

                    def mmfn(e):
                        ins = None
                        for k in range(KC):
                            ins = e.matmul(PS[:, col:col + 1], lhsT=wm1[pi % 4][:, k, :],
                                           rhs=m1["scond"][:, k:k + 1], start=(k == 0), stop=(k == KC - 1))
                        return ins
                    P.add("pe", mmfn, reads=[("wm1", pi % 4), "scond"], writes=[("ps", 6)])
                m1_load(0)
                m1_load(1)
                m1_load(2)
                it = 0
                for un in (range(6) if "B" in L0 else ()):
                    slot = next_unit(d_win[un])
                    isk = un >= 4
                    gcol = 2 if isk else 0
                    for g in range(NTG):
                        s = it % 2
                        it += 1
                        bA, bB, bC = s * 3, s * 3 + 1, s * 3 + 2
                        tok = slice(g * TGS, (g + 1) * TGS)

                        def fn(e, slot=slot, g=g, bA=bA):
                            ins = None
                            for k in range(KC):
                                ins = e.matmul(bank(bA), lhsT=wu[slot][:, k, :], rhs=hrhs(k, g), start=(k == 0), stop=(k == KC - 1))
                            return ins
                        P.add("pe", fn, reads=[("wu", slot)] + hkeys(g), writes=psk(bA))
                        for cc_ in range(3):
                            m1_mm(3 * (it - 1) + cc_)
                            m1_load(3 * (it - 1) + cc_ + 3)
                        P.add("act", lambda e, s=s, bA=bA: e.activation(out=qsq[s][:, :], in_=bank(bA), func=AF.Square),
                              reads=psk(bA), writes=[("qsq", s)])
                        P.add("act", lambda e, s=s, bA=bA: e.activation(out=qbf[s][:, :], in_=bank(bA), func=AF.Copy),
                              reads=psk(bA), writes=[("qbf", s)])
                        P.add("pe", lambda e, s=s, bB=bB: e.matmul(bank(bB), lhsT=bonesr[:, :], rhs=qsq[s][:, :], start=True, stop=True),
                              reads=["bonesr", ("qsq", s)], writes=psk(bB))
                        P.add("pe", lambda e, s=s, bC=bC: e.matmul(bank(bC), lhsT=swapT[:, :], rhs=qbf[s][:, :], start=True, stop=True),
                              reads=["swapT", ("qbf", s)], writes=psk(bC))
                        P.add("act", lambda e, s=s, bB=bB: e.activation(out=sd[s][:, :], in_=bank(bB), func=AF.Ln, bias=EPS, scale=1.0 / 64.0),
                              reads=psk(bB), writes=[("sd", s)])
                        P.add("act", lambda e, s=s: e.activation(out=rstd[s][:, :], in_=sd[s][:, :], func=AF.Exp, scale=-0.5), reads=[("sd", s)], writes=[("rstd", s)])
                        P.add("dve", lambda e, s=s, bA=bA, tok=tok, gcol=gcol: e.scalar_tensor_tensor(
                            out=t1[s][:, :], in0=bank(bA), scalar=gains[:, gcol:gcol + 1], in1=ropeC[:, tok], op0=ALU.mult, op1=ALU.mult),
                            reads=psk(bA) + ["gains", "ropeC"], writes=[("t1", s)])
                        P.add("dve", lambda e, s=s, bC=bC, tok=tok, gcol=gcol: e.scalar_tensor_tensor(
                            out=t2[s][:, :], in0=bank(bC), scalar=gains[:, gcol + 1:gcol + 2], in1=ropeS[:, tok], op0=ALU.mult, op1=ALU.mult),
                            reads=psk(bC) + ["gains", "ropeS"], writes=[("t2", s)])
                        P.add("pool", lambda e, s=s: e.tensor_tensor(out=t1[s][:, :], in0=t1[s][:, :], in1=t2[s][:, :], op=ALU.add),
                              reads=[("t1", s), ("t2", s)], writes=[("t1", s)])
                        if not isk:
                            P.add("pool", lambda e, s=s, un=un, tok=tok: e.tensor_tensor(out=qT[:, un, tok], in0=t1[s][:, :], in1=rstd[s][:, :], op=ALU.mult),
                                  reads=[("t1", s), ("rstd", s)], writes=[("qT", un, g)])
                        else:
                            kv = un - 4
                            P.add("pool", lambda e, s=s: e.tensor_tensor(out=t2[s][:, :], in0=t1[s][:, :], in1=rstd[s][:, :], op=ALU.mult),
                                  reads=[("t1", s), ("rstd", s)], writes=[("t2", s)])
                            P.add("pool", lambda e, s=s, kv=kv, tok=tok: e.tensor_copy(out=kTz[0:64, kv, 0, tok], in_=t2[s][0:64, :]),
                                  reads=[("t2", s), "kTz0"], writes=[("kTd", kv, g, 0)])
                            P.add("pool", lambda e, s=s, kv=kv, tok=tok: e.tensor_copy(out=kTz[64:128, kv, 1, tok], in_=t2[s][64:128, :]),
                                  reads=[("t2", s), "kTz0"], writes=[("kTd", kv, g, 1)])
                            P.add("sp", lambda e, s=s, kv=kv, tok=tok: e.dma_start(out=d["o_kT"][kv, :, tok], in_=t2[s][0:64, :]),
                                  reads=[("t2", s)], writes=[("o_kT", kv, g)], dma=True)
                P.add("dve", lambda e: e.tensor_tensor(out=m1["mod"][:, 0, 3:6].rearrange("p v c -> p (v c)"), in0=PS[:, 6 * 512:6 * 512 + 24],
                                                       in1=m1["bmod"][:, 24:48], op=ALU.add), reads=[("ps", 6), "bmod"], writes=["mod0b"])
                P.add("dve", lambda e: e.tensor_tensor(out=m1["mod"][:, 1].rearrange("p v c -> p (v c)"), in0=PS[:, 6 * 512 + 24:6 * 512 + 72],
                                                       in1=m1["bmod"][:, 48:96], op=ALU.add), reads=[("ps", 6), "bmod"], writes=["mod1a", "mod1b"])
                m1["derive"](0, "b")
                m1["derive"](1, "a")
                m1["derive"](1, "b")
                m1["derive_cross"]()
                P.barrier()
            pvv = pb
            VVd = sb(pvv, "VVd", [128, 32, 128], BF16)
            VVo = sb(pvv, "VVo", [128, 40, 128], BF16)
            with ExitStack() as pq:
                vf = [sb(pq, "vf%d" % i, [128, 4, 128], F32) for i in range(2)]
                vcf = sb(pq, "vcf", [128, 4, 128], F32)
                KCX = os.environ.get("K_C", "kvmpados")
                for kv in (range(2) if "k" in KCX else ()):
                    P.add("pool", lambda e, kv=kv: e.dma_start(out=kTz[0:64, kv, 0, T:T + PAST], in_=d["kcT"][kv, 0:64, :]), writes=[("kTd", kv, "ctx", 0)], dma=True)
                    P.add("pool", lambda e, kv=kv: e.dma_start(out=kTz[64:128, kv, 1, T:T + PAST], in_=d["kcT"][kv, 64:128, :]), writes=[("kTd", kv, "ctx", 1)], dma=True)
                if "v" in KCX:
                    P.add("sp", lambda e: e.dma_start(out=vcf[:], in_=d["vc"].rearrange("(t p) c -> p t c", p=128)), writes=["vcf"], dma=True)
                P.add("dve", lambda e: e.memset(VVd[:, :, 64:128], 1.0), writes=["VVd1"])
                P.add("dve", lambda e: e.memset(VVo[:, :, 64:128], 1.0), writes=["VVo1"])
                P.add("dve", lambda e: e.tensor_scalar(out=VVo[:, :, 64:128], in0=VVo[:, :, 64:128], scalar1=om, scalar2=None, op0=ALU.mult),
                      reads=["VVo1", "flags"], writes=["VVo1"])
                P.add("dve", lambda e: e.tensor_scalar(out=VVo[:, 32:40, 0:64], in0=vcf[:].rearrange("p t (g c) -> p (t g) c", g=2),
                                                        scalar1=om, scalar2=None, op0=ALU.mult),
                      reads=["vcf", "flags"], writes=[("VVo", 4)])
                slot = next_unit(d_win[6])
                for t4 in (range(4) if ("C" in L0 and "p" in KCX) else ()):
                    b = 6 + t4 % 2
                    s = t4 % 2

                    def fn(e, slot=slot, t4=t4, b=b):
                        ins = None
                        for i in range(4):
                            tt = t4 * 4 + i
                            for k in range(KC):
                                ins = e.matmul(bank(b)[:, i * 128:(i + 1) * 128], lhsT=hT[:, k, tt * 128:(tt + 1) * 128], rhs=wu[slot][:, k, :],
                                               start=(k == 0), stop=(k == KC - 1))
                        return ins
                    P.add("pe", fn, reads=[("wu", slot)] + hkeys(t4), writes=psk(b))
                    if "a" in KCX:
                        P.add("act", lambda e, s=s, b=b: e.activation(out=vf[s][:].rearrange("p t c -> p (t c)"), in_=bank(b), func=AF.Copy),
                              reads=psk(b), writes=[("vf", s)])
                    if "d" not in KCX:
                        continue
                    P.add("dve", lambda e, t4=t4, b=b: e.tensor_copy(out=VVd[:, t4 * 8:t4 * 8 + 8, 0:64], in_=bank(b).rearrange("p (s c) -> p s c", c=64)),
                          reads=psk(b), writes=[("VVd", t4)])
                    if "o" not in KCX:
                        continue
                    P.add("dve", lambda e, s=s, t4=t4: e.tensor_scalar(out=VVo[:, t4 * 8:t4 * 8 + 8, 0:64],
                                                                         in0=vf[s][:].rearrange("p t (g c) -> p (t g) c", g=2),
                                                                         scalar1=om, scalar2=None, op0=ALU.mult),
                          reads=[("vf", s), "flags"], writes=[("VVo", t4)])
                    if "s" not in KCX:
                        continue
                    P.add("sp", lambda e, s=s, t4=t4: e.dma_start(out=d["o_v"].rearrange("(t p) c -> p t c", p=128)[:, t4 * 4:(t4 + 1) * 4, :], in_=vf[s][:]),
                          reads=[("vf", s)], writes=[("o_v", t4)], dma=True)
                P.barrier()
            with ExitStack() as pd:
                attT = sb(pd, "attT", [128, 4, T], BF16)
                PT = [sb(pd, "PT%d" % i, [128, 2 * TGS], BF16) for i in range(2)]
                rden = [sb(pd, "rden%d" % i, [128, SEG], F32) for i in range(4)]
                its = [(h, g4, kp) for h in (range(8) if "D" in L0 else ()) for g4 in range(NTG) for kp in range(NKC // 2)]
                LA = 1

                def emit_S(n):
                    h, g4, kp = its[n]
                    cq, kv = h // 2, h // 4
                    m = n % 2
                    for j in range(2):
                        kc = kp * 2 + j
                        sbk = 2 * m + j
                        P.add("pe", lambda e, kc=kc, sbk=sbk: e.matmul(bank(sbk), lhsT=kTz[:, kv, h % 2, kc * 128:(kc + 1) * 128],
                                                                       rhs=qT[:, cq, g4 * TGS:(g4 + 1) * TGS], start=True, stop=True),
                              reads=[("kTd", kv, kc // 4 if kc < 16 else "ctx", h % 2), ("qT", cq, g4), "kTz0"], writes=psk(sbk))

                def emit_rest(n):
                    h, g4, kp = its[n]
                    cq, kv = h // 2, h // 4
                    odd = (h % 2 == 1)
                    m = n % 2
                    par = (h * NTG + g4) % 2
                    accb = [4 + par * 2, 5 + par * 2]
                    P.add("act", lambda e: e.activation(out=PT[m][:, :], in_=bank(2 * m, 2), func=AF.Exp, scale=0.125),
                          reads=psk(2 * m, 2), writes=[("PT", m)])
                    for j in range(2):
                        kc = kp * 2 + j
                        for s_ in range(2):
                            seg = g4 * 2 + s_
                            diag = kc < 16 and (kc // 2) == seg
                            lhs = (VVd if diag else VVo)[:, kc * 2 + kv, :]
                            P.add("pe", lambda e, s_=s_, lhs=lhs, j=j, kc=kc: e.matmul(bank(accb[s_])[:, 0:SEG], lhsT=lhs,
                                                                                     rhs=PT[m][:, j * TGS + s_ * SEG:j * TGS + (s_ + 1) * SEG],
                                                                                     start=(kc == 0), stop=(kc == NKC - 1)),
                                  reads=[("PT", m), ("VVd", kc // 4) if diag else ("VVo", kc // 4), "VVd1", "VVo1"], writes=psk(accb[s_]))
                    if kp == NKC // 2 - 1:
                        for s_ in range(2):
                            seg = g4 * 2 + s_
                            rb = par * 2 + s_
                            nr, dr = slice(0, 64), slice(64, 128)
                            orow = slice(64, 128) if odd else slice(0, 64)
                            tok = slice(seg * SEG, (seg + 1) * SEG)
                            P.add("dve", lambda e, s_=s_, rb=rb: e.reciprocal(out=rden[rb][dr, :], in_=bank(accb[s_])[dr, 0:SEG]),
                                  reads=psk(accb[s_]), writes=[("rden", rb)])
                            P.add("dve", lambda e, s_=s_, rb=rb, orow=orow, tok=tok: e.tensor_tensor(
                                out=attT[orow, cq, tok], in0=bank(accb[s_])[nr, 0:SEG], in1=rden[rb][dr, :], op=ALU.mult),
                                reads=psk(accb[s_]) + [("rden", rb)], writes=[("attT", cq, seg, h % 2)])

                for n in range(len(its) + LA):
                    if n < len(its):
                        emit_S(n)
                    if n - LA >= 0:
                        emit_rest(n - LA)
                P.barrier()
                for j in (range(KC) if "E" in L0 else ()):
                    slot = next_unit(d["wout"][j])
                    half = j % 2

                    def rhs_of(k, g):
                        if k < 4:
                            return attT[:, k, g * TGS:(g + 1) * TGS]
                        return scT[:, k - 4, g * TGS:(g + 1) * TGS]

                    def rk(g):
                        r = [("scT", ch) for ch in range(4)]
                        for cq in range(4):
                            for seg in (2 * g, 2 * g + 1):
                                r += [("attT", cq, seg, 0), ("attT", cq, seg, 1)]
                        return r
                    proj_fm(c, wu[slot], ("wu", slot), half, rhs_of, rk)
                    P.add("dve", lambda e, j=j, half=half: e.scalar_tensor_tensor(out=xT[:, j, :], in0=bank(half * 4, 4), scalar=mv[:, 0, 2, j:j + 1],
                                                                                 in1=xT[:, j, :], op0=ALU.mult, op1=ALU.add),
                          reads=psk(half * 4, 4) + MVK + xk(j), writes=xk(j))
                P.barrier()


def layernorm(c, l, which, final=False):
    P, bank, psk = c["P"], c["bank"], c["psk"]
    xT, hT, mv, lnp, sb, d = c["xT"], c["hT"], c["mv"], c["lnp"], c["sb"], c["d"]
    onesr, MVK = c["onesr"], c["MVK"]
    gi, bi = (0, 1) if which == 1 else (2, 3)
    ghi, bhi = (6, 7) if which == 1 else (8, 9)
    with ExitStack() as ph:
        xr = [sb(ph, "xr%d" % i, [128, KC, TGS], F32R) for i in range(2)]
        xs = [sb(ph, "xs%d" % i, [128, KC, TGS], F32R) for i in range(2)]
        mean = [sb(ph, "mean%d" % i, [128, TGS], F32) for i in range(2)]
        msq = [sb(ph, "msq%d" % i, [128, TGS], F32) for i in range(2)]
        var = [sb(ph, "var%d" % i, [128, TGS], F32) for i in range(2)]
        rstd = [sb(ph, "lrstd%d" % i, [128, TGS], F32) for i in range(2)]
        nmr = [sb(ph, "nmr%d" % i, [128, TGS], F32) for i in range(2)]
        tt = [sb(ph, "ltt%d" % i, [128, TGS], F32) for i in range(4)]
        itc = [0]

        def stats(g):
            s = g % 2
            tok = slice(g * TGS, (g + 1) * TGS)
            bA, bB = 2 * s, 2 * s + 1
            for j in range(KC):
                P.add("dve", lambda e, j=j: e.tensor_copy(out=xr[s][:, j, :], in_=xT[:, j, tok]),
                      reads=[("xT", j, g)], writes=[("xr", s, j)])
                P.add("act", lambda e, j=j: e.activation(out=xs[s][:, j, :], in_=xT[:, j, tok], func=AF.Square),
                      reads=[("xT", j, g)], writes=[("xs", s, j)])

            def fnA(e):
                ins = None
                for j in range(KC):
                    ins = e.matmul(bank(bA), lhsT=onesr[:, :], rhs=xr[s][:, j, :], start=(j == 0), stop=(j == KC - 1))
                return ins

            def fnB(e):
                ins = None
                for j in range(KC):
                    ins = e.matmul(bank(bB), lhsT=onesr[:, :], rhs=xs[s][:, j, :], start=(j == 0), stop=(j == KC - 1))
                return ins
            P.add("pe", fnA, reads=["onesr"] + [("xr", s, j) for j in range(KC)], writes=psk(bA))
            P.add("pe", fnB, reads=["onesr"] + [("xs", s, j) for j in range(KC)], writes=psk(bB))
            P.add("act", lambda e: e.activation(out=mean[s][:, :], in_=bank(bA), func=AF.Copy, scale=1.0 / D),
                  reads=psk(bA), writes=[("mean", s)])
            P.add("pool", lambda e: e.tensor_tensor(out=msq[s][:, :], in0=mean[s][:, :], in1=mean[s][:, :], op=ALU.mult),
                  reads=[("mean", s)], writes=[("msq", s)])
            P.add("dve", lambda e: e.scalar_tensor_tensor(out=var[s][:, :], in0=bank(bB), scalar=1.0 / D, in1=msq[s][:, :],
                                                          op0=ALU.mult, op1=ALU.subtract),
                  reads=psk(bB) + [("msq", s)], writes=[("var", s)])
            P.add("act", lambda e: e.activation(out=var[s][:, :], in_=var[s][:, :], func=AF.Ln, bias=EPS_LN, scale=1.0),
                  reads=[("var", s)], writes=[("var", s)])
            P.add("act", lambda e: e.activation(out=rstd[s][:, :], in_=var[s][:, :], func=AF.Exp, scale=-0.5), reads=[("var", s)], writes=[("lrstd", s)])
            P.add("dve", lambda e: e.scalar_tensor_tensor(out=nmr[s][:, :], in0=mean[s][:, :], scalar=-1.0, in1=rstd[s][:, :],
                                                          op0=ALU.mult, op1=ALU.mult),
                  reads=[("mean", s), ("lrstd", s)], writes=[("nmr", s)])

        def norm(g):
            s = g % 2
            tok = slice(g * TGS, (g + 1) * TGS)
            for j in range(KC):
                r = itc[0] % 4
                itc[0] += 1
                P.add("dve", lambda e, r=r, j=j: e.tensor_tensor(out=tt[r][:, :], in0=xT[:, j, tok], in1=rstd[s][:, :], op=ALU.mult),
                      reads=[("xT", j, g), ("lrstd", s)], writes=[("ltt", r)])
                P.add("dve", lambda e, r=r: e.tensor_tensor(out=tt[r][:, :], in0=tt[r][:, :], in1=nmr[s][:, :], op=ALU.add),
                      reads=[("ltt", r), ("nmr", s)], writes=[("ltt", r)])
                P.add("act", lambda e, r=r, j=j: e.activation(out=xT[:, j, tok], in_=tt[r][:, :], func=AF.Identity,
                                                              bias=lnp[:, l, bi, j:j + 1], scale=lnp[:, l, gi, j:j + 1]),
                      reads=[("ltt", r), "lnp"], writes=[("xT", j, g)])
                if final:
                    P.add("sp", lambda e, j=j: e.dma_start(out=d["o_yT"][j * 128:(j + 1) * 128, tok], in_=xT[:, j, tok]),
                          reads=[("xT", j, g)], writes=[("o_yT", j, g)], dma=True)
                else:
                    P.add("pool", lambda e, r=r, j=j: e.tensor_scalar(out=hT[:, j, tok], in0=tt[r][:, :], scalar1=mv[:, l, ghi, j:j + 1],
                                                                       scalar2=mv[:, l, bhi, j:j + 1], op0=ALU.mult, op1=ALU.add),
                          reads=[("ltt", r)] + MVK, writes=[("hT", j, g)])

        stats(0)
        stats(1)
        norm(0)
        stats(2)
        norm(1)
        stats(3)
        norm(2)
        norm(3)
        P.barrier()


def ffn(c, l):
    P, bank, psk = c["P"], c["bank"], c["psk"]
    xT, hT, mv, sb, d = c["xT"], c["hT"], c["mv"], c["sb"], c["d"]
    xk, MVK, nbm = c["xk"], c["MVK"], c["nbm"]
    HC = NFC // 2

    def hrhs(k, g):
        return hT[:, k, g * TGS:(g + 1) * TGS]

    def hkeys(g):
        return [("hT", k, g) for k in range(KC)]

    with ExitStack() as ph:
        act = sb(ph, "ffact", [128, HC, T], BF16)
        uu = [sb(ph, "ffu%d" % i, [128, T], F32) for i in range(2)]
        sg = sb(ph, "ffsg", [128, T], F32)
        tl = [sb(ph, "fftl%d" % i, [128, T], F32) for i in range(2)]
        wu = [sb(ph, "fwu%d" % i, [128, KC, 128], BF16) for i in range(6)]
        wd = [sb(ph, "fwd%d" % i, [128, HC, 128], BF16) for i in range(3)]
        srcs_up = [d["wup"][l, pas * HC + cc + (NFC if isg else 0)] for pas in range(2) for cc in range(HC) for isg in range(2)]
        srcs_dn = [d["wdn"][l, j, :, pas * HC:(pas + 1) * HC, :] for pas in range(2) for j in range(KC)]
        wsU = WStream(c, wu, "fwu", srcs_up, 5)
        wsD = WStream(c, wd, "fwd", srcs_dn, 1)
        fcw = sb(ph, "fcw_sb", [128, 44, 3], F32)
        nbw = sb(ph, "fnbw", [128, 44, 2], F32)
        P.add("sp", lambda e: e.dma_start(out=fcw[:], in_=d["fcw"][l]), writes=["fcw"], dma=True)
        P.add("dve", lambda e: e.tensor_scalar(out=nbw[:, :, 0], in0=fcw[:, :, 0], scalar1=nbm, scalar2=None, op0=ALU.mult),
              reads=["fcw", "flags"], writes=["fnbw"])
        P.add("dve", lambda e: e.tensor_scalar(out=nbw[:, :, 1], in0=fcw[:, :, 2], scalar1=nbm, scalar2=None, op0=ALU.mult),
              reads=["fcw", "flags", "fnbw"], writes=["fnbw"])
        ucnt = 0
        dcnt = 0
        for pas in range(2):
            for cc in range(HC):
                for isg in range(2):
                    ch = pas * HC + cc + (NFC if isg else 0)
                    wsU.get(ucnt)
                    slot = ucnt % 6
                    ucnt += 1
                    if cc == HC - 2 and isg:
                        wsD.prefetch(dcnt + 1)
                    half = isg
                    proj_fm(c, wu[slot], ("fwu", slot), half, hrhs, hkeys)
                    u = uu[isg]
                    uk = ("ffu", isg)
                    y = bank(half * 4, 4)
                    yk = psk(half * 4, 4)
                    ysb = tl[isg]
                    ysk = ("fftl", isg)
                    P.add("act", lambda e, ysb=ysb, y=y: e.activation(out=ysb[:, :], in_=y, func=AF.Copy), reads=yk, writes=[ysk])
                    P.add("act", lambda e, u=u, ysb=ysb, ch=ch: e.activation(out=u[:, :], in_=ysb[:, :], func=AF.Copy, scale=fcw[:, ch, 1:2]),
                          reads=[ysk, "fcw"], writes=[uk])
                    P.add("dve", lambda e, u=u, ysb=ysb, ch=ch: e.scalar_tensor_tensor(out=u[:, 1:T], in0=ysb[:, 0:T - 1], scalar=fcw[:, ch, 0:1], in1=u[:, 1:T],
                                                                                       op0=ALU.mult, op1=ALU.add), reads=[ysk, "fcw", uk], writes=[uk])
                    P.add("dve", lambda e, u=u, ysb=ysb, ch=ch: e.scalar_tensor_tensor(out=u[:, 0:T - 1], in0=ysb[:, 1:T], scalar=fcw[:, ch, 2:3], in1=u[:, 0:T - 1],
                                                                                       op0=ALU.mult, op1=ALU.add), reads=[ysk, "fcw", uk], writes=[uk])
                    P.add("dve", lambda e, u=u, ysb=ysb, ch=ch: e.scalar_tensor_tensor(out=u[:, SEG:T:SEG], in0=ysb[:, SEG - 1:T - 1:SEG], scalar=nbw[:, ch, 0:1],
                                                                                       in1=u[:, SEG:T:SEG], op0=ALU.mult, op1=ALU.add),
                          reads=[ysk, "fnbw", uk], writes=[uk])
                    P.add("dve", lambda e, u=u, ysb=ysb, ch=ch: e.scalar_tensor_tensor(out=u[:, SEG - 1:T - 1:SEG], in0=ysb[:, SEG:T:SEG], scalar=nbw[:, ch, 1:2],
                                                                                       in1=u[:, SEG - 1:T - 1:SEG], op0=ALU.mult, op1=ALU.add),
                          reads=[ysk, "fnbw", uk], writes=[uk])
                    if isg:
                        P.add("act", lambda e: e.activation(out=sg[:, :], in_=uu[1][:, :], func=AF.Silu), reads=[("ffu", 1)], writes=["ffsg"])
                        P.add("dve", lambda e, cc=cc: e.tensor_tensor(out=act[:, cc, :], in0=sg[:, :], in1=uu[0][:, :], op=ALU.mult),
                              reads=["ffsg", ("ffu", 0)], writes=[("ffact", cc)])
            for j in range(KC):
                wsD.get(dcnt)
                slot = dcnt % 3
                dcnt += 1
                half = j % 2
                proj_fm(c, wd[slot], ("fwd", slot), half, lambda k, g: act[:, k, g * TGS:(g + 1) * TGS],
                        lambda g: [("ffact", k) for k in range(HC)], nk=HC)
                P.add("dve", lambda e, j=j, half=half: e.scalar_tensor_tensor(out=xT[:, j, :], in0=bank(half * 4, 4), scalar=mv[:, l, 5, j:j + 1],
                                                                             in1=xT[:, j, :], op0=ALU.mult, op1=ALU.add),
                      reads=psk(half * 4, 4) + MVK + xk(j), writes=xk(j))
        P.barrier()


def hgrn_mixer(c):
    P, nc, PS, bank, psk = c["P"], c["nc"], c["PS"], c["bank"], c["psk"]
    xT, hT, mv, sb, d = c["xT"], c["hT"], c["mv"], c["sb"], c["d"]
    xk, MVK = c["xk"], c["MVK"]
    keep, onesr, identb = c["keep"], c["onesr"], c["identb"]
    PSb = PS.bitcast(BF16)
    HL = 512
    NP = T // HL
    NR = 4

    def hrhs(k, g):
        return hT[:, k, g * TGS:(g + 1) * TGS]

    def hkeys(g):
        return [("hT", k, g) for k in range(KC)]

    with ExitStack() as ph:
        yT = sb(ph, "yT", [128, 4, T], BF16)
        wu = [sb(ph, "hwu%d" % i, [128, KC, 128], BF16) for i in range(4)]
        hmask = sb(ph, "hmask_sb", [128, 2, 512], BF16)
        trim = sb(ph, "trim", [128, 2, 128], BF16)
        lbl = sb(ph, "lbl_sb", [128, 2, 2, 8], F32)
        lb = sb(ph, "lb", [128, 2, 8], F32)
        hng = sb(ph, "hng_sb", [128, 1], F32)
        P.add("pool", lambda e: e.dma_start(out=hmask[:], in_=d["hmask"][:]), writes=["hmask"], dma=True)
        P.add("pool", lambda e: e.dma_start(out=trim[:], in_=d["htri"][:]), writes=["trim"], dma=True)
        P.add("sp", lambda e: e.dma_start(out=lbl[:], in_=d["lbl"][:]), writes=["lbl"], dma=True)
        P.add("sp", lambda e: e.dma_start(out=hng[:, :], in_=d["hng"][:, :]), writes=["hng"], dma=True)
        P.add("dve", lambda e: e.tensor_tensor(out=lb[:, :, :], in0=lbl[:, 0, :, :], in1=lbl[:, 1, :, :], op=ALU.subtract), reads=["lbl"], writes=["lb"])
        P.add("act", lambda e: e.activation(out=lb[:].rearrange("p a b -> p (a b)"), in_=lb[:].rearrange("p a b -> p (a b)"), func=AF.Exp),
              reads=["lb"], writes=["lb"])
        P.add("dve", lambda e: e.tensor_scalar(out=lb[:, :, :], in0=lb[:, :, :], scalar1=1.0, scalar2=None, op0=ALU.add), reads=["lb"], writes=["lb"])
        P.add("dve", lambda e: e.reciprocal(out=lb[:].rearrange("p a b -> p (a b)"), in_=lb[:].rearrange("p a b -> p (a b)")), reads=["lb"], writes=["lb"])
        srcsH = []
        for h in range(8):
            srcsH += [d["hin"][0 * 8 + h], d["hin"][1 * 8 + h], d["hin"][3 * 8 + h], d["hin"][4 * 8 + h], d["hin"][2 * 8 + h]]
            if h % 4 == 3:
                srcsH += [(d["hout"][j][:, 4 * (h // 4):4 * (h // 4) + 4, :], 4) for j in range(KC)]
        wsH = WStream(c, wu, "hwu", srcsH, 3)
        ucount = [0]

        def next_unit(src):
            i = ucount[0]
            ucount[0] += 1
            wsH.get(i)
            return i % 4

        with ExitStack() as pp:
            qf = sb(pp, "qf", [128, T], BF16)
            gA = [sb(pp, "gA%d" % i, [128, HL], F32) for i in range(NP)]
            gB = [sb(pp, "gB%d" % i, [128, HL], F32) for i in range(NP)]
            gC = [sb(pp, "gC%d" % i, [128, HL], F32) for i in range(NP)]
            sqr = [sb(pp, "gsq%d" % i, [128, HL], F32R) for i in range(2)]
            qtL = [sb(pp, "qt%d" % i, [128, T], BF16) for i in range(2)]
            ktL = [sb(pp, "kt%d" % i, [128, T], BF16) for i in range(2)]
            khfm = sb(pp, "khfm", [128, T], BF16)
            khTL = [sb(pp, "khT%d" % i, [128, 16, 128], BF16) for i in range(2)]
            vtm = sb(pp, "vtm", [128, 16, 128], BF16)
            of = sb(pp, "of", [128, T], F32)
            decL = [sb(pp, "dec%d" % i, [128, NCH], F32) for i in range(2)]
            Sf = [sb(pp, "Sf%d" % i, [128, 128], F32) for i in range(NR)]
            Sb = [sb(pp, "Sb%d" % i, [128, 128], BF16) for i in range(NR)]
            Am = [sb(pp, "Am%d" % i, [128, 128], BF16) for i in range(2)]
            HST = os.environ.get("K_H", "vgtrn")
            for h in range(int(os.environ.get("K_NH", "8"))):
                s_q = next_unit(d["hin"][0 * 8 + h])
                proj_fm(c, wu[s_q], ("hwu", s_q), 0, hrhs, hkeys)
                s_i = next_unit(d["hin"][1 * 8 + h])
                P.add("act", lambda e: e.activation(out=qf[:, :], in_=bank(0, 4), func=AF.Copy), reads=psk(0, 4), writes=["qf"])
                for t4 in (range(4) if "v" in HST else ()):
                    b = 4 + t4 % 2

                    def fnv(e, t4=t4, b=b, s_i=s_i):
                        ins = None
                        for i in range(4):
                            tt = t4 * 4 + i
                            for k in range(KC):
                                ins = e.matmul(bank(b)[:, i * 128:(i + 1) * 128], lhsT=hT[:, k, tt * 128:(tt + 1) * 128], rhs=wu[s_i][:, k, :],
                                               start=(k == 0), stop=(k == KC - 1))
                        return ins
                    P.add("pe", fnv, reads=[("hwu", s_i)] + hkeys(t4), writes=psk(b))
                    P.add("act" if t4 % 2 == 0 else "dve", (lambda e, t4=t4, b=b: e.activation(out=vtm[:, t4 * 4:(t4 + 1) * 4, :].rearrange("p t c -> p (t c)"), in_=bank(b), func=AF.Copy))
                          if t4 % 2 == 0 else (lambda e, t4=t4, b=b: e.tensor_copy(out=vtm[:, t4 * 4:(t4 + 1) * 4, :].rearrange("p t c -> p (t c)"), in_=bank(b))),
                          reads=psk(b), writes=[("vtm", t4)])
                def gates_for(dd, h=h):
                    qt, kt, dec = qtL[dd], ktL[dd], decL[dd]
                    s_f = next_unit(d["hin"][(3 + dd) * 8 + h])
                    proj_fm(c, wu[s_f], ("hwu", s_f), 0, hrhs, hkeys)
                    lbh = lb[:, dd, h:h + 1]
                    lastoff = CHUNK - 1 if dd == 0 else 0
                    NB = HL // CHUNK

                    def g1(hf):
                        P.add("act", lambda e: e.activation(out=gA[hf][:, :], in_=bank(hf * (HL // 512), HL // 512), func=AF.Exp, scale=-1.0), reads=psk(hf * (HL // 512), HL // 512), writes=[("gA", hf)])

                    def g2(hf):
                        P.add("act", lambda e: e.activation(out=gB[hf][:, :], in_=gA[hf][:, :], func=AF.Ln, bias=1.0, scale=lbh), reads=[("gA", hf), "lb"], writes=[("gB", hf)])

                    def g3(hf):
                        P.add("act", lambda e: e.activation(out=gA[hf][:, :], in_=gA[hf][:, :], func=AF.Ln, bias=1.0, scale=1.0), reads=[("gA", hf)], writes=[("gA", hf)])

                    def g4_(hf):
                        P.add("dve", lambda e: e.tensor_tensor(out=gB[hf][:, :], in0=gB[hf][:, :], in1=gA[hf][:, :], op=ALU.subtract),
                              reads=[("gA", hf), ("gB", hf)], writes=[("gB", hf)])

                    def g5(hf):
                        P.add("act", lambda e: e.activation(out=gA[hf][:, :], in_=gB[hf][:, :], func=AF.Exp), reads=[("gB", hf)], writes=[("gA", hf)])

                    def g6(hf):
                        P.add("dve", lambda e: e.tensor_scalar(out=gA[hf][:, :], in0=gA[hf][:, :], scalar1=-1.0, scalar2=1.0, op0=ALU.mult, op1=ALU.add),
                              reads=[("gA", hf)], writes=[("gA", hf)])

                    def g7(hf):
                        for blk in range(HL // 512):
                            bs = slice(blk * 512, (blk + 1) * 512)
                            if dd == 0:
                                P.add("dve", lambda e, bs=bs: e.tensor_tensor_scan(out=gC[hf][:, bs], data0=hmask[:, 0, :], data1=gB[hf][:, bs], initial=0.0,
                                                                                    op0=ALU.mult, op1=ALU.add), reads=[("gB", hf), "hmask"], writes=[("gC", hf, blk)])
                            else:
                                P.add("dve", lambda e, bs=bs: e.tensor_tensor_scan(out=gC[hf][:, bs][:, ::-1], data0=hmask[:, 1, ::-1], data1=gB[hf][:, bs][:, ::-1],
                                                                                    initial=0.0, op0=ALU.mult, op1=ALU.add),
                                      reads=[("gB", hf), "hmask"], writes=[("gC", hf, blk)])

                    def gck(hf):
                        return [("gC", hf, blk) for blk in range(HL // 512)]

                    def g8(hf):
                        P.add("act", lambda e: e.activation(out=gB[hf][:, :], in_=gC[hf][:, :], func=AF.Exp), reads=gck(hf) + [("gB", hf)], writes=[("gB", hf)])

                    def g9(hf):
                        P.add("act", lambda e: e.activation(out=gC[hf][:, :], in_=gC[hf][:, :], func=AF.Exp, scale=-1.0), reads=gck(hf), writes=gck(hf))

                    def g10(hf):
                        tok = slice(hf * HL, (hf + 1) * HL)
                        P.add("dve", lambda e: e.tensor_tensor(out=qt[:, tok], in0=qf[:, tok], in1=gB[hf][:, :], op=ALU.mult),
                              reads=["qf", ("gB", hf)], writes=[("qt", dd, hf)])

                    def g11(hf):
                        P.add("dve", lambda e: e.tensor_tensor(out=gA[hf][:, :], in0=gA[hf][:, :], in1=gC[hf][:, :], op=ALU.mult),
                              reads=[("gA", hf)] + gck(hf), writes=[("gA", hf)])

                    def g12(hf):
                        tok = slice(hf * HL, (hf + 1) * HL)
                        P.add("pool", lambda e: e.tensor_copy(out=kt[:, tok], in_=gA[hf][:, :]), reads=[("gA", hf)], writes=[("kt", dd, hf)])

                    def g13(hf):
                        tok = slice(hf * HL, (hf + 1) * HL)
                        ebl = gB[hf][:, lastoff::CHUNK]
                        P.add("dve", lambda e: e.tensor_tensor(out=khfm[:, tok].rearrange("p (c l) -> p c l", l=CHUNK),
                                                               in0=gA[hf][:, :].rearrange("p (c l) -> p c l", l=CHUNK),
                                                               in1=ebl.unsqueeze(2).to_broadcast([128, NB, CHUNK]), op=ALU.mult),
                              reads=[("gA", hf), ("gB", hf)], writes=[("khfm", hf)])

                    def g14(hf):
                        ebl = gB[hf][:, lastoff::CHUNK]
                        P.add("pool", lambda e: e.tensor_copy(out=dec[:, hf * NB:(hf + 1) * NB], in_=ebl), reads=[("gB", hf)], writes=[("dec", dd, hf)])

                    return [g1, g2, g3, g4_, g5, g6, g7, g8, g9, g10, g11, g12, g13, g14]

                def transposes_for(dd):
                    khT = khTL[dd]
                    for t8 in (range(2) if "t" in HST else ()):
                        def fnt(e, t8=t8):
                            ins = None
                            for i in range(8):
                                tt = t8 * 8 + i
                                ins = e.transpose(out=PSb[:, (6 + t8) * 1024 + i * 128:(6 + t8) * 1024 + (i + 1) * 128], in_=khfm[:, tt * 128:(tt + 1) * 128],
                                                  identity=identb[:, :])
                            return ins
                        P.add("pe", fnt, reads=[("khfm", hf) for hf in range((t8 * 1024) // HL, ((t8 + 1) * 1024) // HL)] + ["identb"], writes=psk(6 + t8))
                        P.add("act", lambda e, t8=t8: e.activation(out=khT[:, t8 * 8:(t8 + 1) * 8, :].rearrange("p t c -> p (t c)"),
                                                                    in_=PSb[:, (6 + t8) * 1024:(7 + t8) * 1024], func=AF.Copy),
                              reads=psk(6 + t8), writes=[("khT", dd, t8)])

                def recurrence_for(dd, inject, h=h):
                    qt, kt, dec, khT = qtL[dd], ktL[dd], decL[dd], khTL[dd]
                    cur = 0
                    P.add("sp", lambda e, dd=dd, h=h: e.dma_start(out=Sf[0][:, :], in_=d["state"][dd, h]), writes=[("Sf", 0)], dma=True)
                    P.add("act", lambda e: e.activation(out=Sb[0][:, :], in_=Sf[0][:, :], func=AF.Copy), reads=[("Sf", 0)], writes=[("Sb", 0)])
                    tiles = list(range(16)) if dd == 0 else list(range(15, -1, -1))
                    order = list(range(4)) if dd == 0 else list(range(3, -1, -1))

                    def emit_scores(ti, tt):
                        bS = ti % 2
                        tsl = slice(tt * 128, (tt + 1) * 128)
                        P.add("pe", lambda e: e.matmul(bank(bS)[:, 0:128], lhsT=kt[:, tsl], rhs=qt[:, tsl], start=True, stop=True),
                              reads=[("kt", dd, (tt * 128) // HL), ("qt", dd, (tt * 128) // HL)], writes=psk(bS))

                    def emit_U(tt, i):
                        kw = dict(tile_position=(96, 0)) if i == 3 else {}
                        P.add("pe", lambda e: e.matmul(bank(4 + i)[:, 0:128], lhsT=khT[32 * i:32 * i + 32, tt, :],
                                                       rhs=vtm[32 * i:32 * i + 32, tt, :], start=True, stop=True, **kw),
                              reads=[("khT", dd, tt // 8), ("vtm", tt // 4)], writes=psk(4 + i))

                    emit_scores(0, tiles[0])
                    for i in order:
                        emit_U(tiles[0], i)
                    for ti, tt in enumerate(tiles):
                        par = ti % 2
                        bS, bO = par, 2 + par
                        tsl = slice(tt * 128, (tt + 1) * 128)
                        hfk = (tt * 128) // HL
                        nxt_t = tiles[ti + 1] if ti + 1 < len(tiles) else None
                        P.add("dve", lambda e, par=par, bS=bS, dd=dd: e.tensor_tensor(out=Am[par][:, :], in0=bank(bS)[:, 0:128], in1=trim[:, dd, :], op=ALU.mult),
                              reads=psk(bS) + ["trim"], writes=[("Am", par)])
                        P.add("pe", lambda e, par=par, bO=bO, tt=tt: e.matmul(bank(bO)[:, 0:128], lhsT=vtm[:, tt, :], rhs=Am[par][:, :], start=True, stop=False),
                              reads=[("Am", par), ("vtm", tt // 4)], writes=psk(bO))
                        if nxt_t is not None:
                            emit_scores(ti + 1, nxt_t)
                        for oi, i in enumerate(order):
                            cch = tt * 4 + i
                            csl = slice(cch * CHUNK, (cch + 1) * CHUNK)
                            P.add("pe", lambda e, bO=bO, i=i, cur=cur, csl=csl, oi=oi: e.matmul(bank(bO)[:, i * CHUNK:(i + 1) * CHUNK], lhsT=Sb[cur][:, :], rhs=qt[:, csl],
                                                                                                 start=False, stop=(oi == 3)),
                                  reads=[("Sb", cur), ("qt", dd, hfk)], writes=psk(bO))
                            nxt = (cur + 1) % NR
                            P.add("dve", lambda e, nxt=nxt, cur=cur, cch=cch, i=i: e.scalar_tensor_tensor(
                                out=Sf[nxt][:, :], in0=Sf[cur][:, :], scalar=dec[:, cch:cch + 1], in1=bank(4 + i)[:, 0:128], op0=ALU.mult, op1=ALU.add),
                                reads=[("Sf", cur), ("dec", dd, cch // (HL // CHUNK))] + psk(4 + i), writes=[("Sf", nxt)])
                            cur = nxt
                            seg_end = (cch % 8 == 7) if dd == 0 else (cch % 8 == 0)
                            if seg_end:
                                seg = cch // 8
                                P.add("sp", lambda e, cur=cur, seg=seg, dd=dd, h=h: e.dma_start(out=d["o_s"][seg, dd, h], in_=Sf[cur][:, :]),
                                      reads=[("Sf", cur)], writes=[("o_s", seg, dd, h)], dma=True)
                                last = (cch == NCH - 1) if dd == 0 else (cch == 0)
                                if not last:
                                    nxt = (cur + 1) % NR
                                    P.add("dve", lambda e, nxt=nxt, cur=cur: e.tensor_scalar(out=Sf[nxt][:, :], in0=Sf[cur][:, :], scalar1=keep, scalar2=None, op0=ALU.mult),
                                          reads=[("Sf", cur), "flags"], writes=[("Sf", nxt)])
                                    cur = nxt
                            P.add("act", lambda e, cur=cur: e.activation(out=Sb[cur][:, :], in_=Sf[cur][:, :], func=AF.Copy), reads=[("Sf", cur)], writes=[("Sb", cur)])
                            if nxt_t is not None:
                                emit_U(nxt_t, i)
                            if inject is not None:
                                inject()
                        if dd == 0:
                            P.add("act", lambda e, bO=bO, tsl=tsl: e.activation(out=of[:, tsl], in_=bank(bO)[:, 0:128], func=AF.Copy), reads=psk(bO), writes=[("of", tt)])
                        else:
                            P.add("dve", lambda e, bO=bO, tsl=tsl: e.tensor_tensor(out=of[:, tsl], in0=bank(bO)[:, 0:128], in1=of[:, tsl], op=ALU.add),
                                  reads=psk(bO) + [("of", tt)], writes=[("of", tt)])

                st0 = gates_for(0)
                for st in st0:
                    for hf in range(NP):
                        st(hf)
                transposes_for(0)
                st1 = gates_for(1)
                for hf in range(NP):
                    st1[0](hf)
                pend = [(st, hf) for st in st1[1:] for hf in range(NP)]

                def inject():
                    for _ in range(1):
                        if pend:
                            st_, hf_ = pend.pop(0)
                            st_(hf_)
                recurrence_for(0, inject)
                while pend:
                    st_, hf_ = pend.pop(0)
                    st_(hf_)
                transposes_for(1)
                recurrence_for(1, None)
                if "n" not in HST:
                    continue
                s_g = next_unit(d["hin"][2 * 8 + h])
                proj_fm(c, wu[s_g], ("hwu", s_g), 1, hrhs, hkeys)
                TPP = HL // 128

                def n1(p):
                    tok = slice(p * HL, (p + 1) * HL)
                    ofk = [("of", tt) for tt in range(p * TPP, (p + 1) * TPP)]
                    P.add("act", lambda e: e.activation(out=sqr[p % 2][:, :], in_=of[:, tok], func=AF.Square), reads=ofk, writes=[("gsq", p % 2)])

                def n2(p):
                    P.add("pe", lambda e: e.matmul(bank(p), lhsT=onesr[:, :], rhs=sqr[p % 2][:, :], start=True, stop=True),
                          reads=[("gsq", p % 2), "onesr"], writes=psk(p))

                def n3(p):
                    P.add("act", lambda e: e.activation(out=gA[p][:, :], in_=bank(p), func=AF.Ln, bias=EPS, scale=1.0 / 128.0), reads=psk(p), writes=[("gA", p)])

                def n4(p):
                    P.add("act", lambda e: e.activation(out=gA[p][:, :], in_=gA[p][:, :], func=AF.Exp, scale=-0.5), reads=[("gA", p)], writes=[("gA", p)])

                def n5(p):
                    P.add("act", lambda e: e.activation(out=gB[p][:, :], in_=bank(4 + p), func=AF.Silu), reads=psk(4 + p), writes=[("gB", p)])

                def n6(p):
                    tok = slice(p * HL, (p + 1) * HL)
                    ofk = [("of", tt) for tt in range(p * TPP, (p + 1) * TPP)]
                    P.add("dve", lambda e: e.scalar_tensor_tensor(out=gA[p][:, :], in0=of[:, tok], scalar=hng[:, 0:1], in1=gA[p][:, :], op0=ALU.mult, op1=ALU.mult),
                          reads=ofk + [("gA", p), "hng"], writes=[("gA", p)])

                def n7(p, h=h):
                    tok = slice(p * HL, (p + 1) * HL)
                    P.add("pool", lambda e: e.tensor_tensor(out=yT[:, h % 4, tok], in0=gA[p][:, :], in1=gB[p][:, :], op=ALU.mult),
                          reads=[("gA", p), ("gB", p)], writes=[("yT", h % 4, p)])

                for p in range(NP):
                    n1(p)
                    n2(p)
                for st in (n3, n4, n5, n6, n7):
                    for p in range(NP):
                        st(p)
                if h % 4 == 3:
                    for j in range(KC):
                        slot = next_unit(None)
                        half = j % 2
                        proj_fm(c, wu[slot], ("hwu", slot), half, lambda k, g: yT[:, k, g * TGS:(g + 1) * TGS],
                                lambda g: [("yT", k, g) for k in range(4)], nk=4)
                        P.add("dve", lambda e, j=j, half=half: e.scalar_tensor_tensor(out=xT[:, j, :], in0=bank(half * 4, 4), scalar=mv[:, 1, 2, j:j + 1],
                                                                                     in1=xT[:, j, :], op0=ALU.mult, op1=ALU.add),
                              reads=psk(half * 4, 4) + MVK + xk(j), writes=xk(j))
            P.barrier()
        P.barrier()


def _units(W, cols_list):
    K = W.shape[0]
    out = np.empty((len(cols_list), 128, K // 128, 128), dtype=np.float32)
    for i, cols in enumerate(cols_list):
        blk = W[:, cols]
        out[i] = blk.reshape(K // 128, 128, 128).transpose(1, 0, 2)
    return out


def prep_inputs(inp):
    f = lambda a: np.ascontiguousarray(np.asarray(a, dtype=np.float32))
    x_prompt, x_sample = f(inp["x_prompt"]), f(inp["x_sample"])
    cache_k, cache_v, state = f(inp["cache_k"]), f(inp["cache_v"]), f(inp["state_hgrn"])
    cvec, c_ctx = f(inp["c"]), f(inp["c_ctx"])
    shared = {}
    shared["w_mod"] = f(inp["w_mod"])
    bm = f(inp["b_mod"])
    shared["bmod"] = np.ascontiguousarray(np.concatenate([bm[l].reshape(48, 128).T for l in range(2)], axis=1))
    lnp = np.empty((128, 2, 4, KC), np.float32)
    for l in range(2):
        for i, nm in enumerate(("ln1_g", "ln1_b", "ln2_g", "ln2_b")):
            lnp[:, l, i, :] = f(inp[nm])[l].reshape(KC, 128).T
    shared["lnp"] = lnp
    Win = f(inp["attn_w_in"])[0]
    ar = np.arange
    cols = [ar(c * 128, (c + 1) * 128) for c in range(4)]
    cols += [np.concatenate([ar(512, 576), ar(512, 576)]), np.concatenate([ar(576, 640), ar(576, 640)])]
    cols += [ar(640, 768)]
    cols += [ar(768 + c * 128, 768 + (c + 1) * 128) for c in range(12)]
    shared["attn_in_u"] = _units(Win, cols)
    sw = _swap_idx()
    qg, kg = f(inp["attn_q_gain"])[0], f(inp["attn_k_gain"])[0]
    gains = np.stack([np.tile(qg, 2), np.tile(qg[sw], 2), np.tile(kg, 2), np.tile(kg[sw], 2)], axis=1)
    shared["gains"] = np.ascontiguousarray(gains.astype(np.float32))
    swapT = np.zeros((128, 128), np.float32)
    for m in range(128):
        p = (m // 64) * 64 + sw[m % 64]
        swapT[p, m] = 1.0
    bones = np.zeros((128, 128), np.float32)
    bones[:64, :64] = 1.0
    bones[64:, 64:] = 1.0
    shared["consts"] = np.stack([swapT, bones, np.eye(128, dtype=np.float32)])
    scw = f(inp["sconv_w"])[0]
    shared["scw"] = np.ascontiguousarray(scw.reshape(3, 4, 128).transpose(2, 1, 0))
    shared["attn_out_u"] = _units(f(inp["attn_w_out"])[0], [ar(j * 128, (j + 1) * 128) for j in range(8)])
    wup = f(inp["ffn_w_up"])
    shared["ffn_up_u"] = np.stack([_units(wup[l], [ar(c * 128, (c + 1) * 128) for c in range(44)]) for l in range(2)])
    fcw = f(inp["ffn_conv_w"])
    shared["fcw"] = np.ascontiguousarray(fcw.reshape(2, 3, 44, 128).transpose(0, 3, 2, 1))
    wdn = f(inp["ffn_w_down"])
    shared["ffn_dn_u"] = np.stack([_units(wdn[l], [ar(j * 128, (j + 1) * 128) for j in range(8)]) for l in range(2)])
    hin = f(inp["hgrn_w_in"])[0]
    shared["hgrn_in_u"] = _units(hin, [ar(kind * 1024 + h * 128, kind * 1024 + (h + 1) * 128) for kind in range(5) for h in range(8)])
    lbl = f(inp["hgrn_lb_logits"])
    shared["lbl"] = np.ascontiguousarray(lbl.reshape(2, 2, 8, 128).transpose(3, 0, 1, 2))
    shared["hng"] = np.ascontiguousarray(f(inp["hgrn_norm_g"])[0].reshape(128, 1))
    shared["hgrn_out_u"] = _units(f(inp["hgrn_w_out"])[0], [ar(j * 128, (j + 1) * 128) for j in range(8)])
    hm = np.zeros((128, 2, 512), np.float32)
    t = np.arange(512)
    hm[:, 0, :] = (t % CHUNK != 0).astype(np.float32)[None, :]
    hm[:, 1, :] = (t % CHUNK != CHUNK - 1).astype(np.float32)[None, :]
    sidx = np.arange(128)[:, None]
    tidx = np.arange(128)[None, :]
    same = (sidx // CHUNK) == (tidx // CHUNK)
    ht = np.zeros((128, 2, 128), np.float32)
    ht[:, 0, :] = (same & (sidx <= tidx)).astype(np.float32)
    ht[:, 1, :] = (same & (sidx >= tidx)).astype(np.float32)
    shared["hscan"] = hm
    shared["htri"] = ht
    tok = np.arange(T)
    row = (tok // 64).astype(np.float32)
    colp = (tok % 64).astype(np.float32)
    inv = (10000.0 ** (-np.arange(0, 32, 2, dtype=np.float32) / 32.0)).astype(np.float32)
    C = np.zeros((64, T), np.float32)
    S = np.zeros((64, T), np.float32)
    for dd in range(64):
        blk = dd // 16
        pos = row if blk < 2 else colp
        ang = pos * inv[dd % 16]
        C[dd] = np.cos(ang)
        S[dd] = -np.sin(ang) if blk % 2 == 0 else np.sin(ang)
    ropeC_s, ropeS_s = np.tile(C, (2, 1)), np.tile(S, (2, 1))
    ropeC_p, ropeS_p = np.ones((128, T), np.float32), np.zeros((128, T), np.float32)
    maps = []
    for core in range(8):
        m = dict(shared)
        if core < 4:
            b = core
            xt = x_sample[b]
            m["cond"] = np.ascontiguousarray(cvec[b].reshape(KC, 128).T)
            fl = np.zeros((128, 4), np.float32)
            fl[:, 0] = 1.0
            fl[:, 1] = 1.0
            m["ropeC"], m["ropeS"] = ropeC_s, ropeS_s
            kc = cache_k[b, 0]
            m["kcT"] = np.ascontiguousarray(np.stack([np.tile(kc[:, g, :].T, (2, 1)) for g in range(2)]))
            m["vc"] = np.ascontiguousarray(cache_v[b, 0].reshape(PAST, 128))
            m["state"] = np.ascontiguousarray(state[b, 0])
        else:
            p0 = (core - 4) * 8
            xt = x_prompt[p0:p0 + 8].reshape(T, D)
            m["cond"] = np.ascontiguousarray(c_ctx.reshape(KC, 128).T)
            fl = np.zeros((128, 4), np.float32)
            fl[:, 2] = -1.0
            m["ropeC"], m["ropeS"] = ropeC_p, ropeS_p
            m["kcT"] = np.zeros((2, 128, PAST), np.float32)
            m["vc"] = np.zeros((PAST, 128), np.float32)
            m["state"] = np.zeros((2, 8, 128, 128), np.float32)
        m["flags"] = fl
        m["xT"] = np.ascontiguousarray(xt.T)
        maps.append(m)
    return maps


_CACHE = {}


def kernel(**inputs):
    maps = prep_inputs(inputs)
    if "nc" not in _CACHE:
        _CACHE["nc"] = build()[0]
    nc = _CACHE["nc"]
    res = run_bass_kernel_spmd(nc, maps, core_ids=list(range(8)))
    r = res.results
    y_s = np.stack([r[c]["yT"].T for c in range(4)])
    y_p = np.concatenate([r[c]["yT"].T.reshape(8, SEG, D) for c in range(4, 8)])
    nk = np.concatenate([r[c]["koutT"].transpose(2, 0, 1).reshape(8, 1, SEG, 2, 64) for c in range(4, 8)])
    nv = np.concatenate([r[c]["vout"].reshape(8, 1, SEG, 2, 64) for c in range(4, 8)])
    ns = np.concatenate([r[c]["sout"].reshape(8, 1, 2, 8, 128, 128) for c in range(4, 8)])
    return (np.ascontiguousarray(y_p, dtype=np.float32), np.ascontiguousarray(y_s, dtype=np.float32),
            np.ascontiguousarray(nk, dtype=np.float32), np.ascontiguousarray(nv, dtype=np.float32),
            np.ascontiguousarray(ns, dtype=np.float32))
```
